# Optimizing a Trainium2 kernel written in Bass

```python
import math, functools
import jax, jax.numpy as jnp
from jax import lax
import numpy as np

D_MODEL = 2048
BATCH = 2
SEQ = 4096
DEPTH = 1
DEC_BATCH = 128
DEC_SEQ = 8
PAST_LEN = 8192
PAGE_SIZE = 128

SSM_WIDTH = D_MODEL // 2
SSM_CH = 16
SSM_GROUPS = SSM_WIDTH // SSM_CH
SSM_STATE = 64
DT_MIN = 0.001
DT_MAX = 0.1
N_HEADS = 8
NOPE_DIM = 128
ROPE_DIM = 64
V_DIM = 128
ATTN_WIDTH = N_HEADS * V_DIM
Q_LORA = 512
KV_LORA = 512
ROPE_BASE = 10000.0
SCALE = (NOPE_DIM + ROPE_DIM) ** -0.5
Q_BLOCK = 128
IN_WIDTH = SSM_WIDTH + Q_LORA + KV_LORA + ROPE_DIM
D_FF = ((8 * D_MODEL // 3 + 255) // 256) * 256
DN_ALPHA = (2 * DEPTH) ** 0.25
DN_BETA = (8 * DEPTH) ** -0.25

kernel_name = 'hymba_s5_mla_deepnorm_step'


def rms_norm(x, g, eps=1e-6):
    xf = x.astype(jnp.float32)
    y = xf * lax.rsqrt(jnp.mean(xf * xf, axis=-1, keepdims=True) + eps)
    return (y * g.astype(jnp.float32)).astype(x.dtype)


def layer_norm(x, g, b, eps=1e-5):
    xf = x.astype(jnp.float32)
    mu = jnp.mean(xf, axis=-1, keepdims=True)
    var = jnp.mean(jnp.square(xf - mu), axis=-1, keepdims=True)
    y = (xf - mu) * lax.rsqrt(var + eps)
    return (y * g.astype(jnp.float32) + b.astype(jnp.float32)).astype(x.dtype)


def apply_rope(x, pos):
    half = ROPE_DIM // 2
    inv = ROPE_BASE ** (-2.0 * jnp.arange(half, dtype=jnp.float32) / ROPE_DIM)
    ang = pos.astype(jnp.float32)[:, None] * inv[None, :]
    cos = jnp.cos(ang)[None, :, None, :]
    sin = jnp.sin(ang)[None, :, None, :]
    x1 = x[..., :half].astype(jnp.float32)
    x2 = x[..., half:].astype(jnp.float32)
    return jnp.concatenate([x1 * cos - x2 * sin, x1 * sin + x2 * cos], axis=-1).astype(x.dtype)


def _ssm_combine(e1, e2):
    a1, b1 = e1
    a2, b2 = e2
    return a1 * a2, a2 * b1 + b2


def s5_group(u, h0, p):
    bsz, s, _ = u.shape
    f32 = jnp.float32
    uf = u.astype(f32).reshape(bsz, s, SSM_GROUPS, SSM_CH)
    lam = lax.complex(p['ssm_a_re'].astype(f32), p['ssm_a_im'].astype(f32))
    delta = jnp.exp(p['ssm_log_step'].astype(f32))[:, None]
    a_bar = jnp.exp(lam * delta)
    b_mat = lax.complex(p['ssm_b_re'].astype(f32), p['ssm_b_im'].astype(f32))
    b_bar = ((a_bar - 1.0) / lam)[..., None] * b_mat
    bu = jnp.einsum('bsgh,gph->bsgp', uf.astype(jnp.complex64), b_bar)
    if h0 is not None:
        bu = bu.at[:, 0].add(a_bar * h0)
    a_seq = jnp.broadcast_to(a_bar, bu.shape)
    _, h = lax.associative_scan(_ssm_combine, (a_seq, bu), axis=1)
    c_mat = lax.complex(p['ssm_c_re'].astype(f32), p['ssm_c_im'].astype(f32))
    y = jnp.real(jnp.einsum('bsgp,ghp->bsgh', h, c_mat)) + p['ssm_d'].astype(f32) * uf
    g = jax.nn.gelu(y.reshape(bsz, s, SSM_WIDTH))
    out = g * jax.nn.sigmoid(g @ p['w_glu'].astype(f32) + p['b_glu'].astype(f32))
    return out.astype(u.dtype), h[:, -1]


def _scores(q_lat, q_rope, k_lat, k_rope):
    sc = jnp.einsum('bshr,bkr->bhsk', q_lat, k_lat) + jnp.einsum('bshd,bkd->bhsk', q_rope, k_rope)
    return sc.astype(jnp.float32) * SCALE


def prompt_attention(q_lat, q_rope, ckv, krope):
    bsz, s, h, r = q_lat.shape
    nb = s // Q_BLOCK
    ql = q_lat.reshape(bsz, nb, Q_BLOCK, h, r).transpose(1, 0, 2, 3, 4)
    qr = q_rope.reshape(bsz, nb, Q_BLOCK, h, ROPE_DIM).transpose(1, 0, 2, 3, 4)
    k_pos = jnp.arange(s)
    ckv32 = ckv.astype(jnp.float32)

    def block(args):
        i, qlb, qrb = args
        sc = _scores(qlb, qrb, ckv, krope)
        q_pos = i * Q_BLOCK + jnp.arange(Q_BLOCK)
        sc = jnp.where(k_pos[None, :] <= q_pos[:, None], sc, -jnp.inf)
        pr = jax.nn.softmax(sc, axis=-1)
        return jnp.einsum('bhqk,bkr->bqhr', pr, ckv32).astype(q_lat.dtype)

    o = lax.map(block, (jnp.arange(nb), ql, qr))
    return o.transpose(1, 0, 2, 3, 4).reshape(bsz, s, h, r)


def _online_update(carry, sc, k_lat):
    m, l, acc = carry
    m_new = jnp.maximum(m, jnp.max(sc, axis=-1))
    corr = jnp.exp(m - m_new)
    pr = jnp.exp(sc - m_new[..., None])
    l = l * corr + jnp.sum(pr, axis=-1)
    acc = acc * corr[..., None] + jnp.einsum('bhsk,bkr->bhsr', pr, k_lat.astype(jnp.float32))
    return m_new, l, acc


def sample_attention(q_lat, q_rope, ckv_new, kr_new, cache_ckv, cache_kr, page_table):
    bsz, s, h, r = q_lat.shape
    init = (jnp.full((bsz, h, s), -jnp.inf, jnp.float32),
            jnp.zeros((bsz, h, s), jnp.float32),
            jnp.zeros((bsz, h, s, r), jnp.float32))

    def page_step(carry, pages):
        k_lat = cache_ckv[pages]
        k_rope = cache_kr[pages]
        return _online_update(carry, _scores(q_lat, q_rope, k_lat, k_rope), k_lat), None

    carry, _ = lax.scan(page_step, init, page_table.T)
    causal = jnp.tril(jnp.ones((s, s), dtype=bool))
    sc_new = jnp.where(causal, _scores(q_lat, q_rope, ckv_new, kr_new), -jnp.inf)
    _, l, acc = _online_update(carry, sc_new, ckv_new)
    return (acc / l[..., None]).transpose(0, 2, 1, 3).astype(q_lat.dtype)


def layer_forward(x, pos, h0, attend, p):
    bsz, s, _ = x.shape
    proj = x @ p['w_in']
    u, cq, ckv_raw, kr_raw = jnp.split(
        proj, [SSM_WIDTH, SSM_WIDTH + Q_LORA, SSM_WIDTH + Q_LORA + KV_LORA], axis=-1)
    y_ssm, h_last = s5_group(u, h0, p)
    q = (rms_norm(cq, p['g_q']) @ p['w_uq']).reshape(bsz, s, N_HEADS, NOPE_DIM + ROPE_DIM)
    q_nope = q[..., :NOPE_DIM]
    q_rope = apply_rope(q[..., NOPE_DIM:], pos)
    q_lat = jnp.einsum('bshn,rhn->bshr', q_nope, p['w_uk'])
    ckv = rms_norm(ckv_raw, p['g_kv'])
    krope = apply_rope(kr_raw[:, :, None, :], pos)[:, :, 0, :]
    o_lat = attend(q_lat, q_rope, ckv, krope)
    y_att = jnp.einsum('bshr,rhv->bshv', o_lat, p['w_uv']).reshape(bsz, s, ATTN_WIDTH)
    mix = jnp.concatenate([y_ssm, y_att], axis=-1) @ p['w_out']
    x = layer_norm(DN_ALPHA * x + mix, p['ln1_g'], p['ln1_b'])
    ffn = (jax.nn.silu(x @ p['w_gate']) * (x @ p['w_up'])) @ p['w_down']
    x = layer_norm(DN_ALPHA * x + ffn, p['ln2_g'], p['ln2_b'])
    return x, ckv, krope, h_last


def _normal(k, shape, scale):
    return jax.random.normal(k, shape, jnp.float32) * scale


def setup_inputs(seed: int = 0) -> dict:
    key = jax.random.key(seed)
    ks = jax.random.split(key, 32)
    L = DEPTH
    n_pages = PAST_LEN // PAGE_SIZE
    n_pool = (DEC_BATCH * n_pages * 5) // 4
    page_table = jax.random.permutation(ks[6], n_pool)[:DEC_BATCH * n_pages]
    page_table = page_table.reshape(DEC_BATCH, n_pages).astype(jnp.int32)
    a_im = jnp.pi * jnp.arange(SSM_STATE, dtype=jnp.float32) + _normal(ks[14], (L, SSM_GROUPS, SSM_STATE), 0.01)
    return {
        'x_prompt': _normal(ks[0], (BATCH, SEQ, D_MODEL), 1.0),
        'x_sample': _normal(ks[1], (DEC_BATCH, DEC_SEQ, D_MODEL), 1.0),
        'cache_ckv': _normal(ks[2], (L, n_pool, PAGE_SIZE, KV_LORA), 1.0),
        'cache_krope': _normal(ks[3], (L, n_pool, PAGE_SIZE, ROPE_DIM), 1.0),
        'state_ssm_re': _normal(ks[4], (L, DEC_BATCH, SSM_GROUPS, SSM_STATE), 0.5),
        'state_ssm_im': _normal(ks[5], (L, DEC_BATCH, SSM_GROUPS, SSM_STATE), 0.5),
        'page_table': page_table,
        'w_in': _normal(ks[7], (L, D_MODEL, IN_WIDTH), D_MODEL ** -0.5),
        'g_q': 1.0 + _normal(ks[8], (L, Q_LORA), 0.01),
        'w_uq': _normal(ks[9], (L, Q_LORA, N_HEADS * (NOPE_DIM + ROPE_DIM)), Q_LORA ** -0.5),
        'w_uk': _normal(ks[10], (L, KV_LORA, N_HEADS, NOPE_DIM), KV_LORA ** -0.5),
        'g_kv': 1.0 + _normal(ks[11], (L, KV_LORA), 0.01),
        'w_uv': _normal(ks[12], (L, KV_LORA, N_HEADS, V_DIM), KV_LORA ** -0.5),
        'ssm_a_re': -0.5 + _normal(ks[13], (L, SSM_GROUPS, SSM_STATE), 0.01),
        'ssm_a_im': a_im,
        'ssm_log_step': jax.random.uniform(ks[15], (L, SSM_GROUPS), jnp.float32, math.log(DT_MIN), math.log(DT_MAX)),
        'ssm_b_re': _normal(ks[16], (L, SSM_GROUPS, SSM_STATE, SSM_CH), (2 * SSM_CH) ** -0.5),
        'ssm_b_im': _normal(ks[17], (L, SSM_GROUPS, SSM_STATE, SSM_CH), (2 * SSM_CH) ** -0.5),
        'ssm_c_re': _normal(ks[18], (L, SSM_GROUPS, SSM_CH, SSM_STATE), SSM_STATE ** -0.5),
        'ssm_c_im': _normal(ks[19], (L, SSM_GROUPS, SSM_CH, SSM_STATE), SSM_STATE ** -0.5),
        'ssm_d': _normal(ks[20], (L, SSM_GROUPS, SSM_CH), 1.0),
        'w_glu': _normal(ks[21], (L, SSM_WIDTH, SSM_WIDTH), SSM_WIDTH ** -0.5),
        'b_glu': _normal(ks[22], (L, SSM_WIDTH), 0.01),
        'w_out': _normal(ks[23], (L, D_MODEL, D_MODEL), DN_BETA * D_MODEL ** -0.5),
        'ln1_g': 1.0 + _normal(ks[24], (L, D_MODEL), 0.01),
        'ln1_b': _normal(ks[25], (L, D_MODEL), 0.01),
        'w_gate': _normal(ks[26], (L, D_MODEL, D_FF), D_MODEL ** -0.5),
        'w_up': _normal(ks[27], (L, D_MODEL, D_FF), D_MODEL ** -0.5),
        'w_down': _normal(ks[28], (L, D_FF, D_MODEL), DN_BETA * D_FF ** -0.5),
        'ln2_g': 1.0 + _normal(ks[29], (L, D_MODEL), 0.01),
        'ln2_b': _normal(ks[30], (L, D_MODEL), 0.01),
    }


def reference(x_prompt, x_sample, cache_ckv, cache_krope, state_ssm_re, state_ssm_im, page_table,
              w_in, g_q, w_uq, w_uk, g_kv, w_uv, ssm_a_re, ssm_a_im, ssm_log_step,
              ssm_b_re, ssm_b_im, ssm_c_re, ssm_c_im, ssm_d, w_glu, b_glu, w_out,
              ln1_g, ln1_b, w_gate, w_up, w_down, ln2_g, ln2_b):
    pos_p = jnp.arange(x_prompt.shape[1], dtype=jnp.int32)
    past = page_table.shape[1] * PAGE_SIZE
    pos_s = past + jnp.arange(x_sample.shape[1], dtype=jnp.int32)
    y_p, y_s = x_prompt, x_sample
    ckv_p_l, kr_p_l, re_p_l, im_p_l = [], [], [], []
    ckv_s_l, kr_s_l, re_s_l, im_s_l = [], [], [], []
    for l in range(DEPTH):
        p = {
            'w_in': w_in[l], 'g_q': g_q[l], 'w_uq': w_uq[l], 'w_uk': w_uk[l], 'g_kv': g_kv[l],
            'w_uv': w_uv[l], 'ssm_a_re': ssm_a_re[l], 'ssm_a_im': ssm_a_im[l],
            'ssm_log_step': ssm_log_step[l], 'ssm_b_re': ssm_b_re[l], 'ssm_b_im': ssm_b_im[l],
            'ssm_c_re': ssm_c_re[l], 'ssm_c_im': ssm_c_im[l], 'ssm_d': ssm_d[l],
            'w_glu': w_glu[l], 'b_glu': b_glu[l], 'w_out': w_out[l], 'ln1_g': ln1_g[l],
            'ln1_b': ln1_b[l], 'w_gate': w_gate[l], 'w_up': w_up[l], 'w_down': w_down[l],
            'ln2_g': ln2_g[l], 'ln2_b': ln2_b[l],
        }
        y_p, ckv_p, kr_p, h_p = layer_forward(y_p, pos_p, None, prompt_attention, p)
        attend_s = functools.partial(sample_attention, cache_ckv=cache_ckv[l], cache_kr=cache_krope[l],
                                     page_table=page_table)
        h0 = lax.complex(state_ssm_re[l].astype(jnp.float32), state_ssm_im[l].astype(jnp.float32))
        y_s, ckv_s, kr_s, h_s = layer_forward(y_s, pos_s, h0, attend_s, p)
        ckv_p_l.append(ckv_p); kr_p_l.append(kr_p)
        re_p_l.append(jnp.real(h_p)); im_p_l.append(jnp.imag(h_p))
        ckv_s_l.append(ckv_s); kr_s_l.append(kr_s)
        re_s_l.append(jnp.real(h_s)); im_s_l.append(jnp.imag(h_s))
    return (y_p, y_s,
            jnp.stack(ckv_p_l), jnp.stack(kr_p_l), jnp.stack(re_p_l), jnp.stack(im_p_l),
            jnp.stack(ckv_s_l), jnp.stack(kr_s_l), jnp.stack(re_s_l), jnp.stack(im_s_l))
```

```python
import os
import numpy as np
import concourse.bass as bass
import concourse.mybir as mybir
from concourse.bass_utils import run_bass_kernel_spmd
from contextlib import ExitStack

F32 = mybir.dt.float32
BF16 = mybir.dt.bfloat16
I32 = mybir.dt.int32
AF = mybir.ActivationFunctionType
ALU = mybir.AluOpType
AX = mybir.AxisListType

NBLK = 33
OWN = [4 * j + 3 for j in range(8)] + [32]
NOWN = 9
TOK = NOWN * 128
SCALE = (128 + 64) ** -0.5
ALPHA = 2.0 ** 0.25
DFF = 5632
NFF = DFF // 128
STAGE = int(os.environ.get('MK_STAGE', '99'))
NPOOL = int(os.environ.get('MK_NPOOL', '10240'))
NOWN_RUN = int(os.environ.get('MK_NOWN', '9'))
NCORES = int(os.environ.get('MK_CORES', '8'))
SUB = int(os.environ.get('MK_SUB', '99'))
SUB2 = int(os.environ.get('MK_SUB2', '99'))
NBLK_RUN = int(os.environ.get('MK_NBLK', '33'))


class Sched:
    LAT = 60.0

    def __init__(self, nc, es):
        self.nc, self.es = nc, es
        self.engs = ['sync', 'scalar', 'vector', 'gpsimd', 'tensor']
        self.q = {e: [] for e in self.engs}
        self.esem = {e: es.enter_context(nc.semaphore('s_' + e)) for e in self.engs[1:]}
        self.ecnt = {e: 0 for e in self.engs}
        self.seen = {e: {} for e in self.engs}
        self.dsem = {}
        self.free = []
        self.nsem = 0
        self.nodes = []
        self.buf = {}
        self.last_on_sem = {}

    def _add(self, node, reads, writes):
        nid = len(self.nodes)
        deps = {}
        for key in reads:
            b = self.buf.get(key)
            if b and b[0] is not None:
                deps[b[0]] = True
        for key in writes:
            b = self.buf.get(key)
            if b:
                if b[0] is not None:
                    deps[b[0]] = True
                for r in b[1]:
                    if r not in deps:
                        deps[r] = False
        if node['kind'] == 'dma':
            prev = self.last_on_sem.get(node['sem'])
            if prev is not None and prev not in deps:
                deps[prev] = None
            self.last_on_sem[node['sem']] = nid
        deps.pop(nid, None)
        node['deps'] = deps
        self.nodes.append(node)
        for key in reads:
            self.buf.setdefault(key, [None, []])[1].append(nid)
        for key in writes:
            self.buf[key] = [nid, []]

    def op(self, eng, fn, reads=(), writes=(), dur=100.0):
        self._add({'kind': 'op', 'eng': eng, 'fn': fn, 'dur': dur}, reads, writes)

    def dma(self, eng, semname, fn, reads=(), writes=(), dur=2500.0):
        self._add({'kind': 'dma', 'eng': eng, 'fn': fn, 'sem': semname, 'dur': dur}, reads, writes)

    def _schedule(self):
        import heapq
        nodes = self.nodes
        n = len(nodes)
        if n == 0:
            return
        succ = [[] for _ in range(n)]
        ndep = [0] * n
        for i, nd in enumerate(nodes):
            ndep[i] = len(nd['deps'])
            for d in nd['deps']:
                succ[d].append(i)
        heaps = {e: [] for e in self.engs}
        ready = [0.0] * n
        fin = [0.0] * n
        start = [0.0] * n
        free_at = {e: 0.0 for e in self.engs}
        for i in range(n):
            if ndep[i] == 0:
                heapq.heappush(heaps[nodes[i]['eng']], (0.0, i))
        order = {e: [] for e in self.engs}
        done = 0
        while done < n:
            best = None
            for e in self.engs:
                if heaps[e]:
                    r, i = heaps[e][0]
                    st = max(r, free_at[e])
                    if best is None or (st, i) < (best[0], best[2]):
                        best = (st, e, i)
            st, e, i = best
            heapq.heappop(heaps[e])
            nd = nodes[i]
            start[i] = st
            if nd['kind'] == 'dma':
                issue = 900.0 if e == 'gpsimd' else 60.0
                free_at[e] = st + issue
                fin[i] = st + issue + nd['dur']
            else:
                free_at[e] = st + nd['dur']
                fin[i] = st + nd['dur']
            order[e].append(i)
            done += 1
            for j in succ[i]:
                nj = nodes[j]
                t = fin[i] + self.LAT
                if nj['deps'][i] is None:
                    t = start[i] + 1.0
                elif nd['kind'] == 'op' and nd['eng'] == nj['eng'] and nj['kind'] == 'op':
                    t = start[i] + 1.0 if (e == 'tensor' or not nj['deps'][i]) else fin[i]
                if t > ready[j]:
                    ready[j] = t
                ndep[j] -= 1
                if ndep[j] == 0:
                    heapq.heappush(heaps[nj['eng']], (ready[j], j))
        ev = [None] * n
        for e in self.engs:
            for i in order[e]:
                nd = nodes[i]
                if nd['kind'] == 'dma':
                    nm = nd['sem']
                    if nm not in self.dsem:
                        if self.free:
                            self.dsem[nm] = self.free.pop()
                        else:
                            self.nsem += 1
                            self.dsem[nm] = [self.es.enter_context(self.nc.semaphore('d%d' % self.nsem)), 0]
                    d = self.dsem[nm]
                    d[1] += 16
                    ev[i] = (d[0], d[1], 16)
                else:
                    self.ecnt[e] += 1
                    ev[i] = (self.esem[e], self.ecnt[e], 1)
        for e in self.engs:
            for i in order[e]:
                nd = nodes[i]
                waits = {}
                for d, sync in nd['deps'].items():
                    pd = nodes[d]
                    if sync is None:
                        continue
                    if pd['kind'] == 'op' and pd['eng'] == e:
                        if nd['kind'] == 'op' and (e == 'tensor' or not sync):
                            continue
                    sem, val, _ = ev[d]
                    k = id(sem)
                    if self.seen[e].get(k, 0) >= val:
                        continue
                    if k not in waits or waits[k][1] < val:
                        waits[k] = (sem, val)
                for k, (sem, val) in waits.items():
                    self.seen[e][k] = val
                self.q[e].append((list(waits.values()), nd['fn'], (ev[i][0], ev[i][2])))
        self.nodes = []
        self.buf = {}
        self.last_on_sem = {}

    def barrier(self):
        self._schedule()
        for eng in self.engs:
            waits = []
            for o in self.engs[1:]:
                if o != eng and self.ecnt[o] > 0:
                    sem, val = self.esem[o], self.ecnt[o]
                    if self.seen[eng].get(id(sem), 0) < val:
                        waits.append((sem, val))
                        self.seen[eng][id(sem)] = val
            for d in self.dsem.values():
                if d[1] > 0 and self.seen[eng].get(id(d[0]), 0) < d[1]:
                    waits.append((d[0], d[1]))
                    self.seen[eng][id(d[0])] = d[1]
            self.q[eng].append((waits, None, None))
        self.free.extend(self.dsem.values())
        self.dsem = {}

    def emit(self):
        nc = self.nc
        with nc.Block() as block:
            def mk(ename):
                def body(e):
                    for waits, fn, inc in self.q[ename]:
                        for sem, val in waits:
                            e.wait_ge(sem, val)
                        if fn is not None:
                            ins = fn(e)
                            ins.then_inc(inc[0], inc[1])
                return body
            block.sync(mk('sync'))
            block.scalar(mk('scalar'))
            block.vector(mk('vector'))
            block.gpsimd(mk('gpsimd'))
            block.tensor(mk('tensor'))


def build():
    nc = bass.Bass('TRN2', target_bir_lowering=False)

    def din(name, shape, dt=F32):
        return nc.dram_tensor(name, shape, dt, kind='ExternalInput').ap()

    def dout(name, shape, dt=F32):
        return nc.dram_tensor(name, shape, dt, kind='ExternalOutput').ap()

    def dscr(name, shape, dt):
        return nc.dram_tensor(name, shape, dt, kind='Internal').ap()

    xs = din('xs', [NBLK, 128, 2048])
    w_in = din('w_in', [2048, 2112])
    g_q = din('g_q', [1, 512])
    w_uq = din('w_uq', [512, 1536])
    w_uk = din('w_uk', [512, 1024])
    g_kv = din('g_kv', [1, 512])
    w_uv = din('w_uv', [512, 1024])
    a_re_d = din('ssm_a_re', [64, 64])
    a_im_d = din('ssm_a_im', [64, 64])
    lstep_d = din('ssm_log_step', [1, 64])
    b_re_d = din('ssm_b_re', [64, 64, 16])
    b_im_d = din('ssm_b_im', [64, 64, 16])
    c_re_d = din('ssm_c_re', [64, 16, 64])
    c_im_d = din('ssm_c_im', [64, 16, 64])
    d_d = din('ssm_d', [1, 1024])
    w_glu = din('w_glu', [1024, 1024])
    b_glu = din('b_glu', [1, 1024])
    w_out = din('w_out', [2048, 2048])
    ln1_g = din('ln1_g', [1, 2048])
    ln1_b = din('ln1_b', [1, 2048])
    w_gate = din('w_gate', [2048, DFF])
    w_up = din('w_up', [2048, DFF])
    w_down = din('w_down', [DFF, 2048])
    ln2_g = din('ln2_g', [1, 2048])
    ln2_b = din('ln2_b', [1, 2048])
    cache_ckv = din('cache_ckv', [NPOOL * 32, 2048])
    cache_kr = din('cache_kr', [NPOOL * 32, 256])
    st_re = din('st_re', [16, 4096])
    st_im = din('st_im', [16, 4096])
    pt_exp = din('pt_exp', [128, 256], I32)
    qoff = din('qoff', [128, 1])
    ropek = din('ropek', [NBLK, 128, 128])
    ropeq = din('ropeq', [NOWN, 64, 256])
    mask_pad = din('mask_pad', [128, 512])
    mask_diag = din('mask_diag', [128, 512])
    mask_smp = din('mask_smp', [64, 2048])
    identf_d = din('identf', [128, 128])

    y_o = dout('y_o', [NOWN, 128, 2048])
    ckv_o = dout('ckv_o', [NBLK, 128, 512])
    kr_o = dout('kr_o', [NBLK, 128, 64])
    stp_o = dout('stp_o', [2, 32, 128])
    sts_o = dout('sts_o', [2, 16, 4096])

    xT_d = dscr('xT_d', [NBLK, 128, 2048], BF16)
    x1_d = dscr('x1_d', [NOWN, 128, 2048], F32)
    y2_d = dscr('y2_d', [NOWN, 128, 2048], F32)
    cc_d = dscr('cc_d', [NOWN, 128, 2048], BF16)

    es = ExitStack()
    with es:
        S = Sched(nc, es)

        _sbn = [0]

        def sbuf(scope, name, shape, dt):
            _sbn[0] += 1
            return scope.enter_context(nc.sbuf_tensor('sb%d_%s' % (_sbn[0], name), shape, dt))

        def fsz(ap):
            n = 1
            for d in ap.shape[1:]:
                n *= int(d)
            return n

        def mm(out, lhsT, rhs, start, stop, R, W):
            S.op('tensor', lambda e: e.matmul(out, lhsT=lhsT, rhs=rhs, start=start, stop=stop), R, W, dur=max(64, fsz(rhs)) / 1.4 + 12.0)

        def tr(out, in_, ident, R, W):
            S.op('tensor', lambda e: e.transpose(out, in_, ident), R, W, dur=max(64, int(in_.shape[0])) / 1.4 * (2.0 if in_.dtype == F32 else 1.0) + 12.0)

        def act(out, in_, func, R, W, bias=None, scale=None, accum=None):
            kw = {}
            if bias is not None:
                kw['bias'] = bias
            if scale is not None:
                kw['scale'] = scale
            if accum is not None:
                kw['accum_out'] = accum
            S.op('scalar', lambda e: e.activation(out=out, in_=in_, func=func, **kw), R, W, dur=max(64, fsz(in_)) / 1.4 + 180.0)

        def tt(eng, out, in0, in1, op, R, W):
            S.op(eng, lambda e: e.tensor_tensor(out=out, in0=in0, in1=in1, op=op), R, W, dur=max(64, fsz(out)) * (1.05 if eng == 'vector' else 1.1) + (70.0 if eng == 'vector' else 180.0))

        def ts(eng, out, in0, s1, s2, op0, op1, R, W):
            if op1 is None:
                S.op(eng, lambda e: e.tensor_scalar(out=out, in0=in0, scalar1=s1, scalar2=None, op0=op0), R, W, dur=max(64, fsz(out)) * 1.05 + 70.0)
            else:
                S.op(eng, lambda e: e.tensor_scalar(out=out, in0=in0, scalar1=s1, scalar2=s2, op0=op0, op1=op1), R, W, dur=max(64, fsz(out)) * 1.05 + 70.0)

        def stt(out, in0, scalar, in1, op0, op1, R, W):
            S.op('vector', lambda e: e.scalar_tensor_tensor(out=out, in0=in0, scalar=scalar, in1=in1, op0=op0, op1=op1), R, W, dur=max(64, fsz(out)) * 1.05 + 70.0)

        def red(out, in_, op, R, W):
            S.op('vector', lambda e: e.tensor_reduce(out=out, in_=in_, axis=AX.X, op=op), R, W, dur=max(64, fsz(in_)) * 1.05 + 70.0)

        def cp(eng, out, in_, R, W):
            if eng == 'scalar':
                S.op(eng, lambda e: e.copy(out=out, in_=in_), R, W, dur=max(64, fsz(out)) / 1.4 + 180.0)
            else:
                S.op(eng, lambda e: e.tensor_copy(out=out, in_=in_), R, W, dur=max(64, fsz(out)) * 1.05 + (70.0 if eng == 'vector' else 180.0))

        def mset(eng, ap, val, W):
            S.op(eng, lambda e: e.memset(ap, val), (), W)

        def recip(out, in_, R, W):
            S.op('vector', lambda e: e.reciprocal(out=out, in_=in_), R, W)

        def scan(out, d0, d1, R, W):
            S.op('vector', lambda e: e.tensor_tensor_scan(out=out, data0=d0, data1=d1, initial=0.0, op0=ALU.mult, op1=ALU.add), R, W, dur=2.1 * fsz(out) + 70.0)

        def dma(eng, sem, out, in_, R, W, **kw):
            S.dma(eng, sem, lambda e: e.dma_start(out=out, in_=in_, **kw), R, W)

        def gather(sem, out, in_, idx, R, W):
            S.dma('gpsimd', sem, lambda e: e.indirect_dma_start(out=out, out_offset=None, in_=in_, in_offset=bass.IndirectOffsetOnAxis(ap=idx, axis=0)), R, W)

        G = es
        PS = [G.enter_context(nc.psum_tensor('ps%d' % i, [128, 512], F32)) for i in range(8)]
        PSK = ['PS%d' % i for i in range(8)]

        def psb(i):
            return PS[i][:].bitcast(BF16)

        identf = sbuf(G, 'identf', [128, 128], F32)
        identb = sbuf(G, 'identb', [128, 128], BF16)
        dma('sync', 'identf', identf[:], identf_d, [], ['identf'])
        cp('vector', identb[:], identf[:], ['identf'], ['identb'])

        def rmsnorm_rows(src_ps, src_key, gb, gb_key, out_ap, out_key, sq, ssv, eng_scratch_keys):
            jk = eng_scratch_keys or 'sq'
            act(sq[:], src_ps, AF.Square, [src_key], ['sq', jk], accum=ssv[:, 0:1])
            act(ssv[:, 1:2], ssv[:, 0:1], AF.Sqrt, ['sq'], ['ssv'], bias=epsq[:, 0:1], scale=1.0 / 512.0)
            recip(ssv[:, 2:3], ssv[:, 1:2], ['ssv'], ['ssv2'])
            stt(out_ap, src_ps, ssv[:, 2:3], gb, ALU.mult, ALU.mult, [src_key, 'ssv2', gb_key], [out_key])

        epsq = sbuf(G, 'epsq', [128, 2], F32)
        mset('vector', epsq[:, 0:1], 1e-6, ['epsq'])
        mset('vector', epsq[:, 1:2], 1e-5, ['epsq'])

        with ExitStack() as ph:
            xa = [sbuf(ph, 'xa%d' % i, [128, 2048], F32) for i in range(2)]
            xt = [sbuf(ph, 'xt%d' % i, [128, 16, 128], BF16) for i in range(2)]
            for blk in range(NBLK):
                b = blk % 2
                dma('sync', 'xa%d' % b, xa[b][:], xs[blk], [], ['xa%d' % b])
                for g in range(4):
                    pb = g % 2
                    for i in range(4):
                        k = 4 * g + i
                        tr(PS[pb][:, i * 128:(i + 1) * 128], xa[b][:, k * 128:(k + 1) * 128], identf[:],
                           ['xa%d' % b, 'identf'], [PSK[pb]])
                    cp('vector' if g % 2 == 0 else 'scalar', xt[b][:, 4 * g:4 * g + 4, :],
                       PS[pb][:].rearrange('p (a t) -> p a t', t=128), [PSK[pb]], ['xt%d_%d' % (b, g)])
                dma('sync', 'st_xt%d' % b, xT_d[blk].rearrange('p (k t) -> p k t', t=128), xt[b][:],
                    ['xt%d_%d' % (b, g) for g in range(4)], ['xT_d%d' % blk])
            S.barrier()


        def phase_s5():
          with ExitStack() as ph:
            TWO_PI = 6.283185307179586
            prm32 = sbuf(ph, 'prm32', [32, 3, 128], F32)
            lst2 = sbuf(ph, 'lst2', [32, 2], F32)
            prm = sbuf(ph, 'prm', [128, 3, 32], F32)
            sm = sbuf(ph, 'sm', [128, 40, 32], F32)
            smi = sbuf(ph, 'smi', [128, 32], I32)
            SMK = ['sm']

            def st(i):
                return sm[:, i, :]
            dma('sync', 'prm', prm32[:, 0, :], a_re_d.rearrange('(c g) p -> c (g p)', g=2), [], ['prm32a'])
            dma('sync', 'prm', prm32[:, 1, :], a_im_d.rearrange('(c g) p -> c (g p)', g=2), [], ['prm32b'])
            dma('sync', 'prm', lst2[:], lstep_d.rearrange('o (c g) -> (o c) g', g=2), [], ['lst2'])
            for g2 in range(2):
                cp('vector', prm32[:, 2, g2 * 64:(g2 + 1) * 64], lst2[:, g2:g2 + 1].to_broadcast([32, 64]), ['lst2'], ['prm32c%d' % g2])
            for i in range(3):
                tr(PS[0][:, i * 32:(i + 1) * 32], prm32[:, i, :], identf[0:32, 0:32],
                   ['prm32a', 'prm32b', 'prm32c0', 'prm32c1', 'identf'], [PSK[0]])
            cp('vector', prm[:], PS[0][:, 0:96].rearrange('p (a c) -> p a c', c=32), [PSK[0]], SMK)
            lam_r, lam_i, lst = prm[:, 0, :], prm[:, 1, :], prm[:, 2, :]

            def v2(out, a, b, op):
                tt('vector', out, a, b, op, SMK, SMK)

            def v1(out, a, s1, s2, op0, op1=None):
                ts('vector', out, a, s1, s2, op0, op1, SMK, SMK)

            def a1(out, a, func, scale=None, bias=None):
                act(out, a, func, SMK, SMK, bias=bias, scale=scale)
            DLT, XR, TH, MAG, MAGI, RR, KF, FF, TMP, SIN, GG, COS, AR, AI, IR, II, KR_, KI_, NUMR, DEN, T5, T6 = range(22)
            A128R, A128I, A127R, A127I, CURR, CURI = 22, 23, 24, 25, 26, 27
            a1(st(DLT), lst, AF.Exp)
            v2(st(XR), lam_r, st(DLT), ALU.mult)
            v2(st(TH), lam_i, st(DLT), ALU.mult)
            a1(st(MAG), st(XR), AF.Exp)
            a1(st(MAGI), st(XR), AF.Exp, scale=-1.0)
            v1(st(RR), st(TH), 1.0 / TWO_PI, None, ALU.mult)
            cp('vector', smi[:], st(RR), SMK, SMK)
            cp('vector', st(KF), smi[:], SMK, SMK)
            v2(st(FF), st(RR), st(KF), ALU.subtract)
            v1(st(TMP), st(FF), 0.5, None, ALU.is_gt)
            v2(st(FF), st(FF), st(TMP), ALU.subtract)
            v1(st(TMP), st(FF), -0.5, None, ALU.is_lt)
            v2(st(FF), st(FF), st(TMP), ALU.add)
            a1(st(SIN), st(FF), AF.Sin, scale=TWO_PI)
            v1(st(GG), st(FF), 0.25, None, ALU.add)
            v1(st(TMP), st(GG), 0.5, None, ALU.is_gt)
            v2(st(GG), st(GG), st(TMP), ALU.subtract)
            a1(st(COS), st(GG), AF.Sin, scale=TWO_PI)
            v2(st(AR), st(MAG), st(COS), ALU.mult)
            v2(st(AI), st(MAG), st(SIN), ALU.mult)
            v2(st(IR), st(MAGI), st(COS), ALU.mult)
            v2(st(II), st(MAGI), st(SIN), ALU.mult)
            v1(st(II), st(II), -1.0, None, ALU.mult)
            v1(st(NUMR), st(AR), -1.0, None, ALU.add)
            v2(st(DEN), lam_r, lam_r, ALU.mult)
            v2(st(T5), lam_i, lam_i, ALU.mult)
            v2(st(DEN), st(DEN), st(T5), ALU.add)
            recip(st(DEN), st(DEN), SMK, SMK)
            v2(st(T5), st(NUMR), lam_r, ALU.mult)
            v2(st(T6), st(AI), lam_i, ALU.mult)
            v2(st(T5), st(T5), st(T6), ALU.add)
            v2(st(KR_), st(T5), st(DEN), ALU.mult)
            v2(st(T5), st(AI), lam_r, ALU.mult)
            v2(st(T6), st(NUMR), lam_i, ALU.mult)
            v2(st(T5), st(T5), st(T6), ALU.subtract)
            v2(st(KI_), st(T5), st(DEN), ALU.mult)

            if SUB < 1:
                S.barrier()
                return
            BLF = sbuf(ph, 'BLF', [128, 32, 2, 128], BF16)
            ApowT = sbuf(ph, 'ApowT', [128, 2, 32, 128], BF16)
            Bz = sbuf(ph, 'Bz', [128, 2, 32, 32], F32)
            u_tm = sbuf(ph, 'u_tm', [128, 1024], BF16)
            CL3 = sbuf(ph, 'CL3', [128, 8, 2, 64], BF16)
            CLr = sbuf(ph, 'CLr', [128, 32, 32], BF16)
            CLn = sbuf(ph, 'CLn', [128, 32, 32], BF16)
            Tp_r = sbuf(ph, 'Tp_r', [128, 32, 128], BF16)
            Tp_i = sbuf(ph, 'Tp_i', [128, 32, 128], BF16)
            Tn_r = sbuf(ph, 'Tn_r', [128, 32, 128], BF16)
            Tn_i = sbuf(ph, 'Tn_i', [128, 32, 128], BF16)
            with ExitStack() as p2:
                Bn_r = sbuf(p2, 'Bn_r', [128, 32, 16], F32)
                Bn_i = sbuf(p2, 'Bn_i', [128, 32, 16], F32)
                Bb = sbuf(p2, 'Bb', [128, 2, 32, 16], F32)
                Bt = sbuf(p2, 'Bt', [128, 32, 16], F32)
                Cz = sbuf(p2, 'Cz', [128, 2, 8, 128], F32)
                for g2 in range(2):
                    dma('sync', 'Bn', Bn_r[g2 * 64:(g2 + 1) * 64, :, :], b_re_d.rearrange('(c g) p h -> g p c h', g=2)[g2], [], ['Bn_r%d' % g2])
                    dma('sync', 'Bn', Bn_i[g2 * 64:(g2 + 1) * 64, :, :], b_im_d.rearrange('(c g) p h -> g p c h', g=2)[g2], [], ['Bn_i%d' % g2])
                BNK = ['Bn_r0', 'Bn_r1', 'Bn_i0', 'Bn_i1']
                kr_b = st(KR_).unsqueeze(2).to_broadcast([128, 32, 16])
                ki_b = st(KI_).unsqueeze(2).to_broadcast([128, 32, 16])
                tt('vector', Bb[:, 0], Bn_r[:], kr_b, ALU.mult, BNK + SMK, ['Bb'])
                tt('vector', Bt[:], Bn_i[:], ki_b, ALU.mult, BNK + SMK, ['Bt'])
                tt('vector', Bb[:, 0], Bb[:, 0], Bt[:], ALU.subtract, ['Bb', 'Bt'], ['Bb'])
                tt('vector', Bb[:, 1], Bn_i[:], kr_b, ALU.mult, BNK + SMK, ['Bb'])
                tt('vector', Bt[:], Bn_r[:], ki_b, ALU.mult, BNK + SMK + ['Bb'], ['Bt'])
                tt('vector', Bb[:, 1], Bb[:, 1], Bt[:], ALU.add, ['Bb', 'Bt'], ['Bb'])
                mset('vector', Bz[:], 0.0, ['Bz'])
                for ri in range(2):
                    cp('vector', Bz[0:64, ri, :, 0:16], Bb[0:64, ri], ['Bb', 'Bz'], ['Bz'])
                    cp('vector', Bz[64:128, ri, :, 16:32], Bb[64:128, ri], ['Bb', 'Bz'], ['Bz'])
                BzF = sbuf(p2, 'BzF', [128, 32, 128], F32)
                for ri in range(2):
                    mset('vector', BzF[:], 0.0, ['BzF'])
                    bzf_v = BzF[:].rearrange('p (q a) (b h) -> p q a b h', a=4, h=32)
                    bz_v = Bz[:, ri].rearrange('p (q a) h -> p q a h', a=4)
                    for cl in range(4):
                        cp('vector', bzf_v[:, :, cl, cl, :], bz_v[:, :, cl, :], ['Bz', 'BzF'], ['BzF'])
                    for c in range(32):
                        pb = c % 2
                        tr(PS[pb][:, 0:128], BzF[:, c, :], identf[:], ['BzF', 'identf'], [PSK[pb]])
                        cp('vector' if c % 2 == 0 else 'scalar', BLF[:, c, ri, :], PS[pb][:, 0:128], [PSK[pb]], ['BL'])
                mset('vector', CL3[:], 0.0, ['CL3'])
                if SUB < 2:
                    S.barrier()
                    return
                mset('vector', Cz[:], 0.0, ['Cz'])
                for ri, cd in enumerate((c_re_d, c_im_d)):
                    for cl in range(4):
                        for g2 in range(2):
                            r0 = 32 * cl + 16 * g2
                            dma('sync', 'Cz', Cz[r0:r0 + 16, ri, :, 64 * g2:64 * g2 + 64],
                                cd.rearrange('(q r) h p -> r h q p', r=8)[2 * cl + g2], ['Cz'], ['Cz_%d_%d_%d' % (ri, cl, g2)])
                CZK = ['Cz_%d_%d_%d' % (ri, cl, g2) for ri in range(2) for cl in range(4) for g2 in range(2)]
                for ri in range(2):
                    for q in range(8):
                        pb = q % 2
                        tr(PS[pb][:, 0:128], Cz[:, ri, q, :], identf[:], CZK + ['identf'], [PSK[pb]])
                        src = PS[pb][:, 0:128].rearrange('p (a b) -> p a b', b=32)
                        if ri == 0:
                            cp('vector', CLr[:, 4 * q:4 * q + 4, :], src, [PSK[pb]], ['CLr'])
                            cp('vector', CL3[:, q, 0, 32:64], src[:, 3, :], [PSK[pb], 'CL3'], ['CL3'])
                        else:
                            S.op('scalar', lambda e, o=CLn[:, 4 * q:4 * q + 4, :], s=src: e.mul(o, s, -1.0), [PSK[pb]], ['CLn'])
                            S.op('scalar', lambda e, o=CL3[:, q, 1, 32:64], s=src[:, 3, :]: e.mul(o, s, -1.0), [PSK[pb], 'CL3'], ['CL3'])
                if SUB < 3:
                    S.barrier()
                    return
                TFr = sbuf(p2, 'TFr', [128, 32, 128], F32)
                TFi = sbuf(p2, 'TFi', [128, 32, 128], F32)
                tm1 = sbuf(p2, 'tm1', [128, 32, 64], F32)
                tm2 = sbuf(p2, 'tm2', [128, 32, 64], F32)
                TK = ['tab']
                for which in range(2):
                    br, bi = (AR, AI) if which == 0 else (IR, II)
                    cp('vector', st(CURR), st(br), SMK, SMK)
                    cp('vector', st(CURI), st(bi), SMK, SMK)
                    mset('vector', TFr[:, :, 0:1], 1.0, TK)
                    mset('vector', TFi[:, :, 0:1], 0.0, TK)
                    for k in range(7):
                        n = 1 << k
                        cr = st(CURR).unsqueeze(2).to_broadcast([128, 32, n])
                        ci = st(CURI).unsqueeze(2).to_broadcast([128, 32, n])
                        tt('vector', tm1[:, :, 0:n], TFr[:, :, 0:n], cr, ALU.mult, TK + SMK, TK)
                        tt('vector', tm2[:, :, 0:n], TFi[:, :, 0:n], ci, ALU.mult, TK + SMK, TK)
                        tt('vector', TFr[:, :, n:2 * n], tm1[:, :, 0:n], tm2[:, :, 0:n], ALU.subtract, TK, TK)
                        tt('vector', tm1[:, :, 0:n], TFr[:, :, 0:n], ci, ALU.mult, TK + SMK, TK)
                        tt('vector', tm2[:, :, 0:n], TFi[:, :, 0:n], cr, ALU.mult, TK + SMK, TK)
                        tt('vector', TFi[:, :, n:2 * n], tm1[:, :, 0:n], tm2[:, :, 0:n], ALU.add, TK, TK)
                        v2(st(T5), st(CURR), st(CURR), ALU.mult)
                        v2(st(T6), st(CURI), st(CURI), ALU.mult)
                        v2(st(TMP), st(CURR), st(CURI), ALU.mult)
                        v2(st(CURR), st(T5), st(T6), ALU.subtract)
                        v1(st(CURI), st(TMP), 2.0, None, ALU.mult)
                    if which == 0:
                        cp('vector', st(A128R), st(CURR), SMK, SMK)
                        cp('vector', st(A128I), st(CURI), SMK, SMK)
                        cp('vector', st(A127R), TFr[:, :, 127], TK + SMK, SMK)
                        cp('vector', st(A127I), TFi[:, :, 127], TK + SMK, SMK)
                        cp('vector', Tp_r[:], TFr[:], TK, ['Tp'])
                        cp('vector', Tp_i[:], TFi[:], TK, ['Tp'])
                    else:
                        cp('vector', Tn_r[:], TFr[:], TK, ['Tn'])
                        cp('vector', Tn_i[:], TFi[:], TK, ['Tn'])
                        TRb = sbuf(p2, 'TRb', [128, 2, 32, 128], BF16)
                        a7r = st(A127R).unsqueeze(2).to_broadcast([128, 32, 128])
                        a7i = st(A127I).unsqueeze(2).to_broadcast([128, 32, 128])
                        for hh_ in range(2):
                            sl_ = slice(hh_ * 64, (hh_ + 1) * 64)
                            a7r_ = st(A127R).unsqueeze(2).to_broadcast([128, 32, 64])
                            a7i_ = st(A127I).unsqueeze(2).to_broadcast([128, 32, 64])
                            tt('vector', tm1[:], TFr[:, :, sl_], a7r_, ALU.mult, TK + SMK, TK)
                            tt('vector', tm2[:], TFi[:, :, sl_], a7i_, ALU.mult, TK + SMK, TK)
                            tt('vector', TRb[:, 0, :, sl_], tm1[:], tm2[:], ALU.subtract, TK, ['TRb'])
                            tt('vector', tm1[:], TFr[:, :, sl_], a7i_, ALU.mult, TK + SMK + ['TRb'], TK)
                            tt('vector', tm2[:], TFi[:, :, sl_], a7r_, ALU.mult, TK + SMK, TK)
                            tt('vector', TRb[:, 1, :, sl_], tm1[:], tm2[:], ALU.add, TK, ['TRb'])
                        for ri in range(2):
                            for c8 in range(4):
                                pb = c8 % 2
                                for i in range(8):
                                    c = c8 * 8 + i
                                    tr(psb(pb)[:, i * 128:(i + 1) * 128], TRb[:, ri, c, :], identb[:], ['TRb', 'identb'], [PSK[pb]])
                                cp('vector' if c8 % 2 == 0 else 'scalar', ApowT[:, ri, c8 * 8:(c8 + 1) * 8, :], psb(pb)[:, :].rearrange('p (a b) -> p a b', b=128), [PSK[pb]], ['ApowT'])
                S.barrier()

            if SUB < 4:
                S.barrier()
                return
            wB = sbuf(ph, 'wB', [128, 16, 1024], BF16)
            wgl = sbuf(ph, 'wgl', [128, 8, 1024], BF16)
            Dt = sbuf(ph, 'Dt', [128, 8], F32)
            bg = sbuf(ph, 'bg', [128, 8], F32)
            w_in_v = w_in.rearrange('(k p) n -> p k n', p=128)
            for k4 in range(4):
                dma('gpsimd', 'wB', wB[:, 4 * k4:4 * k4 + 4, :], w_in_v[:, 4 * k4:4 * k4 + 4, 0:1024], [], ['wB%d' % k4])
            WBK = ['wB%d' % k4 for k4 in range(4)]
            w_glu_v = w_glu.rearrange('(k p) n -> p k n', p=128)
            for k4 in range(2):
                dma('gpsimd', 'wgl', wgl[:, 4 * k4:4 * k4 + 4, :], w_glu_v[:, 4 * k4:4 * k4 + 4, :], [], ['wgl%d' % k4])
            WGK = ['wgl0', 'wgl1']
            dma('sync', 'Dt', Dt[:], d_d.rearrange('o (k p) -> p (o k)', p=128), [], ['Dt'], allow_slow_non_contiguous=True)
            dma('sync', 'bg', bg[:], b_glu.rearrange('o (k p) -> p (o k)', p=128), [], ['bg'], allow_slow_non_contiguous=True)

            xT = [sbuf(ph, 'xT%d' % i, [128, 16, 128], BF16) for i in range(2)]
            uT_f = sbuf(ph, 'uT_f', [128, 8, 128], F32)
            uT_b = sbuf(ph, 'uT_b', [128, 8, 128], BF16)
            gT = sbuf(ph, 'gT', [128, 8, 128], BF16)
            tmp = [[sbuf(ph, 't%d_%d' % (p, i), [128, 512], F32) for i in range(4)] for p in range(2)]
            z_r = [sbuf(ph, 'z_r%d' % p, [128, 512], F32) for p in range(2)]
            z_i = [sbuf(ph, 'z_i%d' % p, [128, 512], F32) for p in range(2)]
            G_r = [sbuf(ph, 'G_r%d' % p, [128, 512], F32) for p in range(1)]
            G_i = [sbuf(ph, 'G_i%d' % p, [128, 512], F32) for p in range(1)]
            h_r = [sbuf(ph, 'h_r%d' % p, [128, 512], F32) for p in range(1)]
            h_i = [sbuf(ph, 'h_i%d' % p, [128, 512], F32) for p in range(1)]
            hb_r = [sbuf(ph, 'hb_r%d' % p, [128, 4, 128], BF16) for p in range(1)]
            hb_i = [sbuf(ph, 'hb_i%d' % p, [128, 4, 128], BF16) for p in range(1)]
            yv = [sbuf(ph, 'yv%d' % p, [128, 128], F32) for p in range(2)]
            ge = [[sbuf(ph, 'ge%d_%d' % (p, i), [128, 128], F32) for i in range(2)] for p in range(2)]
            sg = [sbuf(ph, 'sg%d' % p, [128, 128], F32) for p in range(1)]
            H_r = sbuf(ph, 'H_r', [128, 32, 16], F32)
            H_i = sbuf(ph, 'H_i', [128, 32, 16], F32)
            CA_r = sbuf(ph, 'CA_r', [128, 32, 16], F32)
            CA_i = sbuf(ph, 'CA_i', [128, 32, 16], F32)
            ctm = sbuf(ph, 'ctm', [128, 32, 16], F32)
            Ss_r = sbuf(ph, 'Ss_r', [128, 32], F32)
            Ss_i = sbuf(ph, 'Ss_i', [128, 32], F32)
            ccs = sbuf(ph, 'ccs', [128, 8, 128], BF16)
            smask_p = sbuf(ph, 'smask_p', [128, 128], F32)
            smask_s = sbuf(ph, 'smask_s', [128, 128], F32)
            s0 = sbuf(ph, 's0', [16, 2, 512], F32)
            mset('vector', smask_p[:], 1.0, ['smask'])
            mset('vector', smask_p[:, 0:1], 0.0, ['smask'])
            mset('vector', smask_s[:], 1.0, ['smask'])
            mset('vector', smask_s[:].rearrange('p (s t) -> p s t', t=8)[:, :, 0:1], 0.0, ['smask'])
            mset('vector', H_r[:], 0.0, ['H'])
            mset('vector', H_i[:], 0.0, ['H'])
            HK = ['H']

            def cmul_b(o_r, o_i, x_r, x_i, y_r, y_i, t, R, W):
                tt('vector', o_r, x_r, y_r, ALU.mult, R, W)
                tt('vector', t, x_i, y_i, ALU.mult, R, ['ctm'])
                tt('vector', o_r, o_r, t, ALU.subtract, W + ['ctm'], W)
                tt('vector', o_i, x_r, y_i, ALU.mult, R, W)
                tt('vector', t, x_i, y_r, ALU.mult, R + W, ['ctm'])
                tt('vector', o_i, o_i, t, ALU.add, W + ['ctm'], W)

            for blk in range(NBLK_RUN):
                own = blk in OWN
                ob = OWN.index(blk) if own else -1
                smp = blk == 32
                ns, tlen = (16, 8) if smp else (1, 128)
                smask = smask_s if smp else smask_p
                xb = blk % 2
                dma('sync', 'xT%d' % xb, xT[xb][:], xT_d[blk].rearrange('p (k t) -> p k t', t=128), ['xT_d%d' % blk], ['xT%d' % xb])
                if smp:
                    for qtr in range(8):
                        dma('sync', 's0r', s0[:, 0, :], st_re[:, qtr * 512:(qtr + 1) * 512], [], ['s0r'])
                        dma('sync', 's0i', s0[:, 1, :], st_im[:, qtr * 512:(qtr + 1) * 512], [], ['s0i'])
                        for ri in range(2):
                            for c8 in range(4):
                                c = qtr * 4 + c8
                                tr(PS[6 + ri][:, c * 16:(c + 1) * 16], s0[:, ri, c8 * 128:(c8 + 1) * 128], identf[0:16, 0:16], ['s0r', 's0i', 'identf'], [PSK[6 + ri]])
                    for ri, Hx in enumerate((H_r, H_i)):
                        cp('vector', Hx[:], PS[6 + ri][:].rearrange('p (c s) -> p c s', s=16), [PSK[6 + ri]], HK)
                if own:
                    arb = st(AR).unsqueeze(2).to_broadcast([128, 32, ns])
                    aib = st(AI).unsqueeze(2).to_broadcast([128, 32, ns])
                    cmul_b(CA_r[:, :, 0:ns], CA_i[:, :, 0:ns], H_r[:, :, 0:ns], H_i[:, :, 0:ns], arb, aib, ctm[:, :, 0:ns], HK + SMK, ['CA'])
                if not own:
                    for n_ in range(2):
                        for k in range(16):
                            mm(PS[n_][:, 0:512], xT[xb][:, k, :], wB[:, k, n_ * 512:(n_ + 1) * 512], k == 0, k == 15, WBK + ['xT%d' % xb], [PSK[n_]])
                        cp('scalar', u_tm[:, n_ * 512:(n_ + 1) * 512], PS[n_][:, 0:512], [PSK[n_]], ['u_tm%d' % n_])
                    for hf in range(2):
                        b_re, b_im = (2, 3) if hf == 0 else (4, 5)
                        for cc in range(16):
                            c = hf * 16 + cc
                            mm(PS[b_re][:, cc * 32:(cc + 1) * 32], ApowT[:, 0, c, :], u_tm[:, c * 32:(c + 1) * 32], True, True, ['ApowT', 'u_tm0', 'u_tm1'], [PSK[b_re]])
                            mm(PS[b_im][:, cc * 32:(cc + 1) * 32], ApowT[:, 1, c, :], u_tm[:, c * 32:(c + 1) * 32], True, True, ['ApowT', 'u_tm0', 'u_tm1'], [PSK[b_im]])
                        vr = PS[b_re][:, 0:512].rearrange('p (c h) -> p c h', h=32)
                        vi = PS[b_im][:, 0:512].rearrange('p (c h) -> p c h', h=32)
                        bzr = Bz[:, 0, hf * 16:(hf + 1) * 16, :]
                        bzi = Bz[:, 1, hf * 16:(hf + 1) * 16, :]
                        t1, t2, t3, t4 = [tmp[hf][i][:].rearrange('p (c h) -> p c h', h=32) for i in range(4)]
                        tk = ['t%d_%d' % (hf, i) for i in range(4)]
                        tt('vector', t1, vr, bzr, ALU.mult, [PSK[b_re], 'Bz'], [tk[0]])
                        tt('vector', t2, vi, bzi, ALU.mult, [PSK[b_im], 'Bz'], [tk[1]])
                        tt('gpsimd', t1, t1, t2, ALU.subtract, [tk[0], tk[1]], [tk[0]])
                        red(Ss_r[:, hf * 16:(hf + 1) * 16], t1, ALU.add, [tk[0]], ['Ss'])
                        tt('vector', t3, vi, bzr, ALU.mult, [PSK[b_im], 'Bz'], [tk[2]])
                        tt('vector', t4, vr, bzi, ALU.mult, [PSK[b_re], 'Bz'], [tk[3]])
                        tt('gpsimd', t3, t3, t4, ALU.add, [tk[2], tk[3]], [tk[2]])
                        red(Ss_i[:, hf * 16:(hf + 1) * 16], t3, ALU.add, [tk[2]], ['Ss'])
                    Hr0, Hi0 = H_r[:, :, 0], H_i[:, :, 0]
                    cmul_b(st(T5), st(T6), Hr0, Hi0, st(A128R), st(A128I), st(TMP), HK + SMK, SMK)
                    tt('vector', Hr0, st(T5), Ss_r[:], ALU.add, SMK + ['Ss'], HK)
                    tt('vector', Hi0, st(T6), Ss_i[:], ALU.add, SMK + ['Ss'], HK)
                    continue
                for m in range(8):
                    bank, slot = m // 4, m % 4
                    for k in range(16):
                        mm(PS[bank][:, slot * 128:(slot + 1) * 128], wB[:, k, m * 128:(m + 1) * 128], xT[xb][:, k, :], k == 0, k == 15,
                           WBK + ['xT%d' % xb], [PSK[bank]])
                cp('scalar', uT_f[:, 0:4, :], PS[0][:].rearrange('p (a t) -> p a t', t=128), [PSK[0]], ['uT_f0'])
                cp('vector', uT_f[:, 4:8, :], PS[1][:].rearrange('p (a t) -> p a t', t=128), [PSK[1]], ['uT_f1'])
                cp('gpsimd', uT_b[:], uT_f[:], ['uT_f0', 'uT_f1'], ['uT_b'])
                if SUB2 < 1:
                    continue
                for o in range(8):
                    p = o % 2
                    br_, bi_ = (2, 3) if o % 2 == 0 else (4, 5)
                    for cl in range(4):
                        l0, l1, rr_ = BLF[:, 4 * o + cl, 0, :], BLF[:, 4 * o + cl, 1, :], uT_b[:, o, :]
                        mm(PS[br_][:, cl * 128:(cl + 1) * 128], l0, rr_, True, True, ['BL', 'uT_b'], [PSK[br_]])
                        mm(PS[bi_][:, cl * 128:(cl + 1) * 128], l1, rr_, True, True, ['BL', 'uT_b'], [PSK[bi_]])

                    def v4(ap):
                        return ap.rearrange('p (c s t) -> p c s t', c=4, t=tlen)

                    def tb(T):
                        return T[:, 4 * o:4 * o + 4, 0:tlen].unsqueeze(2).to_broadcast([128, 4, ns, tlen])
                    if SUB2 < 2:
                        continue
                    pre, pim = v4(PS[br_][:]), v4(PS[bi_][:])
                    t1, t2, t3, t4 = [v4(tmp[p][i][:]) for i in range(4)]
                    tk = ['t%d_%d' % (p, i) for i in range(4)]
                    tt('vector', t1, pre, tb(Tn_r), ALU.mult, [PSK[br_], 'Tn'], [tk[0]])
                    tt('vector', t2, pim, tb(Tn_i), ALU.mult, [PSK[bi_], 'Tn'], [tk[1]])
                    if SUB2 < 3:
                        continue
                    tt('gpsimd', v4(z_r[p][:]), t1, t2, ALU.subtract, [tk[0], tk[1]], ['z_r%d' % p])
                    tt('vector', t3, pre, tb(Tn_i), ALU.mult, [PSK[br_], 'Tn'], [tk[2]])
                    tt('vector', t4, pim, tb(Tn_r), ALU.mult, [PSK[bi_], 'Tn'], [tk[3]])
                    tt('gpsimd', v4(z_i[p][:]), t3, t4, ALU.add, [tk[2], tk[3]], ['z_i%d' % p])
                    if SUB2 < 4:
                        continue
                    if not own:
                        red(Ss_r[:, 4 * o:4 * o + 4], z_r[p][:].rearrange('p (c t) -> p c t', c=4), ALU.add, ['z_r%d' % p], ['Ss'])
                        red(Ss_i[:, 4 * o:4 * o + 4], z_i[p][:].rearrange('p (c t) -> p c t', c=4), ALU.add, ['z_i%d' % p], ['Ss'])
                        continue
                    zr0 = v4(z_r[p][:])[:, :, :, 0]
                    zi0 = v4(z_i[p][:])[:, :, :, 0]
                    tt('vector', zr0, zr0, CA_r[:, 4 * o:4 * o + 4, 0:ns], ALU.add, ['z_r%d' % p, 'CA'], ['z_r%d' % p])
                    tt('vector', zi0, zi0, CA_i[:, 4 * o:4 * o + 4, 0:ns], ALU.add, ['z_i%d' % p, 'CA'], ['z_i%d' % p])
                    for cl in range(4):
                        scan(G_r[0][:, cl * 128:(cl + 1) * 128], smask[:], z_r[p][:, cl * 128:(cl + 1) * 128], ['smask', 'z_r%d' % p], ['G_r0'])
                        scan(G_i[0][:, cl * 128:(cl + 1) * 128], smask[:], z_i[p][:, cl * 128:(cl + 1) * 128], ['smask', 'z_i%d' % p], ['G_i0'])
                    gr, gi = v4(G_r[0][:]), v4(G_i[0][:])
                    tt('vector', t1, gr, tb(Tp_r), ALU.mult, ['G_r0', 'Tp'], [tk[0]])
                    tt('vector', t2, gi, tb(Tp_i), ALU.mult, ['G_i0', 'Tp'], [tk[1]])
                    tt('gpsimd', v4(h_r[0][:]), t1, t2, ALU.subtract, [tk[0], tk[1]], ['h_r0'])
                    tt('vector', t3, gr, tb(Tp_i), ALU.mult, ['G_r0', 'Tp'], [tk[2]])
                    tt('vector', t4, gi, tb(Tp_r), ALU.mult, ['G_i0', 'Tp'], [tk[3]])
                    tt('gpsimd', v4(h_i[0][:]), t3, t4, ALU.add, [tk[2], tk[3]], ['h_i0'])
                    cp('scalar', hb_r[0][:], h_r[0][:].rearrange('p (c t) -> p c t', c=4), ['h_r0'], ['hb_r0'])
                    cp('scalar', hb_i[0][:], h_i[0][:].rearrange('p (c t) -> p c t', c=4), ['h_i0'], ['hb_i0'])
                    cp('scalar', H_r[:, 4 * o:4 * o + 4, 0:ns], v4(h_r[0][:])[:, :, :, tlen - 1], ['h_r0', 'CA'], HK)
                    cp('scalar', H_i[:, 4 * o:4 * o + 4, 0:ns], v4(h_i[0][:])[:, :, :, tlen - 1], ['h_i0', 'CA'], HK)
                    py = 6 + o % 2
                    YR = ['CLr', 'CLn', 'CL3', 'hb_r0', 'hb_i0']
                    mm(PS[py][64:128, 0:128], CL3[:, o, 0, :], hb_r[0][:, 3, :], True, False, YR, [PSK[py]])
                    mm(PS[py][64:128, 0:128], CL3[:, o, 1, :], hb_i[0][:, 3, :], False, False, YR, [PSK[py]])
                    mm(PS[py][64:96, 0:128], CLr[:, 4 * o + 2, :], hb_r[0][:, 2, :], False, False, YR, [PSK[py]])
                    mm(PS[py][64:96, 0:128], CLn[:, 4 * o + 2, :], hb_i[0][:, 2, :], False, True, YR, [PSK[py]])
                    for cl in range(2):
                        c = 4 * o + cl
                        mm(PS[py][32 * cl:32 * cl + 32, 0:128], CLr[:, c, :], hb_r[0][:, cl, :], True, False, YR, [PSK[py]])
                        mm(PS[py][32 * cl:32 * cl + 32, 0:128], CLn[:, c, :], hb_i[0][:, cl, :], False, True, YR, [PSK[py]])
                    stt(yv[p][:], uT_f[:, o, :], Dt[:, o:o + 1], PS[py][:, 0:128], ALU.mult, ALU.add, ['uT_f0', 'uT_f1', 'Dt', PSK[py]], ['yv%d' % p])
                    act(ge[p][0][:], yv[p][:], AF.Square, ['yv%d' % p], ['ge%d_0' % p])
                    ts('vector', ge[p][0][:], ge[p][0][:], 0.044715, 1.0, ALU.mult, ALU.add, ['ge%d_0' % p], ['ge%d_0' % p])
                    tt('vector', ge[p][1][:], ge[p][0][:], yv[p][:], ALU.mult, ['ge%d_0' % p, 'yv%d' % p], ['ge%d_1' % p])
                    act(ge[p][0][:], ge[p][1][:], AF.Sigmoid, ['ge%d_1' % p], ['ge%d_0' % p], scale=1.5957691216057308)
                    tt('vector', gT[:, o, :], yv[p][:], ge[p][0][:], ALU.mult, ['yv%d' % p, 'ge%d_0' % p], ['gT%d' % o])
                if own:
                    GTK = ['gT%d' % o for o in range(8)]
                    for m in range(8):
                        p = 0
                        pg = 6 + m % 2
                        for k in range(8):
                            mm(PS[pg][:, 128:256], wgl[:, k, m * 128:(m + 1) * 128], gT[:, k, :], k == 0, k == 7, WGK + GTK, [PSK[pg]])
                        act(sg[p][:], PS[pg][:, 128:256], AF.Sigmoid, [PSK[pg], 'bg'], ['sg%d' % p], bias=bg[:, m:m + 1])
                        tt('vector', ccs[:, m, :], gT[:, m, :], sg[p][:], ALU.mult, GTK + ['sg%d' % p], ['ccs%d' % m])
                    dma('sync', 'ccs', cc_d[ob].rearrange('p (k t) -> p k t', t=128)[:, 0:8, :], ccs[:], ['ccs%d' % m for m in range(8)], ['cc_d'])
                    if blk == 31:
                        tr(PS[0][0:32, 0:128], H_r[:, :, 0], identf[:], HK + ['identf'], [PSK[0]])
                        tr(PS[0][0:32, 128:256], H_i[:, :, 0], identf[:], HK + ['identf'], [PSK[0]])
                        stp_t = tmp[1][1][0:32, 0:256].rearrange('p (a b) -> p a b', b=128)
                        cp('vector', stp_t, PS[0][0:32, 0:256].rearrange('p (a b) -> p a b', b=128), [PSK[0]], ['t1_1'])
                        dma('sync', 'stp_o', stp_o.rearrange('a c f -> c a f'), stp_t, ['t1_1'], ['stp_o'])
                    if smp:
                        for ri, Hx in enumerate((H_r, H_i)):
                            for c4 in range(8):
                                for i in range(4):
                                    c = 4 * c4 + i
                                    tr(PS[c4 % 2][0:16, i * 128:(i + 1) * 128], Hx[:, c, :], identf[:], HK + ['identf'], [PSK[c4 % 2]])
                                sto_ = tmp[c4 % 2][0][0:16, :]
                                cp('vector' if c4 % 2 == 0 else 'scalar', sto_, PS[c4 % 2][0:16, :], [PSK[c4 % 2]], ['t%d_0' % (c4 % 2)])
                                dma('sync', 'sts_o%d' % (c4 % 2), sts_o[ri, :, c4 * 512:(c4 + 1) * 512], sto_, ['t%d_0' % (c4 % 2)], ['sts_o'])
                elif SUB2 >= 5:
                    Hr0, Hi0 = H_r[:, :, 0], H_i[:, :, 0]
                    cmul_b(st(T5), st(T6), Hr0, Hi0, st(A128R), st(A128I), st(TMP), HK + SMK, SMK)
                    cmul_b(st(NUMR), st(DEN), Ss_r[:], Ss_i[:], st(A127R), st(A127I), st(TMP), ['Ss'] + SMK, SMK)
                    tt('vector', Hr0, st(T5), st(NUMR), ALU.add, SMK, HK)
                    tt('vector', Hi0, st(T6), st(DEN), ALU.add, SMK, HK)
            S.barrier()

        def phase_attn():
          with ExitStack() as ph:
            CKV = sbuf(ph, 'CKV', [128, NBLK, 512], BF16)
            KT = sbuf(ph, 'KT', [128, 4, NBLK * 128], BF16)
            KRT = sbuf(ph, 'KRT', [64, NBLK * 128], BF16)
            ssv = sbuf(ph, 'ssv', [128, 4], F32)
            w_in_v = w_in.rearrange('(k p) n -> p k n', p=128)
            with ExitStack() as pa:
                wA = sbuf(pa, 'wA', [128, 16, 640], BF16)
                gkv_b = sbuf(pa, 'gkv_b', [128, 512], F32)
                sq = sbuf(pa, 'sq', [128, 512], F32)
                xT = [sbuf(pa, 'xTa%d' % i, [128, 16, 128], BF16) for i in range(2)]
                dma('sync', 'gkv_b', gkv_b[:], g_kv.partition_broadcast(128), [], ['gkv_b'])
                ckv_f = [sbuf(pa, 'ckv_f%d' % i, [128, 512], F32) for i in range(2)]
                rk = [sbuf(pa, 'rk%d' % i, [128, 128], F32) for i in range(2)]
                krt = [sbuf(pa, 'krt%d' % i, [128, 128], F32) for i in range(2)]
                kr_f = [sbuf(pa, 'kr_f%d' % i, [128, 64], F32) for i in range(2)]
                krb = [sbuf(pa, 'krb%d' % i, [128, 64], BF16) for i in range(2)]
                for k4 in range(4):
                    dma('gpsimd', 'wA', wA[:, 4 * k4:4 * k4 + 4, 0:576], w_in_v[:, 4 * k4:4 * k4 + 4, 1536:2112], [], ['wA%d' % k4])
                WAK = ['wA%d' % k4 for k4 in range(4)]
                ts('vector', wA[:, :, 576:608], wA[:, :, 544:576], -1.0, None, ALU.mult, None, WAK, ['wArot0'])
                cp('vector', wA[:, :, 608:640], wA[:, :, 512:544], WAK, ['wArot1'])
                WAK = WAK + ['wArot0', 'wArot1']
                for blk in range(NBLK):
                    b = blk % 2
                    p0, p1, p2, p3 = (0, 1, 2, 3) if b == 0 else (4, 5, 6, 7)
                    dma('sync', 'xTa%d' % b, xT[b][:], xT_d[blk].rearrange('p (k t) -> p k t', t=128), [], ['xTa%d' % b])
                    dma('sync', 'rk%d' % b, rk[b][:], ropek[blk], [], ['rk%d' % b])
                    for k in range(16):
                        mm(PS[p0][:, 0:512], xT[b][:, k, :], wA[:, k, 0:512], k == 0, k == 15, WAK + ['xTa%d' % b], [PSK[p0]])
                    for k in range(16):
                        mm(PS[p1][:, 0:128], xT[b][:, k, :], wA[:, k, 512:640], k == 0, k == 15, WAK + ['xTa%d' % b], [PSK[p1]])
                    rmsnorm_rows(PS[p0][:, 0:512], PSK[p0], gkv_b[:], 'gkv_b', ckv_f[b][:], 'ckv_f%d' % b, sq, ssv, None)
                    cp('gpsimd', CKV[:, blk, :], ckv_f[b][:], ['ckv_f%d' % b], ['CKV%d' % blk])
                    dma('sync', 'o_ckv%d' % b, ckv_o[blk], ckv_f[b][:], ['ckv_f%d' % b], ['ckv_o'])
                    tt('vector', krt[b][:], PS[p1][:, 0:128], rk[b][:], ALU.mult, [PSK[p1], 'rk%d' % b], ['krt%d' % b])
                    tt('gpsimd', kr_f[b][:], krt[b][:, 0:64], krt[b][:, 64:128], ALU.add, ['krt%d' % b], ['kr_f%d' % b])
                    dma('sync', 'o_kr%d' % b, kr_o[blk], kr_f[b][:], ['kr_f%d' % b], ['kr_o'])
                    cp('gpsimd', krb[b][:], kr_f[b][:], ['kr_f%d' % b], ['krb%d' % b])
                    for c in range(4):
                        tr(psb(p2)[:, c * 128:(c + 1) * 128], CKV[:, blk, c * 128:(c + 1) * 128], identb[:], ['CKV%d' % blk, 'identb'], [PSK[p2]])
                    cp('scalar', KT[:, :, blk * 128:(blk + 1) * 128], psb(p2)[:, 0:512].rearrange('p (c t) -> p c t', t=128), [PSK[p2]], ['KT%d' % blk])
                    tr(psb(p3)[0:64, 0:128], krb[b][:], identb[:], ['krb%d' % b, 'identb'], [PSK[p3]])
                    cp('vector', KRT[:, blk * 128:(blk + 1) * 128], psb(p3)[0:64, 0:128], [PSK[p3]], ['KRT%d' % blk])
                S.barrier()
            if STAGE < 3:
                return
            with ExitStack() as pc:
                wq = sbuf(pc, 'wq', [128, 16, 512], BF16)
                gq_b = sbuf(pc, 'gq_b', [128, 512], F32)
                xT = [sbuf(pc, 'xTc', [128, 16, 128], BF16)]
                dma('sync', 'gq_b', gq_b[:], g_q.partition_broadcast(128), [], ['gq_b'])
                wuq = sbuf(pc, 'wuq', [128, 4, 1536], BF16)
                wuqr = sbuf(pc, 'wuqr', [128, 4, 8, 64], BF16)
                wukT = sbuf(pc, 'wukT', [128, 8, 512], BF16)
                wuv = sbuf(pc, 'wuv', [128, 4, 1024], BF16)
                cqn_b = sbuf(pc, 'cqn_b', [128, 512], BF16)
                cqnT = sbuf(pc, 'cqnT', [128, 4, 128], BF16)
                qnT = sbuf(pc, 'qnT', [128, 8, 128], BF16)
                QRT = sbuf(pc, 'QRT', [64, 8, 128], BF16)
                QLT = sbuf(pc, 'QLT', [128, 4, 8, 128], BF16)
                OLT = sbuf(pc, 'OLT', [128, 4, 8, 128], BF16)
                cca = sbuf(pc, 'cca', [128, 8, 128], BF16)
                rq = sbuf(pc, 'rq', [64, 256], F32)
                tq = sbuf(pc, 'tq', [64, 256], F32)
                Ssb = [sbuf(pc, 'Ssb%d' % i, [128, 512], F32) for i in range(2)]
                Pb = [sbuf(pc, 'Pb%d' % i, [128, 512], BF16) for i in range(2)]
                PTs = [sbuf(pc, 'PTs%d' % i, [128, 4, 128], BF16) for i in range(2)]
                acc2 = [sbuf(pc, 'acc%d' % i, [128, 512], F32) for i in range(2)]
                ob_ = sbuf(pc, 'ob_', [128, 512], BF16)
                stt2 = [sbuf(pc, 'stat%d' % i, [128, 16], F32) for i in range(2)]

                for k4 in range(4):
                    dma('gpsimd', 'wq', wq[:, 4 * k4:4 * k4 + 4, :], w_in_v[:, 4 * k4:4 * k4 + 4, 1024:1536], [], ['wq%d' % k4])
                WQK = ['wq%d' % k4 for k4 in range(4)]
                dma('gpsimd', 'wuq', wuq[:], w_uq.rearrange('(c p) n -> p c n', p=128), [], ['wuq'])
                dma('gpsimd', 'wuv', wuv[:], w_uv.rearrange('(c p) n -> p c n', p=128), [], ['wuv'])
                wv = wuq[:].rearrange('p c (h d) -> p c h d', d=192)
                ts('vector', wuqr[:, :, :, 0:32], wv[:, :, :, 160:192], -1.0, None, ALU.mult, None, ['wuq'], ['wuqr0'])
                cp('vector', wuqr[:, :, :, 32:64], wv[:, :, :, 128:160], ['wuq'], ['wuqr1'])
                with ExitStack() as pw:
                    wukn = sbuf(pw, 'wukn', [128, 4, 1024], BF16)
                    dma('gpsimd', 'wukn', wukn[:], w_uk.rearrange('(c p) n -> p c n', p=128), [], ['wukn'])
                    for h in range(8):
                        pb = 2 + h % 2
                        for c in range(4):
                            tr(psb(pb)[:, c * 128:(c + 1) * 128], wukn[:, c, h * 128:(h + 1) * 128], identb[:], ['wukn', 'identb'], [PSK[pb]])
                        cp('vector' if h % 2 == 0 else 'scalar', wukT[:, h, :], psb(pb)[:, 0:512], [PSK[pb]], ['wukT'])
                    S.barrier()

                MX, MNEW, NEGM, RSUM, CORR, LRUN, MRUN, LINV = range(8)

                def attn_tile(qr, lhs_list, rhs_list, nk, mask_ap, mask_key, first, pv_list, par, RK, PVK):
                    bs, bpt, bo = par, 2 + par, 4 + par
                    stt_ = stt2[par]
                    acc = acc2[par]
                    STK = ['stat%d' % par]
                    ACK = 'acc%d' % par
                    n = len(lhs_list)
                    for i in range(n):
                        mm(PS[bs][0:qr, 0:nk], lhs_list[i], rhs_list[i], i == 0, i == n - 1, RK, [PSK[bs]])
                    if mask_ap is not None:
                        tt('vector', Ssb[par][0:qr, 0:nk], PS[bs][0:qr, 0:nk], mask_ap, ALU.add, [PSK[bs], mask_key], ['Ssb%d' % par])
                        src, srck = Ssb[par][0:qr, 0:nk], 'Ssb%d' % par
                    else:
                        src, srck = PS[bs][0:qr, 0:nk], PSK[bs]
                    red(stt_[0:qr, MX:MX + 1], src, ALU.max, [srck], STK)
                    if first:
                        cp('vector', stt_[0:qr, MNEW:MNEW + 1], stt_[0:qr, MX:MX + 1], STK, STK)
                    else:
                        tt('vector', stt_[0:qr, MNEW:MNEW + 1], stt_[0:qr, MRUN:MRUN + 1], stt_[0:qr, MX:MX + 1], ALU.max, STK, STK)
                    ts('vector', stt_[0:qr, NEGM:NEGM + 1], stt_[0:qr, MNEW:MNEW + 1], -SCALE, None, ALU.mult, None, STK, STK)
                    act(Pb[par][0:qr, 0:nk], src, AF.Exp, [srck] + STK, ['Pb%d' % par] + STK, bias=stt_[0:qr, NEGM:NEGM + 1], scale=SCALE, accum=stt_[0:qr, RSUM:RSUM + 1])
                    if first:
                        cp('vector', stt_[0:qr, LRUN:LRUN + 1], stt_[0:qr, RSUM:RSUM + 1], STK, STK)
                    else:
                        act(stt_[0:qr, CORR:CORR + 1], stt_[0:qr, MRUN:MRUN + 1], AF.Exp, STK, STK, bias=stt_[0:qr, NEGM:NEGM + 1], scale=SCALE)
                        stt(stt_[0:qr, LRUN:LRUN + 1], stt_[0:qr, LRUN:LRUN + 1], stt_[0:qr, CORR:CORR + 1], stt_[0:qr, RSUM:RSUM + 1], ALU.mult, ALU.add, STK, STK)
                    cp('vector', stt_[0:qr, MRUN:MRUN + 1], stt_[0:qr, MNEW:MNEW + 1], STK, STK)
                    nsub = nk // 128
                    for i in range(nsub):
                        tr(psb(bpt)[:, i * 128:i * 128 + qr], Pb[par][0:qr, i * 128:(i + 1) * 128], identb[0:qr, 0:qr], ['Pb%d' % par, 'identb'], [PSK[bpt]])
                    cp('scalar', PTs[par][:, 0:nsub, 0:qr], psb(bpt)[:, 0:nsub * 128].rearrange('p (a b) -> p a b', b=128)[:, :, 0:qr], [PSK[bpt]], ['PTs%d' % par])
                    for i in range(nsub):
                        mm(PS[bo][0:qr, 0:512], PTs[par][:, i, 0:qr], pv_list[i], i == 0, i == nsub - 1, ['PTs%d' % par] + PVK, [PSK[bo]])
                    if first:
                        cp('vector', acc[0:qr, :], PS[bo][0:qr, 0:512], [PSK[bo]], [ACK])
                    else:
                        stt(acc[0:qr, :], acc[0:qr, :], stt_[0:qr, CORR:CORR + 1], PS[bo][0:qr, 0:512], ALU.mult, ALU.add, [ACK, PSK[bo]] + STK, [ACK])

                def attn_finish(qr, par):
                    stt_ = stt2[par]
                    acc = acc2[par]
                    STK = ['stat%d' % par]
                    recip(stt_[0:qr, LINV:LINV + 1], stt_[0:qr, LRUN:LRUN + 1], STK, STK)
                    ts('vector', ob_[0:qr, :], acc[0:qr, :], stt_[0:qr, LINV:LINV + 1], None, ALU.mult, None, ['acc%d' % par] + STK, ['ob_'])

                KALL = ['KT%d' % b for b in range(NBLK)] + ['KRT%d' % b for b in range(NBLK)]
                CALL = ['CKV%d' % b for b in range(NBLK)]
                tcount = 0
                NB = 2

                def do_block(ob):
                    nonlocal tcount
                    blk = OWN[ob]
                    smp = blk == 32
                    xb = 0
                    dma('sync', 'xTa%d' % xb, xT[xb][:], xT_d[blk].rearrange('p (k t) -> p k t', t=128), [], ['xTa%d' % xb])
                    dma('sync', 'rq', rq[:], ropeq[ob], [], ['rq'])
                    for k in range(16):
                        mm(PS[6][:, 0:512], xT[xb][:, k, :], wq[:, k, :], k == 0, k == 15, WQK + ['xTa%d' % xb], [PSK[6]])
                    rmsnorm_rows(PS[6][:, 0:512], PSK[6], gq_b[:], 'gq_b', cqn_b[:], 'cqn_b', Ssb[0], ssv, 'Ssb0')
                    for c in range(4):
                        tr(psb(7)[:, c * 128:(c + 1) * 128], cqn_b[:, c * 128:(c + 1) * 128], identb[:], ['cqn_b', 'identb'], [PSK[7]])
                    cp('vector', cqnT[:], psb(7)[:, 0:512].rearrange('p (c t) -> p c t', t=128), [PSK[7]], ['cqnT'])
                    for hh in range(2):
                        pb = 6 + hh
                        for i in range(4):
                            h = 4 * hh + i
                            for c in range(4):
                                mm(PS[pb][:, i * 128:(i + 1) * 128], wuq[:, c, h * 192:h * 192 + 128], cqnT[:, c, :], c == 0, c == 3, ['wuq', 'cqnT'], [PSK[pb]])
                        cp('scalar' if hh == 0 else 'vector', qnT[:, 4 * hh:4 * hh + 4, :], PS[pb][:].rearrange('p (a t) -> p a t', t=128), [PSK[pb]], ['qnT%d' % hh])
                    for h in range(8):
                        pb = 6 + h % 2
                        for c in range(4):
                            mm(PS[pb][0:64, 0:128], wuq[:, c, h * 192 + 128:h * 192 + 192], cqnT[:, c, :], c == 0, c == 3, ['wuq', 'cqnT'], [PSK[pb]])
                        for c in range(4):
                            mm(PS[pb][0:64, 128:256], wuqr[:, c, h, :], cqnT[:, c, :], c == 0, c == 3, ['wuqr0', 'wuqr1', 'cqnT'], [PSK[pb]])
                        tt('vector', tq[:], PS[pb][0:64, 0:256], rq[:], ALU.mult, [PSK[pb], 'rq'], ['tq'])
                        if smp:
                            qrs_ = QRT[:].rearrange('p h t -> p (h t)').rearrange('p (s x) -> p s x', x=64)
                            tt('gpsimd', qrs_[:, :, h * 8:(h + 1) * 8], tq[:, 0:128].rearrange('p (s t) -> p s t', t=8), tq[:, 128:256].rearrange('p (s t) -> p s t', t=8), ALU.add, ['tq'], ['QRT'])
                        else:
                            tt('gpsimd', QRT[:, h, :], tq[:, 0:128], tq[:, 128:256], ALU.add, ['tq'], ['QRT'])
                    for h in range(8):
                        pb = 6 + h % 2
                        for c in range(4):
                            mm(PS[pb][:, c * 128:(c + 1) * 128], wukT[:, h, c * 128:(c + 1) * 128], qnT[:, h, :], True, True, ['wukT', 'qnT0', 'qnT1'], [PSK[pb]])
                        if smp:
                            qls_ = QLT[:].rearrange('p c h t -> p c (h t)').rearrange('p c (s x) -> p c s x', x=64)
                            cp('scalar' if h % 2 == 0 else 'vector', qls_[:, :, :, h * 8:(h + 1) * 8], PS[pb][:].rearrange('p (c s t) -> p c s t', c=4, t=8), [PSK[pb]], ['QLT'])
                        else:
                            cp('scalar' if h % 2 == 0 else 'vector', QLT[:, :, h, :], PS[pb][:].rearrange('p (c t) -> p c t', t=128), [PSK[pb]], ['QLT'])
                    QK = ['QLT', 'QRT']
                    QLs = QLT[:].rearrange('p c h t -> p c (h t)').rearrange('p c (s x) -> p c s x', x=64)
                    QRs = QRT[:].rearrange('p h t -> p (h t)').rearrange('p (s x) -> p s x', x=64)
                    if not smp:
                        j = ob
                        for hp in range(4):
                          for kt in range(j + 1):
                            for par in range(2):
                                h = 2 * hp + par
                                lhs = [QLT[:, c, h, :] for c in range(4)] + [QRT[:, h, :]]
                                rhs = [KT[:, c, kt * 512:(kt + 1) * 512] for c in range(4)] + [KRT[:, kt * 512:(kt + 1) * 512]]
                                if kt == 0 and kt == j:
                                    mk, mkk = mboth[:], 'mboth'
                                elif kt == 0:
                                    mk, mkk = mpad[:], 'mpad'
                                elif kt == j:
                                    mk, mkk = mdiag[:], 'mdiag'
                                else:
                                    mk, mkk = None, None
                                pv = [CKV[:, 4 * kt + i, :] for i in range(4)]
                                attn_tile(128, lhs, rhs, 512, mk, mkk, kt == 0, pv, par, QK + KALL, CALL)
                                tcount += 1
                          for par in range(2):
                            h = 2 * hp + par
                            attn_finish(128, par)
                            for c in range(4):
                                tr(psb(6 + par)[:, c * 128:(c + 1) * 128], ob_[:, c * 128:(c + 1) * 128], identb[:], ['ob_', 'identb'], [PSK[6 + par]])
                            cp('scalar', OLT[:, :, h, :], psb(6 + par)[:, 0:512].rearrange('p (c t) -> p c t', t=128), [PSK[6 + par]], ['OLT'])
                    else:
                        for sp in range(8):
                          for kt in range(17):
                            for par in range(2):
                                s = 2 * sp + par
                                lhs = [QLs[:, c, s, :] for c in range(4)] + [QRs[:, s, :]]
                                if kt < 16:
                                    t = s * 16 + kt
                                    sl = kt % NB
                                    Kx, KRx = Kt[par][sl], KRt[par][sl]
                                    kk, krk = 'Kt%d_%d' % (par, sl), 'KRt%d_%d' % (par, sl)
                                    gather(kk, Kx[:].rearrange('p j f -> p (j f)'), cache_ckv, idx[:, t:t + 1], ['idx'], [kk])
                                    gather(krk, KRx[:].rearrange('p j f -> p (j f)'), cache_kr, idx[:, t:t + 1], ['idx'], [krk])
                                    for jj in range(4):
                                        for c in range(4):
                                            tr(psb(6 + c // 2)[:, (c % 2) * 512 + jj * 128:(c % 2) * 512 + (jj + 1) * 128], Kx[:, jj, c * 128:(c + 1) * 128], identb[:],
                                               [kk, 'identb'], [PSK[6 + c // 2]])
                                    cp('scalar', KTt[par][:, 0:2, :], psb(6)[:, :].rearrange('p (c k) -> p c k', k=512), [PSK[6]], ['KTt%d_0' % par])
                                    cp('vector', KTt[par][:, 2:4, :], psb(7)[:, :].rearrange('p (c k) -> p c k', k=512), [PSK[7]], ['KTt%d_1' % par])
                                    for jj in range(4):
                                        tr(psb(2 + par)[0:64, 512 + jj * 128:512 + (jj + 1) * 128], KRx[:, jj, :], identb[:], [krk, 'identb'], [PSK[2 + par]])
                                    cp('vector', KRTt[par][:, :], psb(2 + par)[0:64, 512:1024], [PSK[2 + par]], ['KRTt%d' % par])
                                    rhs = [KTt[par][:, c, :] for c in range(4)] + [KRTt[par][:, :]]
                                    pv = [Kx[:, jj, :] for jj in range(4)]
                                    attn_tile(64, lhs, rhs, 512, None, None, kt == 0, pv, par, QK + ['KTt%d_0' % par, 'KTt%d_1' % par, 'KRTt%d' % par], [kk])
                                else:
                                    rhs = [KT[:, c, 32 * 128:33 * 128] for c in range(4)] + [KRT[:, 32 * 128:33 * 128]]
                                    pv = [CKV[:, 32, :]]
                                    attn_tile(64, lhs, rhs, 128, msmp[:, s * 128:(s + 1) * 128], 'msmp', False, pv, par, QK + KALL, CALL)
                                tcount += 1
                          for par in range(2):
                            s = 2 * sp + par
                            attn_finish(64, par)
                            for c in range(4):
                                tr(psb(6 + par)[:, c * 64:(c + 1) * 64], ob_[0:64, c * 128:(c + 1) * 128], identb[0:64, 0:64], ['ob_', 'identb'], [PSK[6 + par]])
                            cp('scalar', OLT[:, :, :, s * 8:(s + 1) * 8], psb(6 + par)[:, 0:256].rearrange('p (c h t) -> p c h t', c=4, t=8), [PSK[6 + par]], ['OLT'])
                    for hh in range(2):
                        pb = 6 + hh
                        for i in range(4):
                            h = 4 * hh + i
                            for c in range(4):
                                mm(PS[pb][:, i * 128:(i + 1) * 128], wuv[:, c, h * 128:(h + 1) * 128], OLT[:, c, h, :], c == 0, c == 3, ['wuv', 'OLT'], [PSK[pb]])
                        cp('scalar' if hh == 0 else 'vector', cca[:, 4 * hh:4 * hh + 4, :], PS[pb][:].rearrange('p (a t) -> p a t', t=128), [PSK[pb]], ['cca%d' % hh])
                    dma('sync', 'cca', cc_d[ob].rearrange('p (k t) -> p k t', t=128)[:, 8:16, :], cca[:], ['cca0', 'cca1'], ['cc_d'])

                with ExitStack() as pp:
                    mpad = sbuf(pp, 'mpad', [128, 512], F32)
                    mdiag = sbuf(pp, 'mdiag', [128, 512], F32)
                    mboth = sbuf(pp, 'mboth', [128, 512], F32)
                    dma('sync', 'mpad', mpad[:], mask_pad, [], ['mpad'])
                    dma('sync', 'mdiag', mdiag[:], mask_diag, [], ['mdiag'])
                    tt('vector', mboth[:], mpad[:], mdiag[:], ALU.add, ['mpad', 'mdiag'], ['mboth'])
                    for ob in range(NOWN_RUN if NOWN_RUN < 8 else 8):
                        do_block(ob)
                    S.barrier()
                with ExitStack() as psm:
                    msmp = sbuf(psm, 'msmp', [64, 2048], F32)
                    Kt = [[sbuf(psm, 'Kt%d_%d' % (q_, i), [128, 4, 512], BF16) for i in range(NB)] for q_ in range(2)]
                    KRt = [[sbuf(psm, 'KRt%d_%d' % (q_, i), [128, 4, 64], BF16) for i in range(NB)] for q_ in range(2)]
                    KTt = [sbuf(psm, 'KTt%d' % i, [128, 4, 512], BF16) for i in range(2)]
                    KRTt = [sbuf(psm, 'KRTt%d' % i, [64, 512], BF16) for i in range(2)]
                    pti = sbuf(psm, 'pti', [128, 256], I32)
                    ptf = sbuf(psm, 'ptf', [128, 256], F32)
                    qof = sbuf(psm, 'qof', [128, 1], F32)
                    idx = sbuf(psm, 'idx', [128, 256], I32)
                    dma('sync', 'msmp', msmp[:], mask_smp, [], ['msmp'])
                    dma('sync', 'pti', pti[:], pt_exp, [], ['pti'])
                    dma('sync', 'qof', qof[:], qoff, [], ['qof'])
                    cp('vector', ptf[:], pti[:], ['pti'], ['ptf'])
                    ts('vector', ptf[:], ptf[:], 32.0, qof[:, 0:1], ALU.mult, ALU.add, ['ptf', 'qof'], ['ptf'])
                    cp('vector', idx[:], ptf[:], ['ptf'], ['idx'])
                    if NOWN_RUN >= 9:
                        do_block(8)
                    S.barrier()

        def layer_norm_rows(src, src_key, gb, bb, out, out_key, stats, mv, scope_keys):
            for i in range(4):
                S.op('vector', lambda e, i=i: e.bn_stats(out=stats[:, i, :], in_=src[:, i * 512:(i + 1) * 512]), [src_key], ['lnst'])
            S.op('vector', lambda e: e.bn_aggr(out=mv[:, 0:2], in_=stats[:].rearrange('p a b -> p (a b)')), ['lnst'], ['lnmv'])
            act(mv[:, 2:3], mv[:, 1:2], AF.Sqrt, ['lnmv'], ['lnmv2'], bias=epsq[:, 1:2], scale=1.0)
            recip(mv[:, 3:4], mv[:, 2:3], ['lnmv2'], ['lnmv3'])
            ts('vector', out, src[:], mv[:, 0:1], mv[:, 3:4], ALU.subtract, ALU.mult, [src_key, 'lnmv', 'lnmv3'], [out_key])
            tt('gpsimd', out, out, gb, ALU.mult, [out_key, 'lng'], [out_key])
            tt('gpsimd', out, out, bb, ALU.add, [out_key, 'lnb'], [out_key])

        def phase_out():
          with ExitStack() as ph:
            x1T = sbuf(ph, 'x1T', [128, 16, TOK], BF16)
            stats = sbuf(ph, 'lnstats', [128, 4, 6], F32)
            mv = sbuf(ph, 'lnmv', [128, 4], F32)
            with ExitStack() as p1:
                wo = sbuf(p1, 'wo', [128, 16, 2048], BF16)
                g1 = sbuf(p1, 'g1', [128, 2048], F32)
                b1 = sbuf(p1, 'b1', [128, 2048], F32)
                xa = sbuf(p1, 'xa', [128, 2048], F32)
                pre = sbuf(p1, 'pre', [128, 2048], F32)
                x1f = sbuf(p1, 'x1f', [128, 2048], F32)
                x1b = sbuf(p1, 'x1b', [128, 2048], BF16)
                w_out_v = w_out.rearrange('(k p) n -> p k n', p=128)
                for k in range(16):
                    dma('gpsimd', 'wo', wo[:, k, :], w_out_v[:, k, :], [], ['wo%d' % k])
                WOK = ['wo%d' % k for k in range(16)]
                dma('sync', 'g1', g1[:], ln1_g.partition_broadcast(128), [], ['lng'])
                dma('sync', 'b1', b1[:], ln1_b.partition_broadcast(128), [], ['lnb'])
                concatT = sbuf(p1, 'concatT', [128, 16, TOK], BF16)
                for ob in range(NOWN):
                    dma('sync', 'cc', concatT[:, :, ob * 128:(ob + 1) * 128], cc_d[ob].rearrange('p (k t) -> p k t', t=128), [], ['cc%d' % ob])
                CCK = ['cc%d' % ob for ob in range(NOWN)]
                for ob in range(NOWN):
                    blk = OWN[ob]
                    dma('sync', 'xa', xa[:], xs[blk], [], ['xa'])
                    for n in range(4):
                        for k in range(16):
                            mm(PS[n][:, 0:512], concatT[:, k, ob * 128:(ob + 1) * 128], wo[:, k, n * 512:(n + 1) * 512], k == 0, k == 15, WOK + CCK, [PSK[n]])
                        stt(pre[:, n * 512:(n + 1) * 512], xa[:, n * 512:(n + 1) * 512], ALPHA, PS[n][:, 0:512], ALU.mult, ALU.add, ['xa', PSK[n]], ['pre'])
                    layer_norm_rows(pre, 'pre', g1[:], b1[:], x1f[:], 'x1f', stats, mv, None)
                    dma('sync', 'x1_d', x1_d[ob], x1f[:], ['x1f'], ['x1_d%d' % ob])
                    cp('scalar', x1b[:], x1f[:], ['x1f'], ['x1b'])
                    for g in range(2):
                        pb = 4 + g
                        for i in range(8):
                            k = 8 * g + i
                            tr(psb(pb)[:, i * 128:(i + 1) * 128], x1b[:, k * 128:(k + 1) * 128], identb[:], ['x1b', 'identb'], [PSK[pb]])
                        cp('vector', x1T[:, 8 * g:8 * g + 8, ob * 128:(ob + 1) * 128], psb(pb)[:, :].rearrange('p (a t) -> p a t', t=128), [PSK[pb]], ['x1T'])
                S.barrier()
            HT = sbuf(ph, 'HT', [128, NFF, TOK], BF16)
            NT = [(0, 512), (512, 512), (1024, 128)]
            with ExitStack() as p2:
                wg = [sbuf(p2, 'wg%d' % i, [128, 16, 256], BF16) for i in range(2)]
                wu = [sbuf(p2, 'wu%d' % i, [128, 16, 256], BF16) for i in range(2)]
                sgl = [sbuf(p2, 'sgl%d' % i, [128, 512], F32) for i in range(2)]
                w_gate_v = w_gate.rearrange('(k p) n -> p k n', p=128)
                w_up_v = w_up.rearrange('(k p) n -> p k n', p=128)
                cnt = 0
                for fb in range(NFF // 2):
                    b = fb % 2
                    for k2 in range(2):
                        dma('gpsimd', 'wg%d' % b, wg[b][:, 8 * k2:8 * k2 + 8, :], w_gate_v[:, 8 * k2:8 * k2 + 8, fb * 256:(fb + 1) * 256], [], ['wg%d_%d' % (b, k2)])
                        dma('gpsimd', 'wu%d' % b, wu[b][:, 8 * k2:8 * k2 + 8, :], w_up_v[:, 8 * k2:8 * k2 + 8, fb * 256:(fb + 1) * 256], [], ['wu%d_%d' % (b, k2)])
                    WGK = ['wg%d_0' % b, 'wg%d_1' % b]
                    WUK = ['wu%d_0' % b, 'wu%d_1' % b]
                    for half in range(2):
                        f = 2 * fb + half
                        for (t0, tn) in NT:
                            p = cnt % 2
                            cnt += 1
                            pg, pu = p, 2 + p
                            for k in range(16):
                                mm(PS[pg][:, 0:tn], wg[b][:, k, half * 128:(half + 1) * 128], x1T[:, k, t0:t0 + tn], k == 0, k == 15, WGK + ['x1T'], [PSK[pg]])
                            for k in range(16):
                                mm(PS[pu][:, 0:tn], wu[b][:, k, half * 128:(half + 1) * 128], x1T[:, k, t0:t0 + tn], k == 0, k == 15, WUK + ['x1T'], [PSK[pu]])
                            act(sgl[p][:, 0:tn], PS[pg][:, 0:tn], AF.Silu, [PSK[pg]], ['sgl%d' % p])
                            tt('vector', HT[:, f, t0:t0 + tn], sgl[p][:, 0:tn], PS[pu][:, 0:tn], ALU.mult, ['sgl%d' % p, PSK[pu]], ['HT'])
                S.barrier()
            with ExitStack() as p3:
                wd = [sbuf(p3, 'wd%d' % i, [128, NFF, 256], BF16) for i in range(2)]
                yTf = [sbuf(p3, 'yTf%d' % i, [128, TOK], F32) for i in range(2)]
                ystrip = [sbuf(p3, 'ystrip%d' % i, [128, NOWN, 128], F32) for i in range(2)]
                w_down_v = w_down.rearrange('(f p) n -> p f n', p=128)
                cnt = 0
                for ocb in range(8):
                    b = ocb % 2
                    for f4 in range(4):
                        dma('gpsimd', 'wd%d' % b, wd[b][:, 11 * f4:11 * f4 + 11, :], w_down_v[:, 11 * f4:11 * f4 + 11, ocb * 256:(ocb + 1) * 256], [], ['wd%d_%d' % (b, f4)])
                    WDK = ['wd%d_%d' % (b, f4) for f4 in range(4)]
                    for half in range(2):
                        oc = 2 * ocb + half
                        yb = oc % 2
                        for ti, (t0, tn) in enumerate(NT):
                            pd = (cnt % 2) * 3 + ti
                            for f in range(NFF):
                                mm(PS[pd][:, 0:tn], wd[b][:, f, half * 128:(half + 1) * 128], HT[:, f, t0:t0 + tn], f == 0, f == NFF - 1, WDK + ['HT'], [PSK[pd]])
                            cp('scalar' if ti % 2 == 0 else 'vector', yTf[yb][:, t0:t0 + tn], PS[pd][:, 0:tn], [PSK[pd]], ['yTf%d_%d' % (yb, ti)])
                        cnt += 1
                        YK = ['yTf%d_%d' % (yb, ti) for ti in range(3)]
                        for g in range(3):
                            pb = 6 + g % 2
                            nb_ = 4 if g < 2 else 1
                            for i in range(nb_):
                                ob = 4 * g + i
                                tr(PS[pb][:, i * 128:(i + 1) * 128], yTf[yb][:, ob * 128:(ob + 1) * 128], identf[:], YK + ['identf'], [PSK[pb]])
                            cp('vector' if g % 2 == 0 else 'scalar', ystrip[yb][:, 4 * g:4 * g + nb_, :], PS[pb][:, 0:nb_ * 128].rearrange('p (a t) -> p a t', t=128), [PSK[pb]], ['ystrip%d_%d' % (yb, g)])
                        dma('sync', 'y2s%d' % yb, y2_d[:, :, oc * 128:(oc + 1) * 128].rearrange('b p f -> p b f'), ystrip[yb][:],
                            ['ystrip%d_%d' % (yb, g) for g in range(3)], ['y2_d'])
                S.barrier()
            with ExitStack() as p4:
                g2 = sbuf(p4, 'g2', [128, 2048], F32)
                b2 = sbuf(p4, 'b2', [128, 2048], F32)
                x1r = [sbuf(p4, 'x1r%d' % i, [128, 2048], F32) for i in range(2)]
                y2r = [sbuf(p4, 'y2r%d' % i, [128, 2048], F32) for i in range(2)]
                outt = [sbuf(p4, 'outt%d' % i, [128, 2048], F32) for i in range(2)]
                dma('sync', 'g2', g2[:], ln2_g.partition_broadcast(128), [], ['lng'])
                dma('sync', 'b2', b2[:], ln2_b.partition_broadcast(128), [], ['lnb'])
                for ob in range(NOWN):
                    b = ob % 2
                    dma('sync', 'x1r%d' % b, x1r[b][:], x1_d[ob], [], ['x1r%d' % b])
                    dma('sync', 'y2r%d' % b, y2r[b][:], y2_d[ob], [], ['y2r%d' % b])
                    stt(y2r[b][:], x1r[b][:], ALPHA, y2r[b][:], ALU.mult, ALU.add, ['x1r%d' % b, 'y2r%d' % b], ['y2r%d' % b])
                    layer_norm_rows(y2r[b], 'y2r%d' % b, g2[:], b2[:], outt[b][:], 'outt%d' % b, stats, mv, None)
                    dma('sync', 'yo%d' % b, y_o[ob], outt[b][:], ['outt%d' % b], ['y_o'])
                S.barrier()

        if STAGE >= 1:
            phase_s5()
        if STAGE >= 2:
            phase_attn()
        if STAGE >= 4:
            phase_out()
        S.barrier()
        S.emit()
    return nc


_NC = None


def _rope_tables():
    half = 32
    inv = (10000.0 ** (-2.0 * np.arange(half, dtype=np.float32) / 64.0)).astype(np.float32)
    return inv


def _host_inputs(inputs):
    f32 = np.float32
    x_prompt = np.asarray(inputs['x_prompt'], f32)
    x_sample = np.asarray(inputs['x_sample'], f32)
    page_table = np.asarray(inputs['page_table']).astype(np.int32)
    inv = _rope_tables()
    shared = {
        'w_in': np.asarray(inputs['w_in'], f32)[0],
        'g_q': np.asarray(inputs['g_q'], f32).reshape(1, 512),
        'w_uq': np.asarray(inputs['w_uq'], f32)[0],
        'w_uk': np.asarray(inputs['w_uk'], f32)[0].reshape(512, 1024),
        'g_kv': np.asarray(inputs['g_kv'], f32).reshape(1, 512),
        'w_uv': np.asarray(inputs['w_uv'], f32)[0].reshape(512, 1024),
        'ssm_a_re': np.asarray(inputs['ssm_a_re'], f32)[0],
        'ssm_a_im': np.asarray(inputs['ssm_a_im'], f32)[0],
        'ssm_log_step': np.asarray(inputs['ssm_log_step'], f32).reshape(1, 64),
        'ssm_b_re': np.asarray(inputs['ssm_b_re'], f32)[0],
        'ssm_b_im': np.asarray(inputs['ssm_b_im'], f32)[0],
        'ssm_c_re': np.asarray(inputs['ssm_c_re'], f32)[0],
        'ssm_c_im': np.asarray(inputs['ssm_c_im'], f32)[0],
        'ssm_d': np.asarray(inputs['ssm_d'], f32).reshape(1, 1024),
        'w_glu': np.asarray(inputs['w_glu'], f32)[0],
        'b_glu': np.asarray(inputs['b_glu'], f32).reshape(1, 1024),
        'w_out': np.asarray(inputs['w_out'], f32)[0],
        'ln1_g': np.asarray(inputs['ln1_g'], f32).reshape(1, 2048),
        'ln1_b': np.asarray(inputs['ln1_b'], f32).reshape(1, 2048),
        'w_gate': np.asarray(inputs['w_gate'], f32)[0],
        'w_up': np.asarray(inputs['w_up'], f32)[0],
        'w_down': np.asarray(inputs['w_down'], f32)[0],
        'ln2_g': np.asarray(inputs['ln2_g'], f32).reshape(1, 2048),
        'ln2_b': np.asarray(inputs['ln2_b'], f32).reshape(1, 2048),
        'cache_ckv': np.asarray(inputs['cache_ckv'], f32).reshape(-1, 2048),
        'cache_kr': np.asarray(inputs['cache_krope'], f32).reshape(-1, 256),
        'identf': np.eye(128, dtype=f32),
        'qoff': (np.arange(128) % 32).astype(f32).reshape(128, 1),
    }
    NEG = -30000.0
    md = np.zeros((128, 512), f32)
    md[:, 384:] = np.where(np.arange(128)[None, :] <= np.arange(128)[:, None], 0.0, NEG)
    shared['mask_diag'] = md
    ms = np.full((64, 16, 128), NEG, f32)
    tt_ = np.arange(64) % 8
    for s in range(16):
        for tp in range(8):
            ms[tt_ >= tp, s, s * 8 + tp] = 0.0
    shared['mask_smp'] = ms.reshape(64, 2048)
    st_re = np.asarray(inputs['state_ssm_re'], f32)[0].reshape(128, 4096)
    st_im = np.asarray(inputs['state_ssm_im'], f32)[0].reshape(128, 4096)
    in_maps = []
    for c in range(8):
        b, r = c // 4, c % 4
        xs = np.zeros((NBLK, 128, 2048), f32)
        pos = np.zeros((NBLK, 128), f32)
        mp = np.zeros((128, 512), f32)
        for i in range(32):
            a = i + r - 3
            if a >= 0:
                xs[i] = x_prompt[b, a * 128:(a + 1) * 128]
                pos[i] = a * 128 + np.arange(128)
            elif i < 4:
                mp[:, i * 128:(i + 1) * 128] = NEG
        xs[32] = x_sample[16 * c:16 * c + 16].reshape(128, 2048)
        pos[32] = 8192 + (np.arange(128) % 8)
        ang = pos[:, :, None] * inv[None, None, :]
        cs, sn = np.cos(ang).astype(f32), np.sin(ang).astype(f32)
        ropek = np.concatenate([cs, cs, sn, sn], axis=2).astype(f32)
        rq = np.zeros((NOWN, 64, 256), f32)
        for ob, blk in enumerate(OWN):
            rq[ob, :, 0:128] = np.concatenate([cs[blk], cs[blk]], axis=1).T
            rq[ob, :, 128:256] = np.concatenate([sn[blk], sn[blk]], axis=1).T
        pt = page_table[16 * c:16 * c + 16]
        ptx = pt.reshape(16, 16, 4)[:, :, np.arange(128) // 32]
        ptx = np.ascontiguousarray(ptx.transpose(2, 0, 1).reshape(128, 256)).astype(np.int32)
        m = dict(shared)
        m.update({'xs': xs, 'ropek': ropek, 'ropeq': rq, 'mask_pad': mp, 'pt_exp': ptx,
                  'st_re': np.ascontiguousarray(st_re[16 * c:16 * c + 16]),
                  'st_im': np.ascontiguousarray(st_im[16 * c:16 * c + 16])})
        in_maps.append(m)
    return in_maps


def kernel(**inputs):
    global _NC
    if _NC is None:
        _NC = build()
    in_maps = _host_inputs(inputs)
    if os.environ.get('MK_TRACE'):
        res = run_bass_kernel_spmd(_NC, in_maps[:NCORES], core_ids=list(range(NCORES)), trace=True)
        print('EXEC_TIME_NS', res.exec_time_ns)
    else:
        res = run_bass_kernel_spmd(_NC, in_maps[:NCORES], core_ids=list(range(NCORES)))
    R = list(res.results) + [res.results[0]] * (8 - NCORES)
    f32 = np.float32
    y_p = np.zeros((2, 4096, 2048), f32)
    y_s = np.zeros((128, 8, 2048), f32)
    ckv_p = np.zeros((1, 2, 4096, 512), f32)
    kr_p = np.zeros((1, 2, 4096, 64), f32)
    re_p = np.zeros((1, 2, 64, 64), f32)
    im_p = np.zeros((1, 2, 64, 64), f32)
    ckv_s = np.zeros((1, 128, 8, 512), f32)
    kr_s = np.zeros((1, 128, 8, 64), f32)
    re_s = np.zeros((1, 128, 64, 64), f32)
    im_s = np.zeros((1, 128, 64, 64), f32)
    for c in range(8):
        b, r = c // 4, c % 4
        o = R[c]
        for j in range(8):
            a = 4 * j + r
            y_p[b, a * 128:(a + 1) * 128] = o['y_o'][j]
        y_s[16 * c:16 * c + 16] = o['y_o'][8].reshape(16, 8, 2048)
        ckv_s[0, 16 * c:16 * c + 16] = o['ckv_o'][32].reshape(16, 8, 512)
        kr_s[0, 16 * c:16 * c + 16] = o['kr_o'][32].reshape(16, 8, 64)
        re_s[0, 16 * c:16 * c + 16] = o['sts_o'][0].reshape(16, 64, 64)
        im_s[0, 16 * c:16 * c + 16] = o['sts_o'][1].reshape(16, 64, 64)
        if r == 3:
            ckv_p[0, b] = o['ckv_o'][0:32].reshape(4096, 512)
            kr_p[0, b] = o['kr_o'][0:32].reshape(4096, 64)
            re_p[0, b] = o['stp_o'][0].reshape(64, 64)
            im_p[0, b] = o['stp_o'][1].reshape(64, 64)
    return (y_p, y_s, ckv_p, kr_p, re_p, im_p, ckv_s, kr_s, re_s, im_s)
```

```python
import os
import numpy as np
import concourse.bass as bass
import concourse.mybir as mybir
from concourse.bass_utils import run_bass_kernel_spmd
from contextlib import ExitStack

F32 = mybir.dt.float32
BF16 = mybir.dt.bfloat16
I32 = mybir.dt.int32
AF = mybir.ActivationFunctionType
ALU = mybir.AluOpType
AX = mybir.AxisListType

NBLK = 33
OWN = [4 * j + 3 for j in range(8)] + [32]
NOWN = 9
TOK = NOWN * 128
SCALE = (128 + 64) ** -0.5
ALPHA = 2.0 ** 0.25
DFF = 5632
NFF = DFF // 128
STAGE = int(os.environ.get('MK_STAGE', '99'))
NPOOL = int(os.environ.get('MK_NPOOL', '10240'))
NOWN_RUN = int(os.environ.get('MK_NOWN', '9'))
NCORES = int(os.environ.get('MK_CORES', '8'))
SUB = int(os.environ.get('MK_SUB', '99'))
SUB2 = int(os.environ.get('MK_SUB2', '99'))
NBLK_RUN = int(os.environ.get('MK_NBLK', '33'))


class Sched:
    LAT = float(os.environ.get('MK_LAT', '250'))

    def __init__(self, nc, es):
        self.nc, self.es = nc, es
        self.engs = ['sync', 'scalar', 'vector', 'gpsimd', 'tensor']
        self.q = {e: [] for e in self.engs}
        self.esem = {e: es.enter_context(nc.semaphore('s_' + e)) for e in self.engs[1:]}
        self.ecnt = {e: 0 for e in self.engs}
        self.seen = {e: {} for e in self.engs}
        self.dsem = {}
        self.free = []
        self.nsem = 0
        self.nodes = []
        self.buf = {}
        self.last_on_sem = {}

    def _add(self, node, reads, writes):
        nid = len(self.nodes)
        deps = {}
        for key in reads:
            b = self.buf.get(key)
            if b and b[0] is not None:
                deps[b[0]] = True
        for key in writes:
            b = self.buf.get(key)
            if b:
                if b[0] is not None:
                    deps[b[0]] = True
                for r in b[1]:
                    if r not in deps:
                        deps[r] = False
        if node['kind'] == 'dma':
            prev = self.last_on_sem.get(node['sem'])
            if prev is not None and prev not in deps:
                deps[prev] = None
            self.last_on_sem[node['sem']] = nid
        deps.pop(nid, None)
        node['deps'] = deps
        self.nodes.append(node)
        for key in reads:
            self.buf.setdefault(key, [None, []])[1].append(nid)
        for key in writes:
            self.buf[key] = [nid, []]

    def op(self, eng, fn, reads=(), writes=(), dur=100.0):
        self._add({'kind': 'op', 'eng': eng, 'fn': fn, 'dur': dur}, reads, writes)

    def dma(self, eng, semname, fn, reads=(), writes=(), dur=2500.0):
        self._add({'kind': 'dma', 'eng': eng, 'fn': fn, 'sem': semname, 'dur': dur}, reads, writes)

    def _schedule(self):
        import heapq
        nodes = self.nodes
        n = len(nodes)
        if n == 0:
            return
        succ = [[] for _ in range(n)]
        ndep = [0] * n
        for i, nd in enumerate(nodes):
            ndep[i] = len(nd['deps'])
            for d in nd['deps']:
                succ[d].append(i)
        heaps = {e: [] for e in self.engs}
        ready = [0.0] * n
        fin = [0.0] * n
        start = [0.0] * n
        free_at = {e: 0.0 for e in self.engs}
        for i in range(n):
            if ndep[i] == 0:
                heapq.heappush(heaps[nodes[i]['eng']], (0.0, i))
        order = {e: [] for e in self.engs}
        done = 0
        while done < n:
            best = None
            for e in self.engs:
                if heaps[e]:
                    r, i = heaps[e][0]
                    st = max(r, free_at[e])
                    if best is None or (st, i) < (best[0], best[2]):
                        best = (st, e, i)
            st, e, i = best
            heapq.heappop(heaps[e])
            nd = nodes[i]
            start[i] = st
            if nd['kind'] == 'dma':
                issue = 900.0 if e == 'gpsimd' else 60.0
                free_at[e] = st + issue
                fin[i] = st + issue + nd['dur']
            else:
                free_at[e] = st + nd['dur']
                fin[i] = st + nd['dur']
            order[e].append(i)
            done += 1
            for j in succ[i]:
                nj = nodes[j]
                t = fin[i] + self.LAT
                if nj['deps'][i] is None:
                    t = start[i] + 1.0
                elif nd['kind'] == 'op' and nd['eng'] == nj['eng'] and nj['kind'] == 'op':
                    t = start[i] + 1.0 if (e == 'tensor' or not nj['deps'][i]) else fin[i]
                if t > ready[j]:
                    ready[j] = t
                ndep[j] -= 1
                if ndep[j] == 0:
                    heapq.heappush(heaps[nj['eng']], (ready[j], j))
        ev = [None] * n
        for e in self.engs:
            for i in order[e]:
                nd = nodes[i]
                if nd['kind'] == 'dma':
                    nm = nd['sem']
                    if nm not in self.dsem:
                        if self.free:
                            self.dsem[nm] = self.free.pop()
                        else:
                            self.nsem += 1
                            self.dsem[nm] = [self.es.enter_context(self.nc.semaphore('d%d' % self.nsem)), 0]
                    d = self.dsem[nm]
                    d[1] += 16
                    ev[i] = (d[0], d[1], 16)
                else:
                    self.ecnt[e] += 1
                    ev[i] = (self.esem[e], self.ecnt[e], 1)
        for e in self.engs:
            for i in order[e]:
                nd = nodes[i]
                waits = {}
                for d, sync in nd['deps'].items():
                    pd = nodes[d]
                    if sync is None:
                        continue
                    if pd['kind'] == 'op' and pd['eng'] == e:
                        if nd['kind'] == 'op' and (e == 'tensor' or not sync):
                            continue
                    sem, val, _ = ev[d]
                    k = id(sem)
                    if self.seen[e].get(k, 0) >= val:
                        continue
                    if k not in waits or waits[k][1] < val:
                        waits[k] = (sem, val)
                for k, (sem, val) in waits.items():
                    self.seen[e][k] = val
                self.q[e].append((list(waits.values()), nd['fn'], (ev[i][0], ev[i][2])))
        self.nodes = []
        self.buf = {}
        self.last_on_sem = {}

    def barrier(self):
        self._schedule()
        for eng in self.engs:
            waits = []
            for o in self.engs[1:]:
                if o != eng and self.ecnt[o] > 0:
                    sem, val = self.esem[o], self.ecnt[o]
                    if self.seen[eng].get(id(sem), 0) < val:
                        waits.append((sem, val))
                        self.seen[eng][id(sem)] = val
            for d in self.dsem.values():
                if d[1] > 0 and self.seen[eng].get(id(d[0]), 0) < d[1]:
                    waits.append((d[0], d[1]))
                    self.seen[eng][id(d[0])] = d[1]
            self.q[eng].append((waits, None, None))
        self.free.extend(self.dsem.values())
        self.dsem = {}

    def emit(self):
        nc = self.nc
        with nc.Block() as block:
            def mk(ename):
                def body(e):
                    for waits, fn, inc in self.q[ename]:
                        for sem, val in waits:
                            e.wait_ge(sem, val)
                        if fn is not None:
                            ins = fn(e)
                            ins.then_inc(inc[0], inc[1])
                return body
            block.sync(mk('sync'))
            block.scalar(mk('scalar'))
            block.vector(mk('vector'))
            block.gpsimd(mk('gpsimd'))
            block.tensor(mk('tensor'))


def build():
    nc = bass.Bass('TRN2', target_bir_lowering=False)

    def din(name, shape, dt=F32):
        return nc.dram_tensor(name, shape, dt, kind='ExternalInput').ap()

    def dout(name, shape, dt=F32):
        return nc.dram_tensor(name, shape, dt, kind='ExternalOutput').ap()

    def dscr(name, shape, dt):
        return nc.dram_tensor(name, shape, dt, kind='Internal').ap()

    xs = din('xs', [NBLK, 128, 2048])
    w_in = din('w_in', [2048, 2112])
    g_q = din('g_q', [1, 512])
    w_uq = din('w_uq', [512, 1536])
    w_uk = din('w_uk', [512, 1024])
    g_kv = din('g_kv', [1, 512])
    w_uv = din('w_uv', [512, 1024])
    a_re_d = din('ssm_a_re', [64, 64])
    a_im_d = din('ssm_a_im', [64, 64])
    lstep_d = din('ssm_log_step', [1, 64])
    b_re_d = din('ssm_b_re', [64, 64, 16])
    b_im_d = din('ssm_b_im', [64, 64, 16])
    c_re_d = din('ssm_c_re', [64, 16, 64])
    c_im_d = din('ssm_c_im', [64, 16, 64])
    d_d = din('ssm_d', [1, 1024])
    w_glu = din('w_glu', [1024, 1024])
    b_glu = din('b_glu', [1, 1024])
    w_out = din('w_out', [2048, 2048])
    ln1_g = din('ln1_g', [1, 2048])
    ln1_b = din('ln1_b', [1, 2048])
    w_gate = din('w_gate', [2048, DFF])
    w_up = din('w_up', [2048, DFF])
    w_down = din('w_down', [DFF, 2048])
    ln2_g = din('ln2_g', [1, 2048])
    ln2_b = din('ln2_b', [1, 2048])
    cache_ckv = din('cache_ckv', [NPOOL * 32, 2048])
    cache_kr = din('cache_kr', [NPOOL * 32, 256])
    st_re = din('st_re', [16, 4096])
    st_im = din('st_im', [16, 4096])
    pt_exp = din('pt_exp', [128, 256], I32)
    qoff = din('qoff', [128, 1])
    ropek = din('ropek', [NBLK, 128, 128])
    ropeq = din('ropeq', [NOWN, 64, 256])
    mask_pad = din('mask_pad', [128, 512])
    mask_diag = din('mask_diag', [128, 512])
    mask_smp = din('mask_smp', [64, 2048])
    identf_d = din('identf', [128, 128])

    y_o = dout('y_o', [NOWN, 128, 2048])
    ckv_o = dout('ckv_o', [NBLK, 128, 512])
    kr_o = dout('kr_o', [NBLK, 128, 64])
    stp_o = dout('stp_o', [2, 32, 128])
    sts_o = dout('sts_o', [2, 16, 4096])

    xT_d = dscr('xT_d', [NBLK, 128, 2048], BF16)
    x1_d = dscr('x1_d', [NOWN, 128, 2048], F32)
    y2_d = dscr('y2_d', [NOWN, 128, 2048], F32)
    cc_d = dscr('cc_d', [NOWN, 128, 2048], BF16)

    es = ExitStack()
    with es:
        S = Sched(nc, es)

        _sbn = [0]

        def sbuf(scope, name, shape, dt):
            _sbn[0] += 1
            return scope.enter_context(nc.sbuf_tensor('sb%d_%s' % (_sbn[0], name), shape, dt))

        PF = float(os.environ.get('MK_PF', '3.5'))

        def fsz(ap):
            n = 1
            for d in ap.shape[1:]:
                n *= int(d)
            return n

        def mm(out, lhsT, rhs, start, stop, R, W):
            S.op('tensor', lambda e: e.matmul(out, lhsT=lhsT, rhs=rhs, start=start, stop=stop), R, W, dur=max(64, fsz(rhs)) / 2.4 + 30.0)

        def tr(out, in_, ident, R, W):
            S.op('tensor', lambda e: e.transpose(out, in_, ident), R, W, dur=max(64, int(in_.shape[0])) / 2.4 * (2.0 if in_.dtype == F32 else 1.0) + 30.0)

        def act(out, in_, func, R, W, bias=None, scale=None, accum=None):
            kw = {}
            if bias is not None:
                kw['bias'] = bias
            if scale is not None:
                kw['scale'] = scale
            if accum is not None:
                kw['accum_out'] = accum
            S.op('scalar', lambda e: e.activation(out=out, in_=in_, func=func, **kw), R, W, dur=max(64, fsz(in_)) / 1.4 + 180.0)

        def tt(eng, out, in0, in1, op, R, W):
            S.op(eng, lambda e: e.tensor_tensor(out=out, in0=in0, in1=in1, op=op), R, W, dur=max(64, fsz(out)) * (1.05 if eng == 'vector' else PF) + (70.0 if eng == 'vector' else 200.0))

        def ts(eng, out, in0, s1, s2, op0, op1, R, W):
            if op1 is None:
                S.op(eng, lambda e: e.tensor_scalar(out=out, in0=in0, scalar1=s1, scalar2=None, op0=op0), R, W, dur=max(64, fsz(out)) * 1.05 + 70.0)
            else:
                S.op(eng, lambda e: e.tensor_scalar(out=out, in0=in0, scalar1=s1, scalar2=s2, op0=op0, op1=op1), R, W, dur=max(64, fsz(out)) * 1.05 + 70.0)

        def stt(out, in0, scalar, in1, op0, op1, R, W):
            S.op('vector', lambda e: e.scalar_tensor_tensor(out=out, in0=in0, scalar=scalar, in1=in1, op0=op0, op1=op1), R, W, dur=max(64, fsz(out)) * 1.05 + 70.0)

        def red(out, in_, op, R, W):
            S.op('vector', lambda e: e.tensor_reduce(out=out, in_=in_, axis=AX.X, op=op), R, W, dur=max(64, fsz(in_)) * 1.05 + 70.0)

        def cp(eng, out, in_, R, W):
            if eng == 'scalar':
                S.op(eng, lambda e: e.copy(out=out, in_=in_), R, W, dur=max(64, fsz(out)) / 1.4 + 180.0)
            else:
                S.op(eng, lambda e: e.tensor_copy(out=out, in_=in_), R, W, dur=max(64, fsz(out)) * (1.05 if eng == 'vector' else PF) + (70.0 if eng == 'vector' else 200.0))

        def mset(eng, ap, val, W):
            S.op(eng, lambda e: e.memset(ap, val), (), W)

        def recip(out, in_, R, W):
            S.op('vector', lambda e: e.reciprocal(out=out, in_=in_), R, W)

        def scan(out, d0, d1, R, W):
            S.op('vector', lambda e: e.tensor_tensor_scan(out=out, data0=d0, data1=d1, initial=0.0, op0=ALU.mult, op1=ALU.add), R, W, dur=2.1 * fsz(out) + 70.0)

        def dma(eng, sem, out, in_, R, W, **kw):
            S.dma(eng, sem, lambda e: e.dma_start(out=out, in_=in_, **kw), R, W)

        def gather(sem, out, in_, idx, R, W):
            S.dma('gpsimd', sem, lambda e: e.indirect_dma_start(out=out, out_offset=None, in_=in_, in_offset=bass.IndirectOffsetOnAxis(ap=idx, axis=0)), R, W)

        G = es
        PS = [G.enter_context(nc.psum_tensor('ps%d' % i, [128, 512], F32)) for i in range(8)]
        PSK = ['PS%d' % i for i in range(8)]

        def psb(i):
            return PS[i][:].bitcast(BF16)

        identf = sbuf(G, 'identf', [128, 128], F32)
        identb = sbuf(G, 'identb', [128, 128], BF16)
        dma('sync', 'identf', identf[:], identf_d, [], ['identf'])
        cp('vector', identb[:], identf[:], ['identf'], ['identb'])

        def rmsnorm_rows(src_ps, src_key, gb, gb_key, out_ap, out_key, sq, ssv, eng_scratch_keys):
            jk = eng_scratch_keys or 'sq'
            act(sq[:], src_ps, AF.Square, [src_key], ['sq', jk], accum=ssv[:, 0:1])
            act(ssv[:, 1:2], ssv[:, 0:1], AF.Sqrt, ['sq'], ['ssv'], bias=epsq[:, 0:1], scale=1.0 / 512.0)
            recip(ssv[:, 2:3], ssv[:, 1:2], ['ssv'], ['ssv2'])
            stt(out_ap, src_ps, ssv[:, 2:3], gb, ALU.mult, ALU.mult, [src_key, 'ssv2', gb_key], [out_key])

        epsq = sbuf(G, 'epsq', [128, 2], F32)
        mset('vector', epsq[:, 0:1], 1e-6, ['epsq'])
        mset('vector', epsq[:, 1:2], 1e-5, ['epsq'])

        with ExitStack() as ph:
            xa = [sbuf(ph, 'xa%d' % i, [128, 2048], F32) for i in range(2)]
            xt = [sbuf(ph, 'xt%d' % i, [128, 16, 128], BF16) for i in range(2)]
            for blk in range(NBLK):
                b = blk % 2
                dma('sync', 'xa%d' % b, xa[b][:], xs[blk], [], ['xa%d' % b])
                for g in range(4):
                    pb = g % 2
                    for i in range(4):
                        k = 4 * g + i
                        tr(PS[pb][:, i * 128:(i + 1) * 128], xa[b][:, k * 128:(k + 1) * 128], identf[:],
                           ['xa%d' % b, 'identf'], [PSK[pb]])
                    cp('vector' if g % 2 == 0 else 'scalar', xt[b][:, 4 * g:4 * g + 4, :],
                       PS[pb][:].rearrange('p (a t) -> p a t', t=128), [PSK[pb]], ['xt%d_%d' % (b, g)])
                dma('sync', 'st_xt%d' % b, xT_d[blk].rearrange('p (k t) -> p k t', t=128), xt[b][:],
                    ['xt%d_%d' % (b, g) for g in range(4)], ['xT_d%d' % blk])
            S.barrier()


        def phase_s5():
          with ExitStack() as ph:
            TWO_PI = 6.283185307179586
            prm32 = sbuf(ph, 'prm32', [32, 3, 128], F32)
            lst2 = sbuf(ph, 'lst2', [32, 2], F32)
            prm = sbuf(ph, 'prm', [128, 3, 32], F32)
            sm = sbuf(ph, 'sm', [128, 40, 32], F32)
            smi = sbuf(ph, 'smi', [128, 32], I32)
            SMK = ['sm']

            def st(i):
                return sm[:, i, :]
            dma('sync', 'prm', prm32[:, 0, :], a_re_d.rearrange('(c g) p -> c (g p)', g=2), [], ['prm32a'])
            dma('sync', 'prm', prm32[:, 1, :], a_im_d.rearrange('(c g) p -> c (g p)', g=2), [], ['prm32b'])
            dma('sync', 'prm', lst2[:], lstep_d.rearrange('o (c g) -> (o c) g', g=2), [], ['lst2'])
            for g2 in range(2):
                cp('vector', prm32[:, 2, g2 * 64:(g2 + 1) * 64], lst2[:, g2:g2 + 1].to_broadcast([32, 64]), ['lst2'], ['prm32c%d' % g2])
            for i in range(3):
                tr(PS[0][:, i * 32:(i + 1) * 32], prm32[:, i, :], identf[0:32, 0:32],
                   ['prm32a', 'prm32b', 'prm32c0', 'prm32c1', 'identf'], [PSK[0]])
            cp('vector', prm[:], PS[0][:, 0:96].rearrange('p (a c) -> p a c', c=32), [PSK[0]], SMK)
            lam_r, lam_i, lst = prm[:, 0, :], prm[:, 1, :], prm[:, 2, :]

            def v2(out, a, b, op):
                tt('vector', out, a, b, op, SMK, SMK)

            def v1(out, a, s1, s2, op0, op1=None):
                ts('vector', out, a, s1, s2, op0, op1, SMK, SMK)

            def a1(out, a, func, scale=None, bias=None):
                act(out, a, func, SMK, SMK, bias=bias, scale=scale)
            DLT, XR, TH, MAG, MAGI, RR, KF, FF, TMP, SIN, GG, COS, AR, AI, IR, II, KR_, KI_, NUMR, DEN, T5, T6 = range(22)
            A128R, A128I, A127R, A127I, CURR, CURI = 22, 23, 24, 25, 26, 27
            a1(st(DLT), lst, AF.Exp)
            v2(st(XR), lam_r, st(DLT), ALU.mult)
            v2(st(TH), lam_i, st(DLT), ALU.mult)
            a1(st(MAG), st(XR), AF.Exp)
            a1(st(MAGI), st(XR), AF.Exp, scale=-1.0)
            v1(st(RR), st(TH), 1.0 / TWO_PI, None, ALU.mult)
            cp('vector', smi[:], st(RR), SMK, SMK)
            cp('vector', st(KF), smi[:], SMK, SMK)
            v2(st(FF), st(RR), st(KF), ALU.subtract)
            v1(st(TMP), st(FF), 0.5, None, ALU.is_gt)
            v2(st(FF), st(FF), st(TMP), ALU.subtract)
            v1(st(TMP), st(FF), -0.5, None, ALU.is_lt)
            v2(st(FF), st(FF), st(TMP), ALU.add)
            a1(st(SIN), st(FF), AF.Sin, scale=TWO_PI)
            v1(st(GG), st(FF), 0.25, None, ALU.add)
            v1(st(TMP), st(GG), 0.5, None, ALU.is_gt)
            v2(st(GG), st(GG), st(TMP), ALU.subtract)
            a1(st(COS), st(GG), AF.Sin, scale=TWO_PI)
            v2(st(AR), st(MAG), st(COS), ALU.mult)
            v2(st(AI), st(MAG), st(SIN), ALU.mult)
            v2(st(IR), st(MAGI), st(COS), ALU.mult)
            v2(st(II), st(MAGI), st(SIN), ALU.mult)
            v1(st(II), st(II), -1.0, None, ALU.mult)
            v1(st(NUMR), st(AR), -1.0, None, ALU.add)
            v2(st(DEN), lam_r, lam_r, ALU.mult)
            v2(st(T5), lam_i, lam_i, ALU.mult)
            v2(st(DEN), st(DEN), st(T5), ALU.add)
            recip(st(DEN), st(DEN), SMK, SMK)
            v2(st(T5), st(NUMR), lam_r, ALU.mult)
            v2(st(T6), st(AI), lam_i, ALU.mult)
            v2(st(T5), st(T5), st(T6), ALU.add)
            v2(st(KR_), st(T5), st(DEN), ALU.mult)
            v2(st(T5), st(AI), lam_r, ALU.mult)
            v2(st(T6), st(NUMR), lam_i, ALU.mult)
            v2(st(T5), st(T5), st(T6), ALU.subtract)
            v2(st(KI_), st(T5), st(DEN), ALU.mult)

            if SUB < 1:
                S.barrier()
                return
            BLF = sbuf(ph, 'BLF', [128, 32, 2, 128], BF16)
            ApowT = sbuf(ph, 'ApowT', [128, 2, 32, 128], BF16)
            Bz = sbuf(ph, 'Bz', [128, 2, 32, 32], F32)
            u_tm = sbuf(ph, 'u_tm', [128, 1024], BF16)
            CL3 = sbuf(ph, 'CL3', [128, 8, 2, 64], BF16)
            CLr = sbuf(ph, 'CLr', [128, 32, 32], BF16)
            CLn = sbuf(ph, 'CLn', [128, 32, 32], BF16)
            Tp_r = sbuf(ph, 'Tp_r', [128, 32, 128], BF16)
            Tp_i = sbuf(ph, 'Tp_i', [128, 32, 128], BF16)
            Tn_r = sbuf(ph, 'Tn_r', [128, 32, 128], BF16)
            Tn_i = sbuf(ph, 'Tn_i', [128, 32, 128], BF16)
            with ExitStack() as p2:
                Bn_r = sbuf(p2, 'Bn_r', [128, 32, 16], F32)
                Bn_i = sbuf(p2, 'Bn_i', [128, 32, 16], F32)
                Bb = sbuf(p2, 'Bb', [128, 2, 32, 16], F32)
                Bt = sbuf(p2, 'Bt', [128, 32, 16], F32)
                Cz = sbuf(p2, 'Cz', [128, 2, 8, 128], F32)
                for g2 in range(2):
                    dma('sync', 'Bn', Bn_r[g2 * 64:(g2 + 1) * 64, :, :], b_re_d.rearrange('(c g) p h -> g p c h', g=2)[g2], [], ['Bn_r%d' % g2])
                    dma('sync', 'Bn', Bn_i[g2 * 64:(g2 + 1) * 64, :, :], b_im_d.rearrange('(c g) p h -> g p c h', g=2)[g2], [], ['Bn_i%d' % g2])
                BNK = ['Bn_r0', 'Bn_r1', 'Bn_i0', 'Bn_i1']
                kr_b = st(KR_).unsqueeze(2).to_broadcast([128, 32, 16])
                ki_b = st(KI_).unsqueeze(2).to_broadcast([128, 32, 16])
                tt('vector', Bb[:, 0], Bn_r[:], kr_b, ALU.mult, BNK + SMK, ['Bb'])
                tt('vector', Bt[:], Bn_i[:], ki_b, ALU.mult, BNK + SMK, ['Bt'])
                tt('vector', Bb[:, 0], Bb[:, 0], Bt[:], ALU.subtract, ['Bb', 'Bt'], ['Bb'])
                tt('vector', Bb[:, 1], Bn_i[:], kr_b, ALU.mult, BNK + SMK, ['Bb'])
                tt('vector', Bt[:], Bn_r[:], ki_b, ALU.mult, BNK + SMK + ['Bb'], ['Bt'])
                tt('vector', Bb[:, 1], Bb[:, 1], Bt[:], ALU.add, ['Bb', 'Bt'], ['Bb'])
                mset('vector', Bz[:], 0.0, ['Bz'])
                for ri in range(2):
                    cp('vector', Bz[0:64, ri, :, 0:16], Bb[0:64, ri], ['Bb', 'Bz'], ['Bz'])
                    cp('vector', Bz[64:128, ri, :, 16:32], Bb[64:128, ri], ['Bb', 'Bz'], ['Bz'])
                BzF = sbuf(p2, 'BzF', [128, 32, 128], F32)
                for ri in range(2):
                    mset('vector', BzF[:], 0.0, ['BzF'])
                    bzf_v = BzF[:].rearrange('p (q a) (b h) -> p q a b h', a=4, h=32)
                    bz_v = Bz[:, ri].rearrange('p (q a) h -> p q a h', a=4)
                    for cl in range(4):
                        cp('vector', bzf_v[:, :, cl, cl, :], bz_v[:, :, cl, :], ['Bz', 'BzF'], ['BzF'])
                    for c in range(32):
                        pb = c % 2
                        tr(PS[pb][:, 0:128], BzF[:, c, :], identf[:], ['BzF', 'identf'], [PSK[pb]])
                        cp('vector' if c % 2 == 0 else 'scalar', BLF[:, c, ri, :], PS[pb][:, 0:128], [PSK[pb]], ['BL'])
                mset('vector', CL3[:], 0.0, ['CL3'])
                if SUB < 2:
                    S.barrier()
                    return
                mset('vector', Cz[:], 0.0, ['Cz'])
                for ri, cd in enumerate((c_re_d, c_im_d)):
                    for cl in range(4):
                        for g2 in range(2):
                            r0 = 32 * cl + 16 * g2
                            dma('sync', 'Cz', Cz[r0:r0 + 16, ri, :, 64 * g2:64 * g2 + 64],
                                cd.rearrange('(q r) h p -> r h q p', r=8)[2 * cl + g2], ['Cz'], ['Cz_%d_%d_%d' % (ri, cl, g2)])
                CZK = ['Cz_%d_%d_%d' % (ri, cl, g2) for ri in range(2) for cl in range(4) for g2 in range(2)]
                for ri in range(2):
                    for q in range(8):
                        pb = q % 2
                        tr(PS[pb][:, 0:128], Cz[:, ri, q, :], identf[:], CZK + ['identf'], [PSK[pb]])
                        src = PS[pb][:, 0:128].rearrange('p (a b) -> p a b', b=32)
                        if ri == 0:
                            cp('vector', CLr[:, 4 * q:4 * q + 4, :], src, [PSK[pb]], ['CLr'])
                            cp('vector', CL3[:, q, 0, 32:64], src[:, 3, :], [PSK[pb], 'CL3'], ['CL3'])
                        else:
                            S.op('scalar', lambda e, o=CLn[:, 4 * q:4 * q + 4, :], s=src: e.mul(o, s, -1.0), [PSK[pb]], ['CLn'])
                            S.op('scalar', lambda e, o=CL3[:, q, 1, 32:64], s=src[:, 3, :]: e.mul(o, s, -1.0), [PSK[pb], 'CL3'], ['CL3'])
                if SUB < 3:
                    S.barrier()
                    return
                TFr = sbuf(p2, 'TFr', [128, 32, 128], F32)
                TFi = sbuf(p2, 'TFi', [128, 32, 128], F32)
                tm1 = sbuf(p2, 'tm1', [128, 32, 64], F32)
                tm2 = sbuf(p2, 'tm2', [128, 32, 64], F32)
                TK = ['tab']
                for which in range(2):
                    br, bi = (AR, AI) if which == 0 else (IR, II)
                    cp('vector', st(CURR), st(br), SMK, SMK)
                    cp('vector', st(CURI), st(bi), SMK, SMK)
                    mset('vector', TFr[:, :, 0:1], 1.0, TK)
                    mset('vector', TFi[:, :, 0:1], 0.0, TK)
                    for k in range(7):
                        n = 1 << k
                        cr = st(CURR).unsqueeze(2).to_broadcast([128, 32, n])
                        ci = st(CURI).unsqueeze(2).to_broadcast([128, 32, n])
                        tt('vector', tm1[:, :, 0:n], TFr[:, :, 0:n], cr, ALU.mult, TK + SMK, TK)
                        tt('vector', tm2[:, :, 0:n], TFi[:, :, 0:n], ci, ALU.mult, TK + SMK, TK)
                        tt('vector', TFr[:, :, n:2 * n], tm1[:, :, 0:n], tm2[:, :, 0:n], ALU.subtract, TK, TK)
                        tt('vector', tm1[:, :, 0:n], TFr[:, :, 0:n], ci, ALU.mult, TK + SMK, TK)
                        tt('vector', tm2[:, :, 0:n], TFi[:, :, 0:n], cr, ALU.mult, TK + SMK, TK)
                        tt('vector', TFi[:, :, n:2 * n], tm1[:, :, 0:n], tm2[:, :, 0:n], ALU.add, TK, TK)
                        v2(st(T5), st(CURR), st(CURR), ALU.mult)
                        v2(st(T6), st(CURI), st(CURI), ALU.mult)
                        v2(st(TMP), st(CURR), st(CURI), ALU.mult)
                        v2(st(CURR), st(T5), st(T6), ALU.subtract)
                        v1(st(CURI), st(TMP), 2.0, None, ALU.mult)
                    if which == 0:
                        cp('vector', st(A128R), st(CURR), SMK, SMK)
                        cp('vector', st(A128I), st(CURI), SMK, SMK)
                        cp('vector', st(A127R), TFr[:, :, 127], TK + SMK, SMK)
                        cp('vector', st(A127I), TFi[:, :, 127], TK + SMK, SMK)
                        cp('vector', Tp_r[:], TFr[:], TK, ['Tp'])
                        cp('vector', Tp_i[:], TFi[:], TK, ['Tp'])
                    else:
                        cp('vector', Tn_r[:], TFr[:], TK, ['Tn'])
                        cp('vector', Tn_i[:], TFi[:], TK, ['Tn'])
                        TRb = sbuf(p2, 'TRb', [128, 2, 32, 128], BF16)
                        a7r = st(A127R).unsqueeze(2).to_broadcast([128, 32, 128])
                        a7i = st(A127I).unsqueeze(2).to_broadcast([128, 32, 128])
                        for hh_ in range(2):
                            sl_ = slice(hh_ * 64, (hh_ + 1) * 64)
                            a7r_ = st(A127R).unsqueeze(2).to_broadcast([128, 32, 64])
                            a7i_ = st(A127I).unsqueeze(2).to_broadcast([128, 32, 64])
                            tt('vector', tm1[:], TFr[:, :, sl_], a7r_, ALU.mult, TK + SMK, TK)
                            tt('vector', tm2[:], TFi[:, :, sl_], a7i_, ALU.mult, TK + SMK, TK)
                            tt('vector', TRb[:, 0, :, sl_], tm1[:], tm2[:], ALU.subtract, TK, ['TRb'])
                            tt('vector', tm1[:], TFr[:, :, sl_], a7i_, ALU.mult, TK + SMK + ['TRb'], TK)
                            tt('vector', tm2[:], TFi[:, :, sl_], a7r_, ALU.mult, TK + SMK, TK)
                            tt('vector', TRb[:, 1, :, sl_], tm1[:], tm2[:], ALU.add, TK, ['TRb'])
                        for ri in range(2):
                            for c8 in range(4):
                                pb = c8 % 2
                                for i in range(8):
                                    c = c8 * 8 + i
                                    tr(psb(pb)[:, i * 128:(i + 1) * 128], TRb[:, ri, c, :], identb[:], ['TRb', 'identb'], [PSK[pb]])
                                cp('vector' if c8 % 2 == 0 else 'scalar', ApowT[:, ri, c8 * 8:(c8 + 1) * 8, :], psb(pb)[:, :].rearrange('p (a b) -> p a b', b=128), [PSK[pb]], ['ApowT'])
                S.barrier()

            if SUB < 4:
                S.barrier()
                return
            wB = sbuf(ph, 'wB', [128, 16, 1024], BF16)
            wgl = sbuf(ph, 'wgl', [128, 8, 1024], BF16)
            Dt = sbuf(ph, 'Dt', [128, 8], F32)
            bg = sbuf(ph, 'bg', [128, 8], F32)
            w_in_v = w_in.rearrange('(k p) n -> p k n', p=128)
            for k4 in range(4):
                dma('gpsimd', 'wB', wB[:, 4 * k4:4 * k4 + 4, :], w_in_v[:, 4 * k4:4 * k4 + 4, 0:1024], [], ['wB%d' % k4])
            WBK = ['wB%d' % k4 for k4 in range(4)]
            w_glu_v = w_glu.rearrange('(k p) n -> p k n', p=128)
            for k4 in range(2):
                dma('gpsimd', 'wgl', wgl[:, 4 * k4:4 * k4 + 4, :], w_glu_v[:, 4 * k4:4 * k4 + 4, :], [], ['wgl%d' % k4])
            WGK = ['wgl0', 'wgl1']
            dma('sync', 'Dt', Dt[:], d_d.rearrange('o (k p) -> p (o k)', p=128), [], ['Dt'], allow_slow_non_contiguous=True)
            dma('sync', 'bg', bg[:], b_glu.rearrange('o (k p) -> p (o k)', p=128), [], ['bg'], allow_slow_non_contiguous=True)

            xT = [sbuf(ph, 'xT%d' % i, [128, 16, 128], BF16) for i in range(2)]
            uT_f = sbuf(ph, 'uT_f', [128, 8, 128], F32)
            uT_b = sbuf(ph, 'uT_b', [128, 8, 128], BF16)
            gT = sbuf(ph, 'gT', [128, 8, 128], BF16)
            tmp = [[sbuf(ph, 't%d_%d' % (p, i), [128, 512], F32) for i in range(4)] for p in range(2)]
            z_r = [sbuf(ph, 'z_r%d' % p, [128, 512], F32) for p in range(2)]
            z_i = [sbuf(ph, 'z_i%d' % p, [128, 512], F32) for p in range(2)]
            G_r = [sbuf(ph, 'G_r%d' % p, [128, 512], F32) for p in range(1)]
            G_i = [sbuf(ph, 'G_i%d' % p, [128, 512], F32) for p in range(1)]
            h_r = [sbuf(ph, 'h_r%d' % p, [128, 512], F32) for p in range(1)]
            h_i = [sbuf(ph, 'h_i%d' % p, [128, 512], F32) for p in range(1)]
            hb_r = [sbuf(ph, 'hb_r%d' % p, [128, 4, 128], BF16) for p in range(1)]
            hb_i = [sbuf(ph, 'hb_i%d' % p, [128, 4, 128], BF16) for p in range(1)]
            yv = [sbuf(ph, 'yv%d' % p, [128, 128], F32) for p in range(2)]
            ge = [[sbuf(ph, 'ge%d_%d' % (p, i), [128, 128], F32) for i in range(2)] for p in range(2)]
            sg = [sbuf(ph, 'sg%d' % p, [128, 128], F32) for p in range(1)]
            H_r = sbuf(ph, 'H_r', [128, 32, 16], F32)
            H_i = sbuf(ph, 'H_i', [128, 32, 16], F32)
            CA_r = sbuf(ph, 'CA_r', [128, 32, 16], F32)
            CA_i = sbuf(ph, 'CA_i', [128, 32, 16], F32)
            ctm = sbuf(ph, 'ctm', [128, 32, 16], F32)
            Ss_r = sbuf(ph, 'Ss_r', [128, 32], F32)
            Ss_i = sbuf(ph, 'Ss_i', [128, 32], F32)
            ccs = sbuf(ph, 'ccs', [128, 8, 128], BF16)
            smask_p = sbuf(ph, 'smask_p', [128, 128], F32)
            smask_s = sbuf(ph, 'smask_s', [128, 128], F32)
            s0 = sbuf(ph, 's0', [16, 2, 512], F32)
            mset('vector', smask_p[:], 1.0, ['smask'])
            mset('vector', smask_p[:, 0:1], 0.0, ['smask'])
            mset('vector', smask_s[:], 1.0, ['smask'])
            mset('vector', smask_s[:].rearrange('p (s t) -> p s t', t=8)[:, :, 0:1], 0.0, ['smask'])
            mset('vector', H_r[:], 0.0, ['H'])
            mset('vector', H_i[:], 0.0, ['H'])
            HK = ['H']

            def cmul_b(o_r, o_i, x_r, x_i, y_r, y_i, t, R, W):
                tt('vector', o_r, x_r, y_r, ALU.mult, R, W)
                tt('vector', t, x_i, y_i, ALU.mult, R, ['ctm'])
                tt('vector', o_r, o_r, t, ALU.subtract, W + ['ctm'], W)
                tt('vector', o_i, x_r, y_i, ALU.mult, R, W)
                tt('vector', t, x_i, y_r, ALU.mult, R + W, ['ctm'])
                tt('vector', o_i, o_i, t, ALU.add, W + ['ctm'], W)

            for blk in range(NBLK_RUN):
                own = blk in OWN
                ob = OWN.index(blk) if own else -1
                smp = blk == 32
                ns, tlen = (16, 8) if smp else (1, 128)
                smask = smask_s if smp else smask_p
                xb = blk % 2
                dma('sync', 'xT%d' % xb, xT[xb][:], xT_d[blk].rearrange('p (k t) -> p k t', t=128), ['xT_d%d' % blk], ['xT%d' % xb])
                if smp:
                    for qtr in range(8):
                        dma('sync', 's0r', s0[:, 0, :], st_re[:, qtr * 512:(qtr + 1) * 512], [], ['s0r'])
                        dma('sync', 's0i', s0[:, 1, :], st_im[:, qtr * 512:(qtr + 1) * 512], [], ['s0i'])
                        for ri in range(2):
                            for c8 in range(4):
                                c = qtr * 4 + c8
                                tr(PS[6 + ri][:, c * 16:(c + 1) * 16], s0[:, ri, c8 * 128:(c8 + 1) * 128], identf[0:16, 0:16], ['s0r', 's0i', 'identf'], [PSK[6 + ri]])
                    for ri, Hx in enumerate((H_r, H_i)):
                        cp('vector', Hx[:], PS[6 + ri][:].rearrange('p (c s) -> p c s', s=16), [PSK[6 + ri]], HK)
                if own:
                    arb = st(AR).unsqueeze(2).to_broadcast([128, 32, ns])
                    aib = st(AI).unsqueeze(2).to_broadcast([128, 32, ns])
                    cmul_b(CA_r[:, :, 0:ns], CA_i[:, :, 0:ns], H_r[:, :, 0:ns], H_i[:, :, 0:ns], arb, aib, ctm[:, :, 0:ns], HK + SMK, ['CA'])
                if not own:
                    for n_ in range(2):
                        for k in range(16):
                            mm(PS[n_][:, 0:512], xT[xb][:, k, :], wB[:, k, n_ * 512:(n_ + 1) * 512], k == 0, k == 15, WBK + ['xT%d' % xb], [PSK[n_]])
                        cp('scalar', u_tm[:, n_ * 512:(n_ + 1) * 512], PS[n_][:, 0:512], [PSK[n_]], ['u_tm%d' % n_])
                    for hf in range(2):
                        b_re, b_im = (2, 3) if hf == 0 else (4, 5)
                        for cc in range(16):
                            c = hf * 16 + cc
                            mm(PS[b_re][:, cc * 32:(cc + 1) * 32], ApowT[:, 0, c, :], u_tm[:, c * 32:(c + 1) * 32], True, True, ['ApowT', 'u_tm0', 'u_tm1'], [PSK[b_re]])
                            mm(PS[b_im][:, cc * 32:(cc + 1) * 32], ApowT[:, 1, c, :], u_tm[:, c * 32:(c + 1) * 32], True, True, ['ApowT', 'u_tm0', 'u_tm1'], [PSK[b_im]])
                        vr = PS[b_re][:, 0:512].rearrange('p (c h) -> p c h', h=32)
                        vi = PS[b_im][:, 0:512].rearrange('p (c h) -> p c h', h=32)
                        bzr = Bz[:, 0, hf * 16:(hf + 1) * 16, :]
                        bzi = Bz[:, 1, hf * 16:(hf + 1) * 16, :]
                        t1, t2, t3, t4 = [tmp[hf][i][:].rearrange('p (c h) -> p c h', h=32) for i in range(4)]
                        tk = ['t%d_%d' % (hf, i) for i in range(4)]
                        tt('vector', t1, vr, bzr, ALU.mult, [PSK[b_re], 'Bz'], [tk[0]])
                        tt('vector', t2, vi, bzi, ALU.mult, [PSK[b_im], 'Bz'], [tk[1]])
                        tt('gpsimd', t1, t1, t2, ALU.subtract, [tk[0], tk[1]], [tk[0]])
                        red(Ss_r[:, hf * 16:(hf + 1) * 16], t1, ALU.add, [tk[0]], ['Ss'])
                        tt('vector', t3, vi, bzr, ALU.mult, [PSK[b_im], 'Bz'], [tk[2]])
                        tt('vector', t4, vr, bzi, ALU.mult, [PSK[b_re], 'Bz'], [tk[3]])
                        tt('gpsimd', t3, t3, t4, ALU.add, [tk[2], tk[3]], [tk[2]])
                        red(Ss_i[:, hf * 16:(hf + 1) * 16], t3, ALU.add, [tk[2]], ['Ss'])
                    Hr0, Hi0 = H_r[:, :, 0], H_i[:, :, 0]
                    cmul_b(st(T5), st(T6), Hr0, Hi0, st(A128R), st(A128I), st(TMP), HK + SMK, SMK)
                    tt('vector', Hr0, st(T5), Ss_r[:], ALU.add, SMK + ['Ss'], HK)
                    tt('vector', Hi0, st(T6), Ss_i[:], ALU.add, SMK + ['Ss'], HK)
                    continue
                for m in range(8):
                    bank, slot = m // 4, m % 4
                    for k in range(16):
                        mm(PS[bank][:, slot * 128:(slot + 1) * 128], wB[:, k, m * 128:(m + 1) * 128], xT[xb][:, k, :], k == 0, k == 15,
                           WBK + ['xT%d' % xb], [PSK[bank]])
                cp('scalar', uT_f[:, 0:4, :], PS[0][:].rearrange('p (a t) -> p a t', t=128), [PSK[0]], ['uT_f0'])
                cp('vector', uT_f[:, 4:8, :], PS[1][:].rearrange('p (a t) -> p a t', t=128), [PSK[1]], ['uT_f1'])
                cp('gpsimd', uT_b[:], uT_f[:], ['uT_f0', 'uT_f1'], ['uT_b'])
                if SUB2 < 1:
                    continue
                for o in range(8):
                    p = o % 2
                    br_, bi_ = (2, 3) if o % 2 == 0 else (4, 5)
                    for cl in range(4):
                        l0, l1, rr_ = BLF[:, 4 * o + cl, 0, :], BLF[:, 4 * o + cl, 1, :], uT_b[:, o, :]
                        mm(PS[br_][:, cl * 128:(cl + 1) * 128], l0, rr_, True, True, ['BL', 'uT_b'], [PSK[br_]])
                        mm(PS[bi_][:, cl * 128:(cl + 1) * 128], l1, rr_, True, True, ['BL', 'uT_b'], [PSK[bi_]])

                    def v4(ap):
                        return ap.rearrange('p (c s t) -> p c s t', c=4, t=tlen)

                    def tb(T):
                        return T[:, 4 * o:4 * o + 4, 0:tlen].unsqueeze(2).to_broadcast([128, 4, ns, tlen])
                    if SUB2 < 2:
                        continue
                    pre, pim = v4(PS[br_][:]), v4(PS[bi_][:])
                    t1, t2, t3, t4 = [v4(tmp[p][i][:]) for i in range(4)]
                    tk = ['t%d_%d' % (p, i) for i in range(4)]
                    tt('vector', t1, pre, tb(Tn_r), ALU.mult, [PSK[br_], 'Tn'], [tk[0]])
                    tt('vector', t2, pim, tb(Tn_i), ALU.mult, [PSK[bi_], 'Tn'], [tk[1]])
                    if SUB2 < 3:
                        continue
                    tt('gpsimd', v4(z_r[p][:]), t1, t2, ALU.subtract, [tk[0], tk[1]], ['z_r%d' % p])
                    tt('vector', t3, pre, tb(Tn_i), ALU.mult, [PSK[br_], 'Tn'], [tk[2]])
                    tt('vector', t4, pim, tb(Tn_r), ALU.mult, [PSK[bi_], 'Tn'], [tk[3]])
                    tt('gpsimd', v4(z_i[p][:]), t3, t4, ALU.add, [tk[2], tk[3]], ['z_i%d' % p])
                    if SUB2 < 4:
                        continue
                    if not own:
                        red(Ss_r[:, 4 * o:4 * o + 4], z_r[p][:].rearrange('p (c t) -> p c t', c=4), ALU.add, ['z_r%d' % p], ['Ss'])
                        red(Ss_i[:, 4 * o:4 * o + 4], z_i[p][:].rearrange('p (c t) -> p c t', c=4), ALU.add, ['z_i%d' % p], ['Ss'])
                        continue
                    zr0 = v4(z_r[p][:])[:, :, :, 0]
                    zi0 = v4(z_i[p][:])[:, :, :, 0]
                    tt('vector', zr0, zr0, CA_r[:, 4 * o:4 * o + 4, 0:ns], ALU.add, ['z_r%d' % p, 'CA'], ['z_r%d' % p])
                    tt('vector', zi0, zi0, CA_i[:, 4 * o:4 * o + 4, 0:ns], ALU.add, ['z_i%d' % p, 'CA'], ['z_i%d' % p])
                    for cl in range(4):
                        scan(G_r[0][:, cl * 128:(cl + 1) * 128], smask[:], z_r[p][:, cl * 128:(cl + 1) * 128], ['smask', 'z_r%d' % p], ['G_r0'])
                        scan(G_i[0][:, cl * 128:(cl + 1) * 128], smask[:], z_i[p][:, cl * 128:(cl + 1) * 128], ['smask', 'z_i%d' % p], ['G_i0'])
                    gr, gi = v4(G_r[0][:]), v4(G_i[0][:])
                    tt('vector', t1, gr, tb(Tp_r), ALU.mult, ['G_r0', 'Tp'], [tk[0]])
                    tt('vector', t2, gi, tb(Tp_i), ALU.mult, ['G_i0', 'Tp'], [tk[1]])
                    tt('gpsimd', v4(h_r[0][:]), t1, t2, ALU.subtract, [tk[0], tk[1]], ['h_r0'])
                    tt('vector', t3, gr, tb(Tp_i), ALU.mult, ['G_r0', 'Tp'], [tk[2]])
                    tt('vector', t4, gi, tb(Tp_r), ALU.mult, ['G_i0', 'Tp'], [tk[3]])
                    tt('gpsimd', v4(h_i[0][:]), t3, t4, ALU.add, [tk[2], tk[3]], ['h_i0'])
                    cp('scalar', hb_r[0][:], h_r[0][:].rearrange('p (c t) -> p c t', c=4), ['h_r0'], ['hb_r0'])
                    cp('scalar', hb_i[0][:], h_i[0][:].rearrange('p (c t) -> p c t', c=4), ['h_i0'], ['hb_i0'])
                    cp('scalar', H_r[:, 4 * o:4 * o + 4, 0:ns], v4(h_r[0][:])[:, :, :, tlen - 1], ['h_r0', 'CA'], HK)
                    cp('scalar', H_i[:, 4 * o:4 * o + 4, 0:ns], v4(h_i[0][:])[:, :, :, tlen - 1], ['h_i0', 'CA'], HK)
                    py = 6 + o % 2
                    YR = ['CLr', 'CLn', 'CL3', 'hb_r0', 'hb_i0']
                    mm(PS[py][64:128, 0:128], CL3[:, o, 0, :], hb_r[0][:, 3, :], True, False, YR, [PSK[py]])
                    mm(PS[py][64:128, 0:128], CL3[:, o, 1, :], hb_i[0][:, 3, :], False, False, YR, [PSK[py]])
                    mm(PS[py][64:96, 0:128], CLr[:, 4 * o + 2, :], hb_r[0][:, 2, :], False, False, YR, [PSK[py]])
                    mm(PS[py][64:96, 0:128], CLn[:, 4 * o + 2, :], hb_i[0][:, 2, :], False, True, YR, [PSK[py]])
                    for cl in range(2):
                        c = 4 * o + cl
                        mm(PS[py][32 * cl:32 * cl + 32, 0:128], CLr[:, c, :], hb_r[0][:, cl, :], True, False, YR, [PSK[py]])
                        mm(PS[py][32 * cl:32 * cl + 32, 0:128], CLn[:, c, :], hb_i[0][:, cl, :], False, True, YR, [PSK[py]])
                    stt(yv[p][:], uT_f[:, o, :], Dt[:, o:o + 1], PS[py][:, 0:128], ALU.mult, ALU.add, ['uT_f0', 'uT_f1', 'Dt', PSK[py]], ['yv%d' % p])
                    act(ge[p][0][:], yv[p][:], AF.Square, ['yv%d' % p], ['ge%d_0' % p])
                    ts('vector', ge[p][0][:], ge[p][0][:], 0.044715, 1.0, ALU.mult, ALU.add, ['ge%d_0' % p], ['ge%d_0' % p])
                    tt('vector', ge[p][1][:], ge[p][0][:], yv[p][:], ALU.mult, ['ge%d_0' % p, 'yv%d' % p], ['ge%d_1' % p])
                    act(ge[p][0][:], ge[p][1][:], AF.Sigmoid, ['ge%d_1' % p], ['ge%d_0' % p], scale=1.5957691216057308)
                    tt('vector', gT[:, o, :], yv[p][:], ge[p][0][:], ALU.mult, ['yv%d' % p, 'ge%d_0' % p], ['gT%d' % o])
                if own:
                    GTK = ['gT%d' % o for o in range(8)]
                    for m in range(8):
                        p = 0
                        pg = 6 + m % 2
                        for k in range(8):
                            mm(PS[pg][:, 128:256], wgl[:, k, m * 128:(m + 1) * 128], gT[:, k, :], k == 0, k == 7, WGK + GTK, [PSK[pg]])
                        act(sg[p][:], PS[pg][:, 128:256], AF.Sigmoid, [PSK[pg], 'bg'], ['sg%d' % p], bias=bg[:, m:m + 1])
                        tt('vector', ccs[:, m, :], gT[:, m, :], sg[p][:], ALU.mult, GTK + ['sg%d' % p], ['ccs%d' % m])
                    dma('sync', 'ccs', cc_d[ob].rearrange('p (k t) -> p k t', t=128)[:, 0:8, :], ccs[:], ['ccs%d' % m for m in range(8)], ['cc_d'])
                    if blk == 31:
                        tr(PS[0][0:32, 0:128], H_r[:, :, 0], identf[:], HK + ['identf'], [PSK[0]])
                        tr(PS[0][0:32, 128:256], H_i[:, :, 0], identf[:], HK + ['identf'], [PSK[0]])
                        stp_t = tmp[1][1][0:32, 0:256].rearrange('p (a b) -> p a b', b=128)
                        cp('vector', stp_t, PS[0][0:32, 0:256].rearrange('p (a b) -> p a b', b=128), [PSK[0]], ['t1_1'])
                        dma('sync', 'stp_o', stp_o.rearrange('a c f -> c a f'), stp_t, ['t1_1'], ['stp_o'])
                    if smp:
                        for ri, Hx in enumerate((H_r, H_i)):
                            for c4 in range(8):
                                for i in range(4):
                                    c = 4 * c4 + i
                                    tr(PS[c4 % 2][0:16, i * 128:(i + 1) * 128], Hx[:, c, :], identf[:], HK + ['identf'], [PSK[c4 % 2]])
                                sto_ = tmp[c4 % 2][0][0:16, :]
                                cp('vector' if c4 % 2 == 0 else 'scalar', sto_, PS[c4 % 2][0:16, :], [PSK[c4 % 2]], ['t%d_0' % (c4 % 2)])
                                dma('sync', 'sts_o%d' % (c4 % 2), sts_o[ri, :, c4 * 512:(c4 + 1) * 512], sto_, ['t%d_0' % (c4 % 2)], ['sts_o'])
                elif SUB2 >= 5:
                    Hr0, Hi0 = H_r[:, :, 0], H_i[:, :, 0]
                    cmul_b(st(T5), st(T6), Hr0, Hi0, st(A128R), st(A128I), st(TMP), HK + SMK, SMK)
                    cmul_b(st(NUMR), st(DEN), Ss_r[:], Ss_i[:], st(A127R), st(A127I), st(TMP), ['Ss'] + SMK, SMK)
                    tt('vector', Hr0, st(T5), st(NUMR), ALU.add, SMK, HK)
                    tt('vector', Hi0, st(T6), st(DEN), ALU.add, SMK, HK)
            S.barrier()

        def phase_attn():
          with ExitStack() as ph:
            CKV = sbuf(ph, 'CKV', [128, NBLK, 512], BF16)
            KT = sbuf(ph, 'KT', [128, 4, NBLK * 128], BF16)
            KRT = sbuf(ph, 'KRT', [64, NBLK * 128], BF16)
            ssv = sbuf(ph, 'ssv', [128, 4], F32)
            w_in_v = w_in.rearrange('(k p) n -> p k n', p=128)
            with ExitStack() as pa:
                wA = sbuf(pa, 'wA', [128, 16, 640], BF16)
                gkv_b = sbuf(pa, 'gkv_b', [128, 512], F32)
                sq = sbuf(pa, 'sq', [128, 512], F32)
                xT = [sbuf(pa, 'xTa%d' % i, [128, 16, 128], BF16) for i in range(2)]
                dma('sync', 'gkv_b', gkv_b[:], g_kv.partition_broadcast(128), [], ['gkv_b'])
                ckv_f = [sbuf(pa, 'ckv_f%d' % i, [128, 512], F32) for i in range(2)]
                rk = [sbuf(pa, 'rk%d' % i, [128, 128], F32) for i in range(2)]
                krt = [sbuf(pa, 'krt%d' % i, [128, 128], F32) for i in range(2)]
                kr_f = [sbuf(pa, 'kr_f%d' % i, [128, 64], F32) for i in range(2)]
                krb = [sbuf(pa, 'krb%d' % i, [128, 64], BF16) for i in range(2)]
                for k4 in range(4):
                    dma('gpsimd', 'wA', wA[:, 4 * k4:4 * k4 + 4, 0:576], w_in_v[:, 4 * k4:4 * k4 + 4, 1536:2112], [], ['wA%d' % k4])
                WAK = ['wA%d' % k4 for k4 in range(4)]
                ts('vector', wA[:, :, 576:608], wA[:, :, 544:576], -1.0, None, ALU.mult, None, WAK, ['wArot0'])
                cp('vector', wA[:, :, 608:640], wA[:, :, 512:544], WAK, ['wArot1'])
                WAK = WAK + ['wArot0', 'wArot1']
                for blk in range(NBLK):
                    b = blk % 2
                    p0, p1, p2, p3 = (0, 1, 2, 3) if b == 0 else (4, 5, 6, 7)
                    dma('sync', 'xTa%d' % b, xT[b][:], xT_d[blk].rearrange('p (k t) -> p k t', t=128), [], ['xTa%d' % b])
                    dma('sync', 'rk%d' % b, rk[b][:], ropek[blk], [], ['rk%d' % b])
                    for k in range(16):
                        mm(PS[p0][:, 0:512], xT[b][:, k, :], wA[:, k, 0:512], k == 0, k == 15, WAK + ['xTa%d' % b], [PSK[p0]])
                    for k in range(16):
                        mm(PS[p1][:, 0:128], xT[b][:, k, :], wA[:, k, 512:640], k == 0, k == 15, WAK + ['xTa%d' % b], [PSK[p1]])
                    rmsnorm_rows(PS[p0][:, 0:512], PSK[p0], gkv_b[:], 'gkv_b', ckv_f[b][:], 'ckv_f%d' % b, sq, ssv, None)
                    cp('gpsimd', CKV[:, blk, :], ckv_f[b][:], ['ckv_f%d' % b], ['CKV%d' % blk])
                    dma('sync', 'o_ckv%d' % b, ckv_o[blk], ckv_f[b][:], ['ckv_f%d' % b], ['ckv_o'])
                    tt('vector', krt[b][:], PS[p1][:, 0:128], rk[b][:], ALU.mult, [PSK[p1], 'rk%d' % b], ['krt%d' % b])
                    tt('gpsimd', kr_f[b][:], krt[b][:, 0:64], krt[b][:, 64:128], ALU.add, ['krt%d' % b], ['kr_f%d' % b])
                    dma('sync', 'o_kr%d' % b, kr_o[blk], kr_f[b][:], ['kr_f%d' % b], ['kr_o'])
                    cp('gpsimd', krb[b][:], kr_f[b][:], ['kr_f%d' % b], ['krb%d' % b])
                    for c in range(4):
                        tr(psb(p2)[:, c * 128:(c + 1) * 128], CKV[:, blk, c * 128:(c + 1) * 128], identb[:], ['CKV%d' % blk, 'identb'], [PSK[p2]])
                    cp('scalar', KT[:, :, blk * 128:(blk + 1) * 128], psb(p2)[:, 0:512].rearrange('p (c t) -> p c t', t=128), [PSK[p2]], ['KT%d' % blk])
                    tr(psb(p3)[0:64, 0:128], krb[b][:], identb[:], ['krb%d' % b, 'identb'], [PSK[p3]])
                    cp('vector', KRT[:, blk * 128:(blk + 1) * 128], psb(p3)[0:64, 0:128], [PSK[p3]], ['KRT%d' % blk])
                S.barrier()
            if STAGE < 3:
                return
            with ExitStack() as pc:
                wq = sbuf(pc, 'wq', [128, 16, 512], BF16)
                gq_b = sbuf(pc, 'gq_b', [128, 512], F32)
                xT = [sbuf(pc, 'xTc', [128, 16, 128], BF16)]
                dma('sync', 'gq_b', gq_b[:], g_q.partition_broadcast(128), [], ['gq_b'])
                wuq = sbuf(pc, 'wuq', [128, 4, 1536], BF16)
                wuqr = sbuf(pc, 'wuqr', [128, 4, 8, 64], BF16)
                wukT = sbuf(pc, 'wukT', [128, 8, 512], BF16)
                wuv = sbuf(pc, 'wuv', [128, 4, 1024], BF16)
                cqn_b = sbuf(pc, 'cqn_b', [128, 512], BF16)
                cqnT = sbuf(pc, 'cqnT', [128, 4, 128], BF16)
                qnT = sbuf(pc, 'qnT', [128, 8, 128], BF16)
                QRT = sbuf(pc, 'QRT', [64, 8, 128], BF16)
                QLT = sbuf(pc, 'QLT', [128, 4, 8, 128], BF16)
                OLT = sbuf(pc, 'OLT', [128, 4, 8, 128], BF16)
                cca = sbuf(pc, 'cca', [128, 8, 128], BF16)
                rq = sbuf(pc, 'rq', [64, 256], F32)
                tq = sbuf(pc, 'tq', [64, 256], F32)
                Ssb = [sbuf(pc, 'Ssb%d' % i, [128, 512], F32) for i in range(2)]
                Pb = [sbuf(pc, 'Pb%d' % i, [128, 512], BF16) for i in range(2)]
                PTs = [sbuf(pc, 'PTs%d' % i, [128, 4, 128], BF16) for i in range(2)]
                acc2 = [sbuf(pc, 'acc%d' % i, [128, 512], F32) for i in range(2)]
                ob_ = sbuf(pc, 'ob_', [128, 512], BF16)
                stt2 = [sbuf(pc, 'stat%d' % i, [128, 16], F32) for i in range(2)]

                for k4 in range(4):
                    dma('gpsimd', 'wq', wq[:, 4 * k4:4 * k4 + 4, :], w_in_v[:, 4 * k4:4 * k4 + 4, 1024:1536], [], ['wq%d' % k4])
                WQK = ['wq%d' % k4 for k4 in range(4)]
                dma('gpsimd', 'wuq', wuq[:], w_uq.rearrange('(c p) n -> p c n', p=128), [], ['wuq'])
                dma('gpsimd', 'wuv', wuv[:], w_uv.rearrange('(c p) n -> p c n', p=128), [], ['wuv'])
                wv = wuq[:].rearrange('p c (h d) -> p c h d', d=192)
                ts('vector', wuqr[:, :, :, 0:32], wv[:, :, :, 160:192], -1.0, None, ALU.mult, None, ['wuq'], ['wuqr0'])
                cp('vector', wuqr[:, :, :, 32:64], wv[:, :, :, 128:160], ['wuq'], ['wuqr1'])
                with ExitStack() as pw:
                    wukn = sbuf(pw, 'wukn', [128, 4, 1024], BF16)
                    dma('gpsimd', 'wukn', wukn[:], w_uk.rearrange('(c p) n -> p c n', p=128), [], ['wukn'])
                    for h in range(8):
                        pb = 2 + h % 2
                        for c in range(4):
                            tr(psb(pb)[:, c * 128:(c + 1) * 128], wukn[:, c, h * 128:(h + 1) * 128], identb[:], ['wukn', 'identb'], [PSK[pb]])
                        cp('vector' if h % 2 == 0 else 'scalar', wukT[:, h, :], psb(pb)[:, 0:512], [PSK[pb]], ['wukT'])
                    S.barrier()

                MX, MNEW, NEGM, RSUM, CORR, LRUN, MRUN, LINV = range(8)

                def attn_tile(qr, lhs_list, rhs_list, nk, mask_ap, mask_key, first, pv_list, par, RK, PVK):
                    bs, bpt, bo = par, 2 + par, 4 + par
                    stt_ = stt2[par]
                    acc = acc2[par]
                    STK = ['stat%d' % par]
                    ACK = 'acc%d' % par
                    n = len(lhs_list)
                    for i in range(n):
                        mm(PS[bs][0:qr, 0:nk], lhs_list[i], rhs_list[i], i == 0, i == n - 1, RK, [PSK[bs]])
                    if mask_ap is not None:
                        tt('vector', Ssb[par][0:qr, 0:nk], PS[bs][0:qr, 0:nk], mask_ap, ALU.add, [PSK[bs], mask_key], ['Ssb%d' % par])
                        src, srck = Ssb[par][0:qr, 0:nk], 'Ssb%d' % par
                    else:
                        src, srck = PS[bs][0:qr, 0:nk], PSK[bs]
                    red(stt_[0:qr, MX:MX + 1], src, ALU.max, [srck], STK)
                    if first:
                        cp('vector', stt_[0:qr, MNEW:MNEW + 1], stt_[0:qr, MX:MX + 1], STK, STK)
                    else:
                        tt('vector', stt_[0:qr, MNEW:MNEW + 1], stt_[0:qr, MRUN:MRUN + 1], stt_[0:qr, MX:MX + 1], ALU.max, STK, STK)
                    ts('vector', stt_[0:qr, NEGM:NEGM + 1], stt_[0:qr, MNEW:MNEW + 1], -SCALE, None, ALU.mult, None, STK, STK)
                    act(Pb[par][0:qr, 0:nk], src, AF.Exp, [srck] + STK, ['Pb%d' % par] + STK, bias=stt_[0:qr, NEGM:NEGM + 1], scale=SCALE, accum=stt_[0:qr, RSUM:RSUM + 1])
                    if first:
                        cp('vector', stt_[0:qr, LRUN:LRUN + 1], stt_[0:qr, RSUM:RSUM + 1], STK, STK)
                    else:
                        act(stt_[0:qr, CORR:CORR + 1], stt_[0:qr, MRUN:MRUN + 1], AF.Exp, STK, STK, bias=stt_[0:qr, NEGM:NEGM + 1], scale=SCALE)
                        stt(stt_[0:qr, LRUN:LRUN + 1], stt_[0:qr, LRUN:LRUN + 1], stt_[0:qr, CORR:CORR + 1], stt_[0:qr, RSUM:RSUM + 1], ALU.mult, ALU.add, STK, STK)
                    cp('vector', stt_[0:qr, MRUN:MRUN + 1], stt_[0:qr, MNEW:MNEW + 1], STK, STK)
                    nsub = nk // 128
                    for i in range(nsub):
                        tr(psb(bpt)[:, i * 128:i * 128 + qr], Pb[par][0:qr, i * 128:(i + 1) * 128], identb[0:qr, 0:qr], ['Pb%d' % par, 'identb'], [PSK[bpt]])
                    cp('scalar', PTs[par][:, 0:nsub, 0:qr], psb(bpt)[:, 0:nsub * 128].rearrange('p (a b) -> p a b', b=128)[:, :, 0:qr], [PSK[bpt]], ['PTs%d' % par])
                    for i in range(nsub):
                        mm(PS[bo][0:qr, 0:512], PTs[par][:, i, 0:qr], pv_list[i], i == 0, i == nsub - 1, ['PTs%d' % par] + PVK, [PSK[bo]])
                    if first:
                        cp('vector', acc[0:qr, :], PS[bo][0:qr, 0:512], [PSK[bo]], [ACK])
                    else:
                        stt(acc[0:qr, :], acc[0:qr, :], stt_[0:qr, CORR:CORR + 1], PS[bo][0:qr, 0:512], ALU.mult, ALU.add, [ACK, PSK[bo]] + STK, [ACK])

                def attn_finish(qr, par):
                    stt_ = stt2[par]
                    acc = acc2[par]
                    STK = ['stat%d' % par]
                    recip(stt_[0:qr, LINV:LINV + 1], stt_[0:qr, LRUN:LRUN + 1], STK, STK)
                    ts('vector', ob_[0:qr, :], acc[0:qr, :], stt_[0:qr, LINV:LINV + 1], None, ALU.mult, None, ['acc%d' % par] + STK, ['ob_'])

                KALL = ['KT%d' % b for b in range(NBLK)] + ['KRT%d' % b for b in range(NBLK)]
                CALL = ['CKV%d' % b for b in range(NBLK)]
                tcount = 0
                NB = 2

                def do_block(ob):
                    nonlocal tcount
                    blk = OWN[ob]
                    smp = blk == 32
                    xb = 0
                    dma('sync', 'xTa%d' % xb, xT[xb][:], xT_d[blk].rearrange('p (k t) -> p k t', t=128), [], ['xTa%d' % xb])
                    dma('sync', 'rq', rq[:], ropeq[ob], [], ['rq'])
                    for k in range(16):
                        mm(PS[6][:, 0:512], xT[xb][:, k, :], wq[:, k, :], k == 0, k == 15, WQK + ['xTa%d' % xb], [PSK[6]])
                    rmsnorm_rows(PS[6][:, 0:512], PSK[6], gq_b[:], 'gq_b', cqn_b[:], 'cqn_b', Ssb[0], ssv, 'Ssb0')
                    for c in range(4):
                        tr(psb(7)[:, c * 128:(c + 1) * 128], cqn_b[:, c * 128:(c + 1) * 128], identb[:], ['cqn_b', 'identb'], [PSK[7]])
                    cp('vector', cqnT[:], psb(7)[:, 0:512].rearrange('p (c t) -> p c t', t=128), [PSK[7]], ['cqnT'])
                    for hh in range(2):
                        pb = 6 + hh
                        for i in range(4):
                            h = 4 * hh + i
                            for c in range(4):
                                mm(PS[pb][:, i * 128:(i + 1) * 128], wuq[:, c, h * 192:h * 192 + 128], cqnT[:, c, :], c == 0, c == 3, ['wuq', 'cqnT'], [PSK[pb]])
                        cp('scalar' if hh == 0 else 'vector', qnT[:, 4 * hh:4 * hh + 4, :], PS[pb][:].rearrange('p (a t) -> p a t', t=128), [PSK[pb]], ['qnT%d' % hh])
                    for h in range(8):
                        pb = 6 + h % 2
                        for c in range(4):
                            mm(PS[pb][0:64, 0:128], wuq[:, c, h * 192 + 128:h * 192 + 192], cqnT[:, c, :], c == 0, c == 3, ['wuq', 'cqnT'], [PSK[pb]])
                        for c in range(4):
                            mm(PS[pb][0:64, 128:256], wuqr[:, c, h, :], cqnT[:, c, :], c == 0, c == 3, ['wuqr0', 'wuqr1', 'cqnT'], [PSK[pb]])
                        tt('vector', tq[:], PS[pb][0:64, 0:256], rq[:], ALU.mult, [PSK[pb], 'rq'], ['tq'])
                        if smp:
                            qrs_ = QRT[:].rearrange('p h t -> p (h t)').rearrange('p (s x) -> p s x', x=64)
                            tt('gpsimd', qrs_[:, :, h * 8:(h + 1) * 8], tq[:, 0:128].rearrange('p (s t) -> p s t', t=8), tq[:, 128:256].rearrange('p (s t) -> p s t', t=8), ALU.add, ['tq'], ['QRT'])
                        else:
                            tt('gpsimd', QRT[:, h, :], tq[:, 0:128], tq[:, 128:256], ALU.add, ['tq'], ['QRT'])
                    for h in range(8):
                        pb = 6 + h % 2
                        for c in range(4):
                            mm(PS[pb][:, c * 128:(c + 1) * 128], wukT[:, h, c * 128:(c + 1) * 128], qnT[:, h, :], True, True, ['wukT', 'qnT0', 'qnT1'], [PSK[pb]])
                        if smp:
                            qls_ = QLT[:].rearrange('p c h t -> p c (h t)').rearrange('p c (s x) -> p c s x', x=64)
                            cp('scalar' if h % 2 == 0 else 'vector', qls_[:, :, :, h * 8:(h + 1) * 8], PS[pb][:].rearrange('p (c s t) -> p c s t', c=4, t=8), [PSK[pb]], ['QLT'])
                        else:
                            cp('scalar' if h % 2 == 0 else 'vector', QLT[:, :, h, :], PS[pb][:].rearrange('p (c t) -> p c t', t=128), [PSK[pb]], ['QLT'])
                    QK = ['QLT', 'QRT']
                    QLs = QLT[:].rearrange('p c h t -> p c (h t)').rearrange('p c (s x) -> p c s x', x=64)
                    QRs = QRT[:].rearrange('p h t -> p (h t)').rearrange('p (s x) -> p s x', x=64)
                    if not smp:
                        j = ob
                        for hp in range(4):
                          for kt in range(j + 1):
                            for par in range(2):
                                h = 2 * hp + par
                                lhs = [QLT[:, c, h, :] for c in range(4)] + [QRT[:, h, :]]
                                rhs = [KT[:, c, kt * 512:(kt + 1) * 512] for c in range(4)] + [KRT[:, kt * 512:(kt + 1) * 512]]
                                if kt == 0 and kt == j:
                                    mk, mkk = mboth[:], 'mboth'
                                elif kt == 0:
                                    mk, mkk = mpad[:], 'mpad'
                                elif kt == j:
                                    mk, mkk = mdiag[:], 'mdiag'
                                else:
                                    mk, mkk = None, None
                                pv = [CKV[:, 4 * kt + i, :] for i in range(4)]
                                attn_tile(128, lhs, rhs, 512, mk, mkk, kt == 0, pv, par, QK + KALL, CALL)
                                tcount += 1
                          for par in range(2):
                            h = 2 * hp + par
                            attn_finish(128, par)
                            for c in range(4):
                                tr(psb(6 + par)[:, c * 128:(c + 1) * 128], ob_[:, c * 128:(c + 1) * 128], identb[:], ['ob_', 'identb'], [PSK[6 + par]])
                            cp('scalar', OLT[:, :, h, :], psb(6 + par)[:, 0:512].rearrange('p (c t) -> p c t', t=128), [PSK[6 + par]], ['OLT'])
                    else:
                        for sp in range(8):
                          for kt in range(17):
                            for par in range(2):
                                s = 2 * sp + par
                                lhs = [QLs[:, c, s, :] for c in range(4)] + [QRs[:, s, :]]
                                if kt < 16:
                                    t = s * 16 + kt
                                    sl = kt % NB
                                    Kx, KRx = Kt[par][sl], KRt[par][sl]
                                    kk, krk = 'Kt%d_%d' % (par, sl), 'KRt%d_%d' % (par, sl)
                                    gather(kk, Kx[:].rearrange('p j f -> p (j f)'), cache_ckv, idx[:, t:t + 1], ['idx'], [kk])
                                    gather(krk, KRx[:].rearrange('p j f -> p (j f)'), cache_kr, idx[:, t:t + 1], ['idx'], [krk])
                                    for jj in range(4):
                                        for c in range(4):
                                            tr(psb(6 + c // 2)[:, (c % 2) * 512 + jj * 128:(c % 2) * 512 + (jj + 1) * 128], Kx[:, jj, c * 128:(c + 1) * 128], identb[:],
                                               [kk, 'identb'], [PSK[6 + c // 2]])
                                    cp('scalar', KTt[par][:, 0:2, :], psb(6)[:, :].rearrange('p (c k) -> p c k', k=512), [PSK[6]], ['KTt%d_0' % par])
                                    cp('vector', KTt[par][:, 2:4, :], psb(7)[:, :].rearrange('p (c k) -> p c k', k=512), [PSK[7]], ['KTt%d_1' % par])
                                    for jj in range(4):
                                        tr(psb(2 + par)[0:64, 512 + jj * 128:512 + (jj + 1) * 128], KRx[:, jj, :], identb[:], [krk, 'identb'], [PSK[2 + par]])
                                    cp('vector', KRTt[par][:, :], psb(2 + par)[0:64, 512:1024], [PSK[2 + par]], ['KRTt%d' % par])
                                    rhs = [KTt[par][:, c, :] for c in range(4)] + [KRTt[par][:, :]]
                                    pv = [Kx[:, jj, :] for jj in range(4)]
                                    attn_tile(64, lhs, rhs, 512, None, None, kt == 0, pv, par, QK + ['KTt%d_0' % par, 'KTt%d_1' % par, 'KRTt%d' % par], [kk])
                                else:
                                    rhs = [KT[:, c, 32 * 128:33 * 128] for c in range(4)] + [KRT[:, 32 * 128:33 * 128]]
                                    pv = [CKV[:, 32, :]]
                                    attn_tile(64, lhs, rhs, 128, msmp[:, s * 128:(s + 1) * 128], 'msmp', False, pv, par, QK + KALL, CALL)
                                tcount += 1
                          for par in range(2):
                            s = 2 * sp + par
                            attn_finish(64, par)
                            for c in range(4):
                                tr(psb(6 + par)[:, c * 64:(c + 1) * 64], ob_[0:64, c * 128:(c + 1) * 128], identb[0:64, 0:64], ['ob_', 'identb'], [PSK[6 + par]])
                            cp('scalar', OLT[:, :, :, s * 8:(s + 1) * 8], psb(6 + par)[:, 0:256].rearrange('p (c h t) -> p c h t', c=4, t=8), [PSK[6 + par]], ['OLT'])
                    for hh in range(2):
                        pb = 6 + hh
                        for i in range(4):
                            h = 4 * hh + i
                            for c in range(4):
                                mm(PS[pb][:, i * 128:(i + 1) * 128], wuv[:, c, h * 128:(h + 1) * 128], OLT[:, c, h, :], c == 0, c == 3, ['wuv', 'OLT'], [PSK[pb]])
                        cp('scalar' if hh == 0 else 'vector', cca[:, 4 * hh:4 * hh + 4, :], PS[pb][:].rearrange('p (a t) -> p a t', t=128), [PSK[pb]], ['cca%d' % hh])
                    dma('sync', 'cca', cc_d[ob].rearrange('p (k t) -> p k t', t=128)[:, 8:16, :], cca[:], ['cca0', 'cca1'], ['cc_d'])

                with ExitStack() as pp:
                    mpad = sbuf(pp, 'mpad', [128, 512], F32)
                    mdiag = sbuf(pp, 'mdiag', [128, 512], F32)
                    mboth = sbuf(pp, 'mboth', [128, 512], F32)
                    dma('sync', 'mpad', mpad[:], mask_pad, [], ['mpad'])
                    dma('sync', 'mdiag', mdiag[:], mask_diag, [], ['mdiag'])
                    tt('vector', mboth[:], mpad[:], mdiag[:], ALU.add, ['mpad', 'mdiag'], ['mboth'])
                    for ob in range(NOWN_RUN if NOWN_RUN < 8 else 8):
                        do_block(ob)
                    S.barrier()
                with ExitStack() as psm:
                    msmp = sbuf(psm, 'msmp', [64, 2048], F32)
                    Kt = [[sbuf(psm, 'Kt%d_%d' % (q_, i), [128, 4, 512], BF16) for i in range(NB)] for q_ in range(2)]
                    KRt = [[sbuf(psm, 'KRt%d_%d' % (q_, i), [128, 4, 64], BF16) for i in range(NB)] for q_ in range(2)]
                    KTt = [sbuf(psm, 'KTt%d' % i, [128, 4, 512], BF16) for i in range(2)]
                    KRTt = [sbuf(psm, 'KRTt%d' % i, [64, 512], BF16) for i in range(2)]
                    pti = sbuf(psm, 'pti', [128, 256], I32)
                    ptf = sbuf(psm, 'ptf', [128, 256], F32)
                    qof = sbuf(psm, 'qof', [128, 1], F32)
                    idx = sbuf(psm, 'idx', [128, 256], I32)
                    dma('sync', 'msmp', msmp[:], mask_smp, [], ['msmp'])
                    dma('sync', 'pti', pti[:], pt_exp, [], ['pti'])
                    dma('sync', 'qof', qof[:], qoff, [], ['qof'])
                    cp('vector', ptf[:], pti[:], ['pti'], ['ptf'])
                    ts('vector', ptf[:], ptf[:], 32.0, qof[:, 0:1], ALU.mult, ALU.add, ['ptf', 'qof'], ['ptf'])
                    cp('vector', idx[:], ptf[:], ['ptf'], ['idx'])
                    if NOWN_RUN >= 9:
                        do_block(8)
                    S.barrier()

        def layer_norm_rows(src, src_key, gb, bb, out, out_key, stats, mv, scope_keys):
            for i in range(4):
                S.op('vector', lambda e, i=i: e.bn_stats(out=stats[:, i, :], in_=src[:, i * 512:(i + 1) * 512]), [src_key], ['lnst'])
            S.op('vector', lambda e: e.bn_aggr(out=mv[:, 0:2], in_=stats[:].rearrange('p a b -> p (a b)')), ['lnst'], ['lnmv'])
            act(mv[:, 2:3], mv[:, 1:2], AF.Sqrt, ['lnmv'], ['lnmv2'], bias=epsq[:, 1:2], scale=1.0)
            recip(mv[:, 3:4], mv[:, 2:3], ['lnmv2'], ['lnmv3'])
            ts('vector', out, src[:], mv[:, 0:1], mv[:, 3:4], ALU.subtract, ALU.mult, [src_key, 'lnmv', 'lnmv3'], [out_key])
            tt('gpsimd', out, out, gb, ALU.mult, [out_key, 'lng'], [out_key])
            tt('gpsimd', out, out, bb, ALU.add, [out_key, 'lnb'], [out_key])

        def phase_out():
          with ExitStack() as ph:
            x1T = sbuf(ph, 'x1T', [128, 16, TOK], BF16)
            stats = sbuf(ph, 'lnstats', [128, 4, 6], F32)
            mv = sbuf(ph, 'lnmv', [128, 4], F32)
            with ExitStack() as p1:
                wo = sbuf(p1, 'wo', [128, 16, 2048], BF16)
                g1 = sbuf(p1, 'g1', [128, 2048], F32)
                b1 = sbuf(p1, 'b1', [128, 2048], F32)
                xa = sbuf(p1, 'xa', [128, 2048], F32)
                pre = sbuf(p1, 'pre', [128, 2048], F32)
                x1f = sbuf(p1, 'x1f', [128, 2048], F32)
                x1b = sbuf(p1, 'x1b', [128, 2048], BF16)
                w_out_v = w_out.rearrange('(k p) n -> p k n', p=128)
                for k in range(16):
                    dma('gpsimd', 'wo', wo[:, k, :], w_out_v[:, k, :], [], ['wo%d' % k])
                WOK = ['wo%d' % k for k in range(16)]
                dma('sync', 'g1', g1[:], ln1_g.partition_broadcast(128), [], ['lng'])
                dma('sync', 'b1', b1[:], ln1_b.partition_broadcast(128), [], ['lnb'])
                concatT = sbuf(p1, 'concatT', [128, 16, TOK], BF16)
                for ob in range(NOWN):
                    dma('sync', 'cc', concatT[:, :, ob * 128:(ob + 1) * 128], cc_d[ob].rearrange('p (k t) -> p k t', t=128), [], ['cc%d' % ob])
                CCK = ['cc%d' % ob for ob in range(NOWN)]
                for ob in range(NOWN):
                    blk = OWN[ob]
                    dma('sync', 'xa', xa[:], xs[blk], [], ['xa'])
                    for n in range(4):
                        for k in range(16):
                            mm(PS[n][:, 0:512], concatT[:, k, ob * 128:(ob + 1) * 128], wo[:, k, n * 512:(n + 1) * 512], k == 0, k == 15, WOK + CCK, [PSK[n]])
                        stt(pre[:, n * 512:(n + 1) * 512], xa[:, n * 512:(n + 1) * 512], ALPHA, PS[n][:, 0:512], ALU.mult, ALU.add, ['xa', PSK[n]], ['pre'])
                    layer_norm_rows(pre, 'pre', g1[:], b1[:], x1f[:], 'x1f', stats, mv, None)
                    dma('sync', 'x1_d', x1_d[ob], x1f[:], ['x1f'], ['x1_d%d' % ob])
                    cp('scalar', x1b[:], x1f[:], ['x1f'], ['x1b'])
                    for g in range(2):
                        pb = 4 + g
                        for i in range(8):
                            k = 8 * g + i
                            tr(psb(pb)[:, i * 128:(i + 1) * 128], x1b[:, k * 128:(k + 1) * 128], identb[:], ['x1b', 'identb'], [PSK[pb]])
                        cp('vector', x1T[:, 8 * g:8 * g + 8, ob * 128:(ob + 1) * 128], psb(pb)[:, :].rearrange('p (a t) -> p a t', t=128), [PSK[pb]], ['x1T'])
                S.barrier()
            HT = sbuf(ph, 'HT', [128, NFF, TOK], BF16)
            NT = [(0, 512), (512, 512), (1024, 128)]
            with ExitStack() as p2:
                wg = [sbuf(p2, 'wg%d' % i, [128, 16, 256], BF16) for i in range(2)]
                wu = [sbuf(p2, 'wu%d' % i, [128, 16, 256], BF16) for i in range(2)]
                sgl = [sbuf(p2, 'sgl%d' % i, [128, 512], F32) for i in range(2)]
                w_gate_v = w_gate.rearrange('(k p) n -> p k n', p=128)
                w_up_v = w_up.rearrange('(k p) n -> p k n', p=128)
                cnt = 0
                for fb in range(NFF // 2):
                    b = fb % 2
                    for k2 in range(2):
                        dma('gpsimd', 'wg%d' % b, wg[b][:, 8 * k2:8 * k2 + 8, :], w_gate_v[:, 8 * k2:8 * k2 + 8, fb * 256:(fb + 1) * 256], [], ['wg%d_%d' % (b, k2)])
                        dma('gpsimd', 'wu%d' % b, wu[b][:, 8 * k2:8 * k2 + 8, :], w_up_v[:, 8 * k2:8 * k2 + 8, fb * 256:(fb + 1) * 256], [], ['wu%d_%d' % (b, k2)])
                    WGK = ['wg%d_0' % b, 'wg%d_1' % b]
                    WUK = ['wu%d_0' % b, 'wu%d_1' % b]
                    for half in range(2):
                        f = 2 * fb + half
                        for (t0, tn) in NT:
                            p = cnt % 2
                            cnt += 1
                            pg, pu = p, 2 + p
                            for k in range(16):
                                mm(PS[pg][:, 0:tn], wg[b][:, k, half * 128:(half + 1) * 128], x1T[:, k, t0:t0 + tn], k == 0, k == 15, WGK + ['x1T'], [PSK[pg]])
                            for k in range(16):
                                mm(PS[pu][:, 0:tn], wu[b][:, k, half * 128:(half + 1) * 128], x1T[:, k, t0:t0 + tn], k == 0, k == 15, WUK + ['x1T'], [PSK[pu]])
                            act(sgl[p][:, 0:tn], PS[pg][:, 0:tn], AF.Silu, [PSK[pg]], ['sgl%d' % p])
                            tt('vector', HT[:, f, t0:t0 + tn], sgl[p][:, 0:tn], PS[pu][:, 0:tn], ALU.mult, ['sgl%d' % p, PSK[pu]], ['HT'])
                S.barrier()
            with ExitStack() as p3:
                wd = [sbuf(p3, 'wd%d' % i, [128, NFF, 256], BF16) for i in range(2)]
                yTf = [sbuf(p3, 'yTf%d' % i, [128, TOK], F32) for i in range(2)]
                ystrip = [sbuf(p3, 'ystrip%d' % i, [128, NOWN, 128], F32) for i in range(2)]
                w_down_v = w_down.rearrange('(f p) n -> p f n', p=128)
                cnt = 0
                for ocb in range(8):
                    b = ocb % 2
                    for f4 in range(4):
                        dma('gpsimd', 'wd%d' % b, wd[b][:, 11 * f4:11 * f4 + 11, :], w_down_v[:, 11 * f4:11 * f4 + 11, ocb * 256:(ocb + 1) * 256], [], ['wd%d_%d' % (b, f4)])
                    WDK = ['wd%d_%d' % (b, f4) for f4 in range(4)]
                    for half in range(2):
                        oc = 2 * ocb + half
                        yb = oc % 2
                        for ti, (t0, tn) in enumerate(NT):
                            pd = (cnt % 2) * 3 + ti
                            for f in range(NFF):
                                mm(PS[pd][:, 0:tn], wd[b][:, f, half * 128:(half + 1) * 128], HT[:, f, t0:t0 + tn], f == 0, f == NFF - 1, WDK + ['HT'], [PSK[pd]])
                            cp('scalar' if ti % 2 == 0 else 'vector', yTf[yb][:, t0:t0 + tn], PS[pd][:, 0:tn], [PSK[pd]], ['yTf%d_%d' % (yb, ti)])
                        cnt += 1
                        YK = ['yTf%d_%d' % (yb, ti) for ti in range(3)]
                        for g in range(3):
                            pb = 6 + g % 2
                            nb_ = 4 if g < 2 else 1
                            for i in range(nb_):
                                ob = 4 * g + i
                                tr(PS[pb][:, i * 128:(i + 1) * 128], yTf[yb][:, ob * 128:(ob + 1) * 128], identf[:], YK + ['identf'], [PSK[pb]])
                            cp('vector' if g % 2 == 0 else 'scalar', ystrip[yb][:, 4 * g:4 * g + nb_, :], PS[pb][:, 0:nb_ * 128].rearrange('p (a t) -> p a t', t=128), [PSK[pb]], ['ystrip%d_%d' % (yb, g)])
                        dma('sync', 'y2s%d' % yb, y2_d[:, :, oc * 128:(oc + 1) * 128].rearrange('b p f -> p b f'), ystrip[yb][:],
                            ['ystrip%d_%d' % (yb, g) for g in range(3)], ['y2_d'])
                S.barrier()
            with ExitStack() as p4:
                g2 = sbuf(p4, 'g2', [128, 2048], F32)
                b2 = sbuf(p4, 'b2', [128, 2048], F32)
                x1r = [sbuf(p4, 'x1r%d' % i, [128, 2048], F32) for i in range(2)]
                y2r = [sbuf(p4, 'y2r%d' % i, [128, 2048], F32) for i in range(2)]
                outt = [sbuf(p4, 'outt%d' % i, [128, 2048], F32) for i in range(2)]
                dma('sync', 'g2', g2[:], ln2_g.partition_broadcast(128), [], ['lng'])
                dma('sync', 'b2', b2[:], ln2_b.partition_broadcast(128), [], ['lnb'])
                for ob in range(NOWN):
                    b = ob % 2
                    dma('sync', 'x1r%d' % b, x1r[b][:], x1_d[ob], [], ['x1r%d' % b])
                    dma('sync', 'y2r%d' % b, y2r[b][:], y2_d[ob], [], ['y2r%d' % b])
                    stt(y2r[b][:], x1r[b][:], ALPHA, y2r[b][:], ALU.mult, ALU.add, ['x1r%d' % b, 'y2r%d' % b], ['y2r%d' % b])
                    layer_norm_rows(y2r[b], 'y2r%d' % b, g2[:], b2[:], outt[b][:], 'outt%d' % b, stats, mv, None)
                    dma('sync', 'yo%d' % b, y_o[ob], outt[b][:], ['outt%d' % b], ['y_o'])
                S.barrier()

        if STAGE >= 1:
            phase_s5()
        if STAGE >= 2:
            phase_attn()
        if STAGE >= 4:
            phase_out()
        S.barrier()
        S.emit()
    return nc


_NC = None


def _rope_tables():
    half = 32
    inv = (10000.0 ** (-2.0 * np.arange(half, dtype=np.float32) / 64.0)).astype(np.float32)
    return inv


def _host_inputs(inputs):
    f32 = np.float32
    x_prompt = np.asarray(inputs['x_prompt'], f32)
    x_sample = np.asarray(inputs['x_sample'], f32)
    page_table = np.asarray(inputs['page_table']).astype(np.int32)
    inv = _rope_tables()
    shared = {
        'w_in': np.asarray(inputs['w_in'], f32)[0],
        'g_q': np.asarray(inputs['g_q'], f32).reshape(1, 512),
        'w_uq': np.asarray(inputs['w_uq'], f32)[0],
        'w_uk': np.asarray(inputs['w_uk'], f32)[0].reshape(512, 1024),
        'g_kv': np.asarray(inputs['g_kv'], f32).reshape(1, 512),
        'w_uv': np.asarray(inputs['w_uv'], f32)[0].reshape(512, 1024),
        'ssm_a_re': np.asarray(inputs['ssm_a_re'], f32)[0],
        'ssm_a_im': np.asarray(inputs['ssm_a_im'], f32)[0],
        'ssm_log_step': np.asarray(inputs['ssm_log_step'], f32).reshape(1, 64),
        'ssm_b_re': np.asarray(inputs['ssm_b_re'], f32)[0],
        'ssm_b_im': np.asarray(inputs['ssm_b_im'], f32)[0],
        'ssm_c_re': np.asarray(inputs['ssm_c_re'], f32)[0],
        'ssm_c_im': np.asarray(inputs['ssm_c_im'], f32)[0],
        'ssm_d': np.asarray(inputs['ssm_d'], f32).reshape(1, 1024),
        'w_glu': np.asarray(inputs['w_glu'], f32)[0],
        'b_glu': np.asarray(inputs['b_glu'], f32).reshape(1, 1024),
        'w_out': np.asarray(inputs['w_out'], f32)[0],
        'ln1_g': np.asarray(inputs['ln1_g'], f32).reshape(1, 2048),
        'ln1_b': np.asarray(inputs['ln1_b'], f32).reshape(1, 2048),
        'w_gate': np.asarray(inputs['w_gate'], f32)[0],
        'w_up': np.asarray(inputs['w_up'], f32)[0],
        'w_down': np.asarray(inputs['w_down'], f32)[0],
        'ln2_g': np.asarray(inputs['ln2_g'], f32).reshape(1, 2048),
        'ln2_b': np.asarray(inputs['ln2_b'], f32).reshape(1, 2048),
        'cache_ckv': np.asarray(inputs['cache_ckv'], f32).reshape(-1, 2048),
        'cache_kr': np.asarray(inputs['cache_krope'], f32).reshape(-1, 256),
        'identf': np.eye(128, dtype=f32),
        'qoff': (np.arange(128) % 32).astype(f32).reshape(128, 1),
    }
    NEG = -30000.0
    md = np.zeros((128, 512), f32)
    md[:, 384:] = np.where(np.arange(128)[None, :] <= np.arange(128)[:, None], 0.0, NEG)
    shared['mask_diag'] = md
    ms = np.full((64, 16, 128), NEG, f32)
    tt_ = np.arange(64) % 8
    for s in range(16):
        for tp in range(8):
            ms[tt_ >= tp, s, s * 8 + tp] = 0.0
    shared['mask_smp'] = ms.reshape(64, 2048)
    st_re = np.asarray(inputs['state_ssm_re'], f32)[0].reshape(128, 4096)
    st_im = np.asarray(inputs['state_ssm_im'], f32)[0].reshape(128, 4096)
    in_maps = []
    for c in range(8):
        b, r = c // 4, c % 4
        xs = np.zeros((NBLK, 128, 2048), f32)
        pos = np.zeros((NBLK, 128), f32)
        mp = np.zeros((128, 512), f32)
        for i in range(32):
            a = i + r - 3
            if a >= 0:
                xs[i] = x_prompt[b, a * 128:(a + 1) * 128]
                pos[i] = a * 128 + np.arange(128)
            elif i < 4:
                mp[:, i * 128:(i + 1) * 128] = NEG
        xs[32] = x_sample[16 * c:16 * c + 16].reshape(128, 2048)
        pos[32] = 8192 + (np.arange(128) % 8)
        ang = pos[:, :, None] * inv[None, None, :]
        cs, sn = np.cos(ang).astype(f32), np.sin(ang).astype(f32)
        ropek = np.concatenate([cs, cs, sn, sn], axis=2).astype(f32)
        rq = np.zeros((NOWN, 64, 256), f32)
        for ob, blk in enumerate(OWN):
            rq[ob, :, 0:128] = np.concatenate([cs[blk], cs[blk]], axis=1).T
            rq[ob, :, 128:256] = np.concatenate([sn[blk], sn[blk]], axis=1).T
        pt = page_table[16 * c:16 * c + 16]
        ptx = pt.reshape(16, 16, 4)[:, :, np.arange(128) // 32]
        ptx = np.ascontiguousarray(ptx.transpose(2, 0, 1).reshape(128, 256)).astype(np.int32)
        m = dict(shared)
        m.update({'xs': xs, 'ropek': ropek, 'ropeq': rq, 'mask_pad': mp, 'pt_exp': ptx,
                  'st_re': np.ascontiguousarray(st_re[16 * c:16 * c + 16]),
                  'st_im': np.ascontiguousarray(st_im[16 * c:16 * c + 16])})
        in_maps.append(m)
    return in_maps


def kernel(**inputs):
    global _NC
    if _NC is None:
        _NC = build()
    in_maps = _host_inputs(inputs)
    if os.environ.get('MK_TRACE'):
        res = run_bass_kernel_spmd(_NC, in_maps[:NCORES], core_ids=list(range(NCORES)), trace=True)
        print('EXEC_TIME_NS', res.exec_time_ns)
    else:
        res = run_bass_kernel_spmd(_NC, in_maps[:NCORES], core_ids=list(range(NCORES)))
    R = list(res.results) + [res.results[0]] * (8 - NCORES)
    f32 = np.float32
    y_p = np.zeros((2, 4096, 2048), f32)
    y_s = np.zeros((128, 8, 2048), f32)
    ckv_p = np.zeros((1, 2, 4096, 512), f32)
    kr_p = np.zeros((1, 2, 4096, 64), f32)
    re_p = np.zeros((1, 2, 64, 64), f32)
    im_p = np.zeros((1, 2, 64, 64), f32)
    ckv_s = np.zeros((1, 128, 8, 512), f32)
    kr_s = np.zeros((1, 128, 8, 64), f32)
    re_s = np.zeros((1, 128, 64, 64), f32)
    im_s = np.zeros((1, 128, 64, 64), f32)
    for c in range(8):
        b, r = c // 4, c % 4
        o = R[c]
        for j in range(8):
            a = 4 * j + r
            y_p[b, a * 128:(a + 1) * 128] = o['y_o'][j]
        y_s[16 * c:16 * c + 16] = o['y_o'][8].reshape(16, 8, 2048)
        ckv_s[0, 16 * c:16 * c + 16] = o['ckv_o'][32].reshape(16, 8, 512)
        kr_s[0, 16 * c:16 * c + 16] = o['kr_o'][32].reshape(16, 8, 64)
        re_s[0, 16 * c:16 * c + 16] = o['sts_o'][0].reshape(16, 64, 64)
        im_s[0, 16 * c:16 * c + 16] = o['sts_o'][1].reshape(16, 64, 64)
        if r == 3:
            ckv_p[0, b] = o['ckv_o'][0:32].reshape(4096, 512)
            kr_p[0, b] = o['kr_o'][0:32].reshape(4096, 64)
            re_p[0, b] = o['stp_o'][0].reshape(64, 64)
            im_p[0, b] = o['stp_o'][1].reshape(64, 64)
    return (y_p, y_s, ckv_p, kr_p, re_p, im_p, ckv_s, kr_s, re_s, im_s)
```

```python
import os
import numpy as np
import concourse.bass as bass
import concourse.mybir as mybir
from concourse.bass_utils import run_bass_kernel_spmd
from contextlib import ExitStack

F32 = mybir.dt.float32
BF16 = mybir.dt.bfloat16
I32 = mybir.dt.int32
AF = mybir.ActivationFunctionType
ALU = mybir.AluOpType
AX = mybir.AxisListType

NBLK = 33
OWN = [4 * j + 3 for j in range(8)] + [32]
NOWN = 9
TOK = NOWN * 128
SCALE = (128 + 64) ** -0.5
ALPHA = 2.0 ** 0.25
DFF = 5632
NFF = DFF // 128
STAGE = int(os.environ.get('MK_STAGE', '99'))
NPOOL = int(os.environ.get('MK_NPOOL', '10240'))
NOWN_RUN = int(os.environ.get('MK_NOWN', '9'))
NCORES = int(os.environ.get('MK_CORES', '8'))
SUB = int(os.environ.get('MK_SUB', '99'))
SUB2 = int(os.environ.get('MK_SUB2', '99'))
NBLK_RUN = int(os.environ.get('MK_NBLK', '33'))


class Sched:
    LAT = float(os.environ.get('MK_LAT', '250'))

    def __init__(self, nc, es):
        self.nc, self.es = nc, es
        self.engs = ['sync', 'scalar', 'vector', 'gpsimd', 'tensor']
        self.q = {e: [] for e in self.engs}
        self.esem = {e: es.enter_context(nc.semaphore('s_' + e)) for e in self.engs[1:]}
        self.ecnt = {e: 0 for e in self.engs}
        self.seen = {e: {} for e in self.engs}
        self.dsem = {}
        self.free = []
        self.nsem = 0
        self.nodes = []
        self.buf = {}
        self.last_on_sem = {}

    def _add(self, node, reads, writes):
        nid = len(self.nodes)
        deps = {}
        for key in reads:
            b = self.buf.get(key)
            if b and b[0] is not None:
                deps[b[0]] = True
        for key in writes:
            b = self.buf.get(key)
            if b:
                if b[0] is not None:
                    deps[b[0]] = True
                for r in b[1]:
                    if r not in deps:
                        deps[r] = False
        if node['kind'] == 'dma':
            prev = self.last_on_sem.get(node['sem'])
            if prev is not None and prev not in deps:
                deps[prev] = None
            self.last_on_sem[node['sem']] = nid
        deps.pop(nid, None)
        node['deps'] = deps
        self.nodes.append(node)
        for key in reads:
            self.buf.setdefault(key, [None, []])[1].append(nid)
        for key in writes:
            self.buf[key] = [nid, []]

    def op(self, eng, fn, reads=(), writes=(), dur=100.0):
        self._add({'kind': 'op', 'eng': eng, 'fn': fn, 'dur': dur}, reads, writes)

    def dma(self, eng, semname, fn, reads=(), writes=(), dur=2500.0):
        self._add({'kind': 'dma', 'eng': eng, 'fn': fn, 'sem': semname, 'dur': dur}, reads, writes)

    def _schedule(self):
        import heapq
        nodes = self.nodes
        n = len(nodes)
        if n == 0:
            return
        succ = [[] for _ in range(n)]
        ndep = [0] * n
        for i, nd in enumerate(nodes):
            ndep[i] = len(nd['deps'])
            for d in nd['deps']:
                succ[d].append(i)
        rank = [0.0] * n
        for i in range(n - 1, -1, -1):
            m = 0.0
            for j in succ[i]:
                if rank[j] > m:
                    m = rank[j]
            rank[i] = nodes[i]['dur'] + m
        pend = {e: [] for e in self.engs}
        avail = {e: [] for e in self.engs}
        ready = [0.0] * n
        fin = [0.0] * n
        start = [0.0] * n
        free_at = {e: 0.0 for e in self.engs}
        for i in range(n):
            if ndep[i] == 0:
                heapq.heappush(pend[nodes[i]['eng']], (0.0, i))
        order = {e: [] for e in self.engs}
        done = 0
        USE_RANK = os.environ.get('MK_RANK', '1') == '1'
        while done < n:
            best = None
            for e in self.engs:
                while pend[e] and pend[e][0][0] <= free_at[e]:
                    r, i = heapq.heappop(pend[e])
                    heapq.heappush(avail[e], ((-rank[i] if USE_RANK else r), i))
                if avail[e]:
                    st, i = free_at[e], avail[e][0][1]
                elif pend[e]:
                    st, i = pend[e][0]
                else:
                    continue
                if best is None or (st, i) < (best[0], best[2]):
                    best = (st, e, i)
            st, e, i = best
            if avail[e] and avail[e][0][1] == i:
                heapq.heappop(avail[e])
            else:
                heapq.heappop(pend[e])
            nd = nodes[i]
            start[i] = st
            if nd['kind'] == 'dma':
                issue = 900.0 if e == 'gpsimd' else 60.0
                free_at[e] = st + issue
                fin[i] = st + issue + nd['dur']
            else:
                free_at[e] = st + nd['dur']
                fin[i] = st + nd['dur']
            order[e].append(i)
            done += 1
            for j in succ[i]:
                nj = nodes[j]
                t = fin[i] + self.LAT
                if nj['deps'][i] is None:
                    t = start[i] + 1.0
                elif nd['kind'] == 'op' and nd['eng'] == nj['eng'] and nj['kind'] == 'op':
                    t = start[i] + 1.0 if (e == 'tensor' or not nj['deps'][i]) else fin[i]
                if t > ready[j]:
                    ready[j] = t
                ndep[j] -= 1
                if ndep[j] == 0:
                    heapq.heappush(pend[nj['eng']], (ready[j], j))
        ev = [None] * n
        for e in self.engs:
            for i in order[e]:
                nd = nodes[i]
                if nd['kind'] == 'dma':
                    nm = nd['sem']
                    if nm not in self.dsem:
                        if self.free:
                            self.dsem[nm] = self.free.pop()
                        else:
                            self.nsem += 1
                            self.dsem[nm] = [self.es.enter_context(self.nc.semaphore('d%d' % self.nsem)), 0]
                    d = self.dsem[nm]
                    d[1] += 16
                    ev[i] = (d[0], d[1], 16)
                else:
                    self.ecnt[e] += 1
                    ev[i] = (self.esem[e], self.ecnt[e], 1)
        for e in self.engs:
            for i in order[e]:
                nd = nodes[i]
                waits = {}
                for d, sync in nd['deps'].items():
                    pd = nodes[d]
                    if sync is None:
                        continue
                    if pd['kind'] == 'op' and pd['eng'] == e:
                        if nd['kind'] == 'op' and (e == 'tensor' or not sync):
                            continue
                    sem, val, _ = ev[d]
                    k = id(sem)
                    if self.seen[e].get(k, 0) >= val:
                        continue
                    if k not in waits or waits[k][1] < val:
                        waits[k] = (sem, val)
                for k, (sem, val) in waits.items():
                    self.seen[e][k] = val
                self.q[e].append((list(waits.values()), nd['fn'], (ev[i][0], ev[i][2])))
        self.nodes = []
        self.buf = {}
        self.last_on_sem = {}

    def barrier(self):
        self._schedule()
        for eng in self.engs:
            waits = []
            for o in self.engs[1:]:
                if o != eng and self.ecnt[o] > 0:
                    sem, val = self.esem[o], self.ecnt[o]
                    if self.seen[eng].get(id(sem), 0) < val:
                        waits.append((sem, val))
                        self.seen[eng][id(sem)] = val
            for d in self.dsem.values():
                if d[1] > 0 and self.seen[eng].get(id(d[0]), 0) < d[1]:
                    waits.append((d[0], d[1]))
                    self.seen[eng][id(d[0])] = d[1]
            self.q[eng].append((waits, None, None))
        self.free.extend(self.dsem.values())
        self.dsem = {}

    def emit(self):
        nc = self.nc
        with nc.Block() as block:
            def mk(ename):
                def body(e):
                    for waits, fn, inc in self.q[ename]:
                        for sem, val in waits:
                            e.wait_ge(sem, val)
                        if fn is not None:
                            ins = fn(e)
                            ins.then_inc(inc[0], inc[1])
                return body
            block.sync(mk('sync'))
            block.scalar(mk('scalar'))
            block.vector(mk('vector'))
            block.gpsimd(mk('gpsimd'))
            block.tensor(mk('tensor'))


def build():
    nc = bass.Bass('TRN2', target_bir_lowering=False)

    def din(name, shape, dt=F32):
        return nc.dram_tensor(name, shape, dt, kind='ExternalInput').ap()

    def dout(name, shape, dt=F32):
        return nc.dram_tensor(name, shape, dt, kind='ExternalOutput').ap()

    def dscr(name, shape, dt):
        return nc.dram_tensor(name, shape, dt, kind='Internal').ap()

    xs = din('xs', [NBLK, 128, 2048])
    w_in = din('w_in', [2048, 2112])
    g_q = din('g_q', [1, 512])
    w_uq = din('w_uq', [512, 1536])
    w_uk = din('w_uk', [512, 1024])
    g_kv = din('g_kv', [1, 512])
    w_uv = din('w_uv', [512, 1024])
    a_re_d = din('ssm_a_re', [64, 64])
    a_im_d = din('ssm_a_im', [64, 64])
    lstep_d = din('ssm_log_step', [1, 64])
    b_re_d = din('ssm_b_re', [64, 64, 16])
    b_im_d = din('ssm_b_im', [64, 64, 16])
    c_re_d = din('ssm_c_re', [64, 16, 64])
    c_im_d = din('ssm_c_im', [64, 16, 64])
    d_d = din('ssm_d', [1, 1024])
    w_glu = din('w_glu', [1024, 1024])
    b_glu = din('b_glu', [1, 1024])
    w_out = din('w_out', [2048, 2048])
    ln1_g = din('ln1_g', [1, 2048])
    ln1_b = din('ln1_b', [1, 2048])
    w_gate = din('w_gate', [2048, DFF])
    w_up = din('w_up', [2048, DFF])
    w_down = din('w_down', [DFF, 2048])
    ln2_g = din('ln2_g', [1, 2048])
    ln2_b = din('ln2_b', [1, 2048])
    cache_ckv = din('cache_ckv', [NPOOL * 32, 2048])
    cache_kr = din('cache_kr', [NPOOL * 32, 256])
    st_re = din('st_re', [16, 4096])
    st_im = din('st_im', [16, 4096])
    pt_exp = din('pt_exp', [128, 256], I32)
    qoff = din('qoff', [128, 1])
    ropek = din('ropek', [NBLK, 128, 128])
    ropeq = din('ropeq', [NOWN, 64, 256])
    mask_pad = din('mask_pad', [128, 512])
    mask_diag = din('mask_diag', [128, 512])
    mask_smp = din('mask_smp', [64, 2048])
    identf_d = din('identf', [128, 128])

    y_o = dout('y_o', [NOWN, 128, 2048])
    ckv_o = dout('ckv_o', [NBLK, 128, 512])
    kr_o = dout('kr_o', [NBLK, 128, 64])
    stp_o = dout('stp_o', [2, 32, 128])
    sts_o = dout('sts_o', [2, 16, 4096])

    xT_d = dscr('xT_d', [NBLK, 128, 2048], BF16)
    x1_d = dscr('x1_d', [NOWN, 128, 2048], F32)
    y2_d = dscr('y2_d', [NOWN, 128, 2048], F32)
    cc_d = dscr('cc_d', [NOWN, 128, 2048], BF16)

    es = ExitStack()
    with es:
        S = Sched(nc, es)

        _sbn = [0]

        def sbuf(scope, name, shape, dt):
            _sbn[0] += 1
            return scope.enter_context(nc.sbuf_tensor('sb%d_%s' % (_sbn[0], name), shape, dt))

        PF = float(os.environ.get('MK_PF', '3.5'))

        def fsz(ap):
            n = 1
            for d in ap.shape[1:]:
                n *= int(d)
            return n

        def mm(out, lhsT, rhs, start, stop, R, W):
            S.op('tensor', lambda e: e.matmul(out, lhsT=lhsT, rhs=rhs, start=start, stop=stop), R, W, dur=max(64, fsz(rhs)) / 2.4 + 30.0)

        def tr(out, in_, ident, R, W):
            S.op('tensor', lambda e: e.transpose(out, in_, ident), R, W, dur=max(64, int(in_.shape[0])) / 2.4 * (2.0 if in_.dtype == F32 else 1.0) + 30.0)

        def act(out, in_, func, R, W, bias=None, scale=None, accum=None):
            kw = {}
            if bias is not None:
                kw['bias'] = bias
            if scale is not None:
                kw['scale'] = scale
            if accum is not None:
                kw['accum_out'] = accum
            S.op('scalar', lambda e: e.activation(out=out, in_=in_, func=func, **kw), R, W, dur=max(64, fsz(in_)) / 1.4 + 180.0)

        def tt(eng, out, in0, in1, op, R, W):
            S.op(eng, lambda e: e.tensor_tensor(out=out, in0=in0, in1=in1, op=op), R, W, dur=max(64, fsz(out)) * (1.05 if eng == 'vector' else PF) + (70.0 if eng == 'vector' else 200.0))

        def ts(eng, out, in0, s1, s2, op0, op1, R, W):
            if op1 is None:
                S.op(eng, lambda e: e.tensor_scalar(out=out, in0=in0, scalar1=s1, scalar2=None, op0=op0), R, W, dur=max(64, fsz(out)) * 1.05 + 70.0)
            else:
                S.op(eng, lambda e: e.tensor_scalar(out=out, in0=in0, scalar1=s1, scalar2=s2, op0=op0, op1=op1), R, W, dur=max(64, fsz(out)) * 1.05 + 70.0)

        def stt(out, in0, scalar, in1, op0, op1, R, W):
            S.op('vector', lambda e: e.scalar_tensor_tensor(out=out, in0=in0, scalar=scalar, in1=in1, op0=op0, op1=op1), R, W, dur=max(64, fsz(out)) * 1.05 + 70.0)

        def red(out, in_, op, R, W):
            S.op('vector', lambda e: e.tensor_reduce(out=out, in_=in_, axis=AX.X, op=op), R, W, dur=max(64, fsz(in_)) * 1.05 + 70.0)

        def cp(eng, out, in_, R, W):
            if eng == 'scalar':
                S.op(eng, lambda e: e.copy(out=out, in_=in_), R, W, dur=max(64, fsz(out)) / 1.4 + 180.0)
            else:
                S.op(eng, lambda e: e.tensor_copy(out=out, in_=in_), R, W, dur=max(64, fsz(out)) * (1.05 if eng == 'vector' else PF) + (70.0 if eng == 'vector' else 200.0))

        def mset(eng, ap, val, W):
            S.op(eng, lambda e: e.memset(ap, val), (), W)

        def recip(out, in_, R, W):
            S.op('vector', lambda e: e.reciprocal(out=out, in_=in_), R, W)

        def scan(out, d0, d1, R, W):
            S.op('vector', lambda e: e.tensor_tensor_scan(out=out, data0=d0, data1=d1, initial=0.0, op0=ALU.mult, op1=ALU.add), R, W, dur=2.1 * fsz(out) + 70.0)

        def dma(eng, sem, out, in_, R, W, **kw):
            S.dma(eng, sem, lambda e: e.dma_start(out=out, in_=in_, **kw), R, W)

        def gather(sem, out, in_, idx, R, W):
            S.dma('gpsimd', sem, lambda e: e.indirect_dma_start(out=out, out_offset=None, in_=in_, in_offset=bass.IndirectOffsetOnAxis(ap=idx, axis=0)), R, W)

        G = es
        PS = [G.enter_context(nc.psum_tensor('ps%d' % i, [128, 512], F32)) for i in range(8)]
        PSK = ['PS%d' % i for i in range(8)]

        def psb(i):
            return PS[i][:].bitcast(BF16)

        identf = sbuf(G, 'identf', [128, 128], F32)
        identb = sbuf(G, 'identb', [128, 128], BF16)
        dma('sync', 'identf', identf[:], identf_d, [], ['identf'])
        cp('vector', identb[:], identf[:], ['identf'], ['identb'])

        def rmsnorm_rows(src_ps, src_key, gb, gb_key, out_ap, out_key, sq, ssv, eng_scratch_keys):
            jk = eng_scratch_keys or 'sq'
            act(sq[:], src_ps, AF.Square, [src_key], ['sq', jk], accum=ssv[:, 0:1])
            act(ssv[:, 1:2], ssv[:, 0:1], AF.Sqrt, ['sq'], ['ssv'], bias=epsq[:, 0:1], scale=1.0 / 512.0)
            recip(ssv[:, 2:3], ssv[:, 1:2], ['ssv'], ['ssv2'])
            stt(out_ap, src_ps, ssv[:, 2:3], gb, ALU.mult, ALU.mult, [src_key, 'ssv2', gb_key], [out_key])

        epsq = sbuf(G, 'epsq', [128, 2], F32)
        mset('vector', epsq[:, 0:1], 1e-6, ['epsq'])
        mset('vector', epsq[:, 1:2], 1e-5, ['epsq'])

        with ExitStack() as ph:
            xa = [sbuf(ph, 'xa%d' % i, [128, 2048], F32) for i in range(2)]
            xt = [sbuf(ph, 'xt%d' % i, [128, 16, 128], BF16) for i in range(2)]
            for blk in range(NBLK):
                b = blk % 2
                dma('sync', 'xa%d' % b, xa[b][:], xs[blk], [], ['xa%d' % b])
                for g in range(4):
                    pb = g % 2
                    for i in range(4):
                        k = 4 * g + i
                        tr(PS[pb][:, i * 128:(i + 1) * 128], xa[b][:, k * 128:(k + 1) * 128], identf[:],
                           ['xa%d' % b, 'identf'], [PSK[pb]])
                    cp('vector' if g % 2 == 0 else 'scalar', xt[b][:, 4 * g:4 * g + 4, :],
                       PS[pb][:].rearrange('p (a t) -> p a t', t=128), [PSK[pb]], ['xt%d_%d' % (b, g)])
                dma('sync', 'st_xt%d' % b, xT_d[blk].rearrange('p (k t) -> p k t', t=128), xt[b][:],
                    ['xt%d_%d' % (b, g) for g in range(4)], ['xT_d%d' % blk])
            S.barrier()


        def phase_s5():
          with ExitStack() as ph:
            TWO_PI = 6.283185307179586
            prm32 = sbuf(ph, 'prm32', [32, 3, 128], F32)
            lst2 = sbuf(ph, 'lst2', [32, 2], F32)
            prm = sbuf(ph, 'prm', [128, 3, 32], F32)
            sm = sbuf(ph, 'sm', [128, 40, 32], F32)
            smi = sbuf(ph, 'smi', [128, 32], I32)
            SMK = ['sm']

            def st(i):
                return sm[:, i, :]
            dma('sync', 'prm', prm32[:, 0, :], a_re_d.rearrange('(c g) p -> c (g p)', g=2), [], ['prm32a'])
            dma('sync', 'prm', prm32[:, 1, :], a_im_d.rearrange('(c g) p -> c (g p)', g=2), [], ['prm32b'])
            dma('sync', 'prm', lst2[:], lstep_d.rearrange('o (c g) -> (o c) g', g=2), [], ['lst2'])
            for g2 in range(2):
                cp('vector', prm32[:, 2, g2 * 64:(g2 + 1) * 64], lst2[:, g2:g2 + 1].to_broadcast([32, 64]), ['lst2'], ['prm32c%d' % g2])
            for i in range(3):
                tr(PS[0][:, i * 32:(i + 1) * 32], prm32[:, i, :], identf[0:32, 0:32],
                   ['prm32a', 'prm32b', 'prm32c0', 'prm32c1', 'identf'], [PSK[0]])
            cp('vector', prm[:], PS[0][:, 0:96].rearrange('p (a c) -> p a c', c=32), [PSK[0]], SMK)
            lam_r, lam_i, lst = prm[:, 0, :], prm[:, 1, :], prm[:, 2, :]

            def v2(out, a, b, op):
                tt('vector', out, a, b, op, SMK, SMK)

            def v1(out, a, s1, s2, op0, op1=None):
                ts('vector', out, a, s1, s2, op0, op1, SMK, SMK)

            def a1(out, a, func, scale=None, bias=None):
                act(out, a, func, SMK, SMK, bias=bias, scale=scale)
            DLT, XR, TH, MAG, MAGI, RR, KF, FF, TMP, SIN, GG, COS, AR, AI, IR, II, KR_, KI_, NUMR, DEN, T5, T6 = range(22)
            A128R, A128I, A127R, A127I, CURR, CURI = 22, 23, 24, 25, 26, 27
            a1(st(DLT), lst, AF.Exp)
            v2(st(XR), lam_r, st(DLT), ALU.mult)
            v2(st(TH), lam_i, st(DLT), ALU.mult)
            a1(st(MAG), st(XR), AF.Exp)
            a1(st(MAGI), st(XR), AF.Exp, scale=-1.0)
            v1(st(RR), st(TH), 1.0 / TWO_PI, None, ALU.mult)
            cp('vector', smi[:], st(RR), SMK, SMK)
            cp('vector', st(KF), smi[:], SMK, SMK)
            v2(st(FF), st(RR), st(KF), ALU.subtract)
            v1(st(TMP), st(FF), 0.5, None, ALU.is_gt)
            v2(st(FF), st(FF), st(TMP), ALU.subtract)
            v1(st(TMP), st(FF), -0.5, None, ALU.is_lt)
            v2(st(FF), st(FF), st(TMP), ALU.add)
            a1(st(SIN), st(FF), AF.Sin, scale=TWO_PI)
            v1(st(GG), st(FF), 0.25, None, ALU.add)
            v1(st(TMP), st(GG), 0.5, None, ALU.is_gt)
            v2(st(GG), st(GG), st(TMP), ALU.subtract)
            a1(st(COS), st(GG), AF.Sin, scale=TWO_PI)
            v2(st(AR), st(MAG), st(COS), ALU.mult)
            v2(st(AI), st(MAG), st(SIN), ALU.mult)
            v2(st(IR), st(MAGI), st(COS), ALU.mult)
            v2(st(II), st(MAGI), st(SIN), ALU.mult)
            v1(st(II), st(II), -1.0, None, ALU.mult)
            v1(st(NUMR), st(AR), -1.0, None, ALU.add)
            v2(st(DEN), lam_r, lam_r, ALU.mult)
            v2(st(T5), lam_i, lam_i, ALU.mult)
            v2(st(DEN), st(DEN), st(T5), ALU.add)
            recip(st(DEN), st(DEN), SMK, SMK)
            v2(st(T5), st(NUMR), lam_r, ALU.mult)
            v2(st(T6), st(AI), lam_i, ALU.mult)
            v2(st(T5), st(T5), st(T6), ALU.add)
            v2(st(KR_), st(T5), st(DEN), ALU.mult)
            v2(st(T5), st(AI), lam_r, ALU.mult)
            v2(st(T6), st(NUMR), lam_i, ALU.mult)
            v2(st(T5), st(T5), st(T6), ALU.subtract)
            v2(st(KI_), st(T5), st(DEN), ALU.mult)

            if SUB < 1:
                S.barrier()
                return
            BLF = sbuf(ph, 'BLF', [128, 32, 2, 128], BF16)
            ApowT = sbuf(ph, 'ApowT', [128, 2, 32, 128], BF16)
            Bz = sbuf(ph, 'Bz', [128, 2, 32, 32], F32)
            u_tm = sbuf(ph, 'u_tm', [128, 1024], BF16)
            CL3 = sbuf(ph, 'CL3', [128, 8, 2, 64], BF16)
            CLr = sbuf(ph, 'CLr', [128, 32, 32], BF16)
            CLn = sbuf(ph, 'CLn', [128, 32, 32], BF16)
            Tp_r = sbuf(ph, 'Tp_r', [128, 32, 128], BF16)
            Tp_i = sbuf(ph, 'Tp_i', [128, 32, 128], BF16)
            Tn_r = sbuf(ph, 'Tn_r', [128, 32, 128], BF16)
            Tn_i = sbuf(ph, 'Tn_i', [128, 32, 128], BF16)
            with ExitStack() as p2:
                Bn_r = sbuf(p2, 'Bn_r', [128, 32, 16], F32)
                Bn_i = sbuf(p2, 'Bn_i', [128, 32, 16], F32)
                Bb = sbuf(p2, 'Bb', [128, 2, 32, 16], F32)
                Bt = sbuf(p2, 'Bt', [128, 32, 16], F32)
                Cz = sbuf(p2, 'Cz', [128, 2, 8, 128], F32)
                for g2 in range(2):
                    dma('sync', 'Bn', Bn_r[g2 * 64:(g2 + 1) * 64, :, :], b_re_d.rearrange('(c g) p h -> g p c h', g=2)[g2], [], ['Bn_r%d' % g2])
                    dma('sync', 'Bn', Bn_i[g2 * 64:(g2 + 1) * 64, :, :], b_im_d.rearrange('(c g) p h -> g p c h', g=2)[g2], [], ['Bn_i%d' % g2])
                BNK = ['Bn_r0', 'Bn_r1', 'Bn_i0', 'Bn_i1']
                kr_b = st(KR_).unsqueeze(2).to_broadcast([128, 32, 16])
                ki_b = st(KI_).unsqueeze(2).to_broadcast([128, 32, 16])
                tt('vector', Bb[:, 0], Bn_r[:], kr_b, ALU.mult, BNK + SMK, ['Bb'])
                tt('vector', Bt[:], Bn_i[:], ki_b, ALU.mult, BNK + SMK, ['Bt'])
                tt('vector', Bb[:, 0], Bb[:, 0], Bt[:], ALU.subtract, ['Bb', 'Bt'], ['Bb'])
                tt('vector', Bb[:, 1], Bn_i[:], kr_b, ALU.mult, BNK + SMK, ['Bb'])
                tt('vector', Bt[:], Bn_r[:], ki_b, ALU.mult, BNK + SMK + ['Bb'], ['Bt'])
                tt('vector', Bb[:, 1], Bb[:, 1], Bt[:], ALU.add, ['Bb', 'Bt'], ['Bb'])
                mset('vector', Bz[:], 0.0, ['Bz'])
                for ri in range(2):
                    cp('vector', Bz[0:64, ri, :, 0:16], Bb[0:64, ri], ['Bb', 'Bz'], ['Bz'])
                    cp('vector', Bz[64:128, ri, :, 16:32], Bb[64:128, ri], ['Bb', 'Bz'], ['Bz'])
                BzF = sbuf(p2, 'BzF', [128, 32, 128], F32)
                for ri in range(2):
                    mset('vector', BzF[:], 0.0, ['BzF'])
                    bzf_v = BzF[:].rearrange('p (q a) (b h) -> p q a b h', a=4, h=32)
                    bz_v = Bz[:, ri].rearrange('p (q a) h -> p q a h', a=4)
                    for cl in range(4):
                        cp('vector', bzf_v[:, :, cl, cl, :], bz_v[:, :, cl, :], ['Bz', 'BzF'], ['BzF'])
                    for c in range(32):
                        pb = c % 2
                        tr(PS[pb][:, 0:128], BzF[:, c, :], identf[:], ['BzF', 'identf'], [PSK[pb]])
                        cp('vector' if c % 2 == 0 else 'scalar', BLF[:, c, ri, :], PS[pb][:, 0:128], [PSK[pb]], ['BL'])
                mset('vector', CL3[:], 0.0, ['CL3'])
                if SUB < 2:
                    S.barrier()
                    return
                mset('vector', Cz[:], 0.0, ['Cz'])
                for ri, cd in enumerate((c_re_d, c_im_d)):
                    for cl in range(4):
                        for g2 in range(2):
                            r0 = 32 * cl + 16 * g2
                            dma('sync', 'Cz', Cz[r0:r0 + 16, ri, :, 64 * g2:64 * g2 + 64],
                                cd.rearrange('(q r) h p -> r h q p', r=8)[2 * cl + g2], ['Cz'], ['Cz_%d_%d_%d' % (ri, cl, g2)])
                CZK = ['Cz_%d_%d_%d' % (ri, cl, g2) for ri in range(2) for cl in range(4) for g2 in range(2)]
                for ri in range(2):
                    for q in range(8):
                        pb = q % 2
                        tr(PS[pb][:, 0:128], Cz[:, ri, q, :], identf[:], CZK + ['identf'], [PSK[pb]])
                        src = PS[pb][:, 0:128].rearrange('p (a b) -> p a b', b=32)
                        if ri == 0:
                            cp('vector', CLr[:, 4 * q:4 * q + 4, :], src, [PSK[pb]], ['CLr'])
                            cp('vector', CL3[:, q, 0, 32:64], src[:, 3, :], [PSK[pb], 'CL3'], ['CL3'])
                        else:
                            S.op('scalar', lambda e, o=CLn[:, 4 * q:4 * q + 4, :], s=src: e.mul(o, s, -1.0), [PSK[pb]], ['CLn'])
                            S.op('scalar', lambda e, o=CL3[:, q, 1, 32:64], s=src[:, 3, :]: e.mul(o, s, -1.0), [PSK[pb], 'CL3'], ['CL3'])
                if SUB < 3:
                    S.barrier()
                    return
                TFr = sbuf(p2, 'TFr', [128, 32, 128], F32)
                TFi = sbuf(p2, 'TFi', [128, 32, 128], F32)
                tm1 = sbuf(p2, 'tm1', [128, 32, 64], F32)
                tm2 = sbuf(p2, 'tm2', [128, 32, 64], F32)
                TK = ['tab']
                for which in range(2):
                    br, bi = (AR, AI) if which == 0 else (IR, II)
                    cp('vector', st(CURR), st(br), SMK, SMK)
                    cp('vector', st(CURI), st(bi), SMK, SMK)
                    mset('vector', TFr[:, :, 0:1], 1.0, TK)
                    mset('vector', TFi[:, :, 0:1], 0.0, TK)
                    for k in range(7):
                        n = 1 << k
                        cr = st(CURR).unsqueeze(2).to_broadcast([128, 32, n])
                        ci = st(CURI).unsqueeze(2).to_broadcast([128, 32, n])
                        tt('vector', tm1[:, :, 0:n], TFr[:, :, 0:n], cr, ALU.mult, TK + SMK, TK)
                        tt('vector', tm2[:, :, 0:n], TFi[:, :, 0:n], ci, ALU.mult, TK + SMK, TK)
                        tt('vector', TFr[:, :, n:2 * n], tm1[:, :, 0:n], tm2[:, :, 0:n], ALU.subtract, TK, TK)
                        tt('vector', tm1[:, :, 0:n], TFr[:, :, 0:n], ci, ALU.mult, TK + SMK, TK)
                        tt('vector', tm2[:, :, 0:n], TFi[:, :, 0:n], cr, ALU.mult, TK + SMK, TK)
                        tt('vector', TFi[:, :, n:2 * n], tm1[:, :, 0:n], tm2[:, :, 0:n], ALU.add, TK, TK)
                        v2(st(T5), st(CURR), st(CURR), ALU.mult)
                        v2(st(T6), st(CURI), st(CURI), ALU.mult)
                        v2(st(TMP), st(CURR), st(CURI), ALU.mult)
                        v2(st(CURR), st(T5), st(T6), ALU.subtract)
                        v1(st(CURI), st(TMP), 2.0, None, ALU.mult)
                    if which == 0:
                        cp('vector', st(A128R), st(CURR), SMK, SMK)
                        cp('vector', st(A128I), st(CURI), SMK, SMK)
                        cp('vector', st(A127R), TFr[:, :, 127], TK + SMK, SMK)
                        cp('vector', st(A127I), TFi[:, :, 127], TK + SMK, SMK)
                        cp('vector', Tp_r[:], TFr[:], TK, ['Tp'])
                        cp('vector', Tp_i[:], TFi[:], TK, ['Tp'])
                    else:
                        cp('vector', Tn_r[:], TFr[:], TK, ['Tn'])
                        cp('vector', Tn_i[:], TFi[:], TK, ['Tn'])
                        TRb = sbuf(p2, 'TRb', [128, 2, 32, 128], BF16)
                        a7r = st(A127R).unsqueeze(2).to_broadcast([128, 32, 128])
                        a7i = st(A127I).unsqueeze(2).to_broadcast([128, 32, 128])
                        for hh_ in range(2):
                            sl_ = slice(hh_ * 64, (hh_ + 1) * 64)
                            a7r_ = st(A127R).unsqueeze(2).to_broadcast([128, 32, 64])
                            a7i_ = st(A127I).unsqueeze(2).to_broadcast([128, 32, 64])
                            tt('vector', tm1[:], TFr[:, :, sl_], a7r_, ALU.mult, TK + SMK, TK)
                            tt('vector', tm2[:], TFi[:, :, sl_], a7i_, ALU.mult, TK + SMK, TK)
                            tt('vector', TRb[:, 0, :, sl_], tm1[:], tm2[:], ALU.subtract, TK, ['TRb'])
                            tt('vector', tm1[:], TFr[:, :, sl_], a7i_, ALU.mult, TK + SMK + ['TRb'], TK)
                            tt('vector', tm2[:], TFi[:, :, sl_], a7r_, ALU.mult, TK + SMK, TK)
                            tt('vector', TRb[:, 1, :, sl_], tm1[:], tm2[:], ALU.add, TK, ['TRb'])
                        for ri in range(2):
                            for c8 in range(4):
                                pb = c8 % 2
                                for i in range(8):
                                    c = c8 * 8 + i
                                    tr(psb(pb)[:, i * 128:(i + 1) * 128], TRb[:, ri, c, :], identb[:], ['TRb', 'identb'], [PSK[pb]])
                                cp('vector' if c8 % 2 == 0 else 'scalar', ApowT[:, ri, c8 * 8:(c8 + 1) * 8, :], psb(pb)[:, :].rearrange('p (a b) -> p a b', b=128), [PSK[pb]], ['ApowT'])
                S.barrier()

            if SUB < 4:
                S.barrier()
                return
            wB = sbuf(ph, 'wB', [128, 16, 1024], BF16)
            wgl = sbuf(ph, 'wgl', [128, 8, 1024], BF16)
            Dt = sbuf(ph, 'Dt', [128, 8], F32)
            bg = sbuf(ph, 'bg', [128, 8], F32)
            w_in_v = w_in.rearrange('(k p) n -> p k n', p=128)
            for k4 in range(4):
                dma('gpsimd', 'wB', wB[:, 4 * k4:4 * k4 + 4, :], w_in_v[:, 4 * k4:4 * k4 + 4, 0:1024], [], ['wB%d' % k4])
            WBK = ['wB%d' % k4 for k4 in range(4)]
            w_glu_v = w_glu.rearrange('(k p) n -> p k n', p=128)
            for k4 in range(2):
                dma('gpsimd', 'wgl', wgl[:, 4 * k4:4 * k4 + 4, :], w_glu_v[:, 4 * k4:4 * k4 + 4, :], [], ['wgl%d' % k4])
            WGK = ['wgl0', 'wgl1']
            dma('sync', 'Dt', Dt[:], d_d.rearrange('o (k p) -> p (o k)', p=128), [], ['Dt'], allow_slow_non_contiguous=True)
            dma('sync', 'bg', bg[:], b_glu.rearrange('o (k p) -> p (o k)', p=128), [], ['bg'], allow_slow_non_contiguous=True)

            xT = [sbuf(ph, 'xT%d' % i, [128, 16, 128], BF16) for i in range(2)]
            uT_f = sbuf(ph, 'uT_f', [128, 8, 128], F32)
            uT_b = sbuf(ph, 'uT_b', [128, 8, 128], BF16)
            gT = sbuf(ph, 'gT', [128, 8, 128], BF16)
            tmp = [[sbuf(ph, 't%d_%d' % (p, i), [128, 512], F32) for i in range(4)] for p in range(2)]
            z_r = [sbuf(ph, 'z_r%d' % p, [128, 512], F32) for p in range(2)]
            z_i = [sbuf(ph, 'z_i%d' % p, [128, 512], F32) for p in range(2)]
            G_r = [sbuf(ph, 'G_r%d' % p, [128, 512], F32) for p in range(1)]
            G_i = [sbuf(ph, 'G_i%d' % p, [128, 512], F32) for p in range(1)]
            h_r = [sbuf(ph, 'h_r%d' % p, [128, 512], F32) for p in range(1)]
            h_i = [sbuf(ph, 'h_i%d' % p, [128, 512], F32) for p in range(1)]
            hb_r = [sbuf(ph, 'hb_r%d' % p, [128, 4, 128], BF16) for p in range(1)]
            hb_i = [sbuf(ph, 'hb_i%d' % p, [128, 4, 128], BF16) for p in range(1)]
            yv = [sbuf(ph, 'yv%d' % p, [128, 128], F32) for p in range(2)]
            ge = [[sbuf(ph, 'ge%d_%d' % (p, i), [128, 128], F32) for i in range(2)] for p in range(2)]
            sg = [sbuf(ph, 'sg%d' % p, [128, 128], F32) for p in range(1)]
            H_r = sbuf(ph, 'H_r', [128, 32, 16], F32)
            H_i = sbuf(ph, 'H_i', [128, 32, 16], F32)
            CA_r = sbuf(ph, 'CA_r', [128, 32, 16], F32)
            CA_i = sbuf(ph, 'CA_i', [128, 32, 16], F32)
            ctm = sbuf(ph, 'ctm', [128, 32, 16], F32)
            Ss_r = sbuf(ph, 'Ss_r', [128, 32], F32)
            Ss_i = sbuf(ph, 'Ss_i', [128, 32], F32)
            ccs = sbuf(ph, 'ccs', [128, 8, 128], BF16)
            smask_p = sbuf(ph, 'smask_p', [128, 512], BF16)
            smask_s = sbuf(ph, 'smask_s', [128, 512], BF16)
            s0 = sbuf(ph, 's0', [16, 2, 512], F32)
            mset('vector', smask_p[:], 1.0, ['smask'])
            mset('vector', smask_p[:].rearrange('p (c t) -> p c t', t=128)[:, :, 0:1], 0.0, ['smask'])
            mset('vector', smask_s[:], 1.0, ['smask'])
            mset('vector', smask_s[:].rearrange('p (s t) -> p s t', t=8)[:, :, 0:1], 0.0, ['smask'])
            mset('vector', H_r[:], 0.0, ['H'])
            mset('vector', H_i[:], 0.0, ['H'])
            HK = ['H']

            def cmul_b(o_r, o_i, x_r, x_i, y_r, y_i, t, R, W):
                tt('vector', o_r, x_r, y_r, ALU.mult, R, W)
                tt('vector', t, x_i, y_i, ALU.mult, R, ['ctm'])
                tt('vector', o_r, o_r, t, ALU.subtract, W + ['ctm'], W)
                tt('vector', o_i, x_r, y_i, ALU.mult, R, W)
                tt('vector', t, x_i, y_r, ALU.mult, R + W, ['ctm'])
                tt('vector', o_i, o_i, t, ALU.add, W + ['ctm'], W)

            for blk in range(NBLK_RUN):
                own = blk in OWN
                ob = OWN.index(blk) if own else -1
                smp = blk == 32
                ns, tlen = (16, 8) if smp else (1, 128)
                smask = smask_s if smp else smask_p
                xb = blk % 2
                dma('sync', 'xT%d' % xb, xT[xb][:], xT_d[blk].rearrange('p (k t) -> p k t', t=128), ['xT_d%d' % blk], ['xT%d' % xb])
                if smp:
                    for qtr in range(8):
                        dma('sync', 's0r', s0[:, 0, :], st_re[:, qtr * 512:(qtr + 1) * 512], [], ['s0r'])
                        dma('sync', 's0i', s0[:, 1, :], st_im[:, qtr * 512:(qtr + 1) * 512], [], ['s0i'])
                        for ri in range(2):
                            for c8 in range(4):
                                c = qtr * 4 + c8
                                tr(PS[6 + ri][:, c * 16:(c + 1) * 16], s0[:, ri, c8 * 128:(c8 + 1) * 128], identf[0:16, 0:16], ['s0r', 's0i', 'identf'], [PSK[6 + ri]])
                    for ri, Hx in enumerate((H_r, H_i)):
                        cp('vector', Hx[:], PS[6 + ri][:].rearrange('p (c s) -> p c s', s=16), [PSK[6 + ri]], HK)
                if own:
                    arb = st(AR).unsqueeze(2).to_broadcast([128, 32, ns])
                    aib = st(AI).unsqueeze(2).to_broadcast([128, 32, ns])
                    cmul_b(CA_r[:, :, 0:ns], CA_i[:, :, 0:ns], H_r[:, :, 0:ns], H_i[:, :, 0:ns], arb, aib, ctm[:, :, 0:ns], HK + SMK, ['CA'])
                if not own:
                    for n_ in range(2):
                        for k in range(16):
                            mm(PS[n_][:, 0:512], xT[xb][:, k, :], wB[:, k, n_ * 512:(n_ + 1) * 512], k == 0, k == 15, WBK + ['xT%d' % xb], [PSK[n_]])
                        cp('scalar', u_tm[:, n_ * 512:(n_ + 1) * 512], PS[n_][:, 0:512], [PSK[n_]], ['u_tm%d' % n_])
                    for hf in range(2):
                        b_re, b_im = (2, 3) if hf == 0 else (4, 5)
                        for cc in range(16):
                            c = hf * 16 + cc
                            mm(PS[b_re][:, cc * 32:(cc + 1) * 32], ApowT[:, 0, c, :], u_tm[:, c * 32:(c + 1) * 32], True, True, ['ApowT', 'u_tm0', 'u_tm1'], [PSK[b_re]])
                            mm(PS[b_im][:, cc * 32:(cc + 1) * 32], ApowT[:, 1, c, :], u_tm[:, c * 32:(c + 1) * 32], True, True, ['ApowT', 'u_tm0', 'u_tm1'], [PSK[b_im]])
                        vr = PS[b_re][:, 0:512].rearrange('p (c h) -> p c h', h=32)
                        vi = PS[b_im][:, 0:512].rearrange('p (c h) -> p c h', h=32)
                        bzr = Bz[:, 0, hf * 16:(hf + 1) * 16, :]
                        bzi = Bz[:, 1, hf * 16:(hf + 1) * 16, :]
                        t1, t2, t3, t4 = [tmp[hf][i][:].rearrange('p (c h) -> p c h', h=32) for i in range(4)]
                        tk = ['t%d_%d' % (hf, i) for i in range(4)]
                        tt('vector', t1, vr, bzr, ALU.mult, [PSK[b_re], 'Bz'], [tk[0]])
                        tt('vector', t2, vi, bzi, ALU.mult, [PSK[b_im], 'Bz'], [tk[1]])
                        tt('gpsimd', t1, t1, t2, ALU.subtract, [tk[0], tk[1]], [tk[0]])
                        red(Ss_r[:, hf * 16:(hf + 1) * 16], t1, ALU.add, [tk[0]], ['Ss'])
                        tt('vector', t3, vi, bzr, ALU.mult, [PSK[b_im], 'Bz'], [tk[2]])
                        tt('vector', t4, vr, bzi, ALU.mult, [PSK[b_re], 'Bz'], [tk[3]])
                        tt('gpsimd', t3, t3, t4, ALU.add, [tk[2], tk[3]], [tk[2]])
                        red(Ss_i[:, hf * 16:(hf + 1) * 16], t3, ALU.add, [tk[2]], ['Ss'])
                    Hr0, Hi0 = H_r[:, :, 0], H_i[:, :, 0]
                    cmul_b(st(T5), st(T6), Hr0, Hi0, st(A128R), st(A128I), st(TMP), HK + SMK, SMK)
                    tt('vector', Hr0, st(T5), Ss_r[:], ALU.add, SMK + ['Ss'], HK)
                    tt('vector', Hi0, st(T6), Ss_i[:], ALU.add, SMK + ['Ss'], HK)
                    continue
                for m in range(8):
                    bank, slot = m // 4, m % 4
                    for k in range(16):
                        mm(PS[bank][:, slot * 128:(slot + 1) * 128], wB[:, k, m * 128:(m + 1) * 128], xT[xb][:, k, :], k == 0, k == 15,
                           WBK + ['xT%d' % xb], [PSK[bank]])
                cp('scalar', uT_f[:, 0:4, :], PS[0][:].rearrange('p (a t) -> p a t', t=128), [PSK[0]], ['uT_f0'])
                cp('vector', uT_f[:, 4:8, :], PS[1][:].rearrange('p (a t) -> p a t', t=128), [PSK[1]], ['uT_f1'])
                cp('gpsimd', uT_b[:], uT_f[:], ['uT_f0', 'uT_f1'], ['uT_b'])
                if SUB2 < 1:
                    continue
                for o in range(8):
                    p = o % 2
                    br_, bi_ = (2, 3) if o % 2 == 0 else (4, 5)
                    for cl in range(4):
                        l0, l1, rr_ = BLF[:, 4 * o + cl, 0, :], BLF[:, 4 * o + cl, 1, :], uT_b[:, o, :]
                        mm(PS[br_][:, cl * 128:(cl + 1) * 128], l0, rr_, True, True, ['BL', 'uT_b'], [PSK[br_]])
                        mm(PS[bi_][:, cl * 128:(cl + 1) * 128], l1, rr_, True, True, ['BL', 'uT_b'], [PSK[bi_]])

                    def v4(ap):
                        return ap.rearrange('p (c s t) -> p c s t', c=4, t=tlen)

                    def tb(T):
                        return T[:, 4 * o:4 * o + 4, 0:tlen].unsqueeze(2).to_broadcast([128, 4, ns, tlen])
                    if SUB2 < 2:
                        continue
                    pre, pim = v4(PS[br_][:]), v4(PS[bi_][:])
                    t1, t2, t3, t4 = [v4(tmp[p][i][:]) for i in range(4)]
                    tk = ['t%d_%d' % (p, i) for i in range(4)]
                    tt('vector', t1, pre, tb(Tn_r), ALU.mult, [PSK[br_], 'Tn'], [tk[0]])
                    tt('vector', t2, pim, tb(Tn_i), ALU.mult, [PSK[bi_], 'Tn'], [tk[1]])
                    if SUB2 < 3:
                        continue
                    tt('gpsimd', v4(z_r[p][:]), t1, t2, ALU.subtract, [tk[0], tk[1]], ['z_r%d' % p])
                    tt('vector', t3, pre, tb(Tn_i), ALU.mult, [PSK[br_], 'Tn'], [tk[2]])
                    tt('vector', t4, pim, tb(Tn_r), ALU.mult, [PSK[bi_], 'Tn'], [tk[3]])
                    tt('gpsimd', v4(z_i[p][:]), t3, t4, ALU.add, [tk[2], tk[3]], ['z_i%d' % p])
                    if SUB2 < 4:
                        continue
                    if not own:
                        red(Ss_r[:, 4 * o:4 * o + 4], z_r[p][:].rearrange('p (c t) -> p c t', c=4), ALU.add, ['z_r%d' % p], ['Ss'])
                        red(Ss_i[:, 4 * o:4 * o + 4], z_i[p][:].rearrange('p (c t) -> p c t', c=4), ALU.add, ['z_i%d' % p], ['Ss'])
                        continue
                    zr0 = v4(z_r[p][:])[:, :, :, 0]
                    zi0 = v4(z_i[p][:])[:, :, :, 0]
                    tt('vector', zr0, zr0, CA_r[:, 4 * o:4 * o + 4, 0:ns], ALU.add, ['z_r%d' % p, 'CA'], ['z_r%d' % p])
                    tt('vector', zi0, zi0, CA_i[:, 4 * o:4 * o + 4, 0:ns], ALU.add, ['z_i%d' % p, 'CA'], ['z_i%d' % p])
                    scan(G_r[0][:], smask[:], z_r[p][:], ['smask', 'z_r%d' % p], ['G_r0'])
                    scan(G_i[0][:], smask[:], z_i[p][:], ['smask', 'z_i%d' % p], ['G_i0'])
                    gr, gi = v4(G_r[0][:]), v4(G_i[0][:])
                    tt('vector', t1, gr, tb(Tp_r), ALU.mult, ['G_r0', 'Tp'], [tk[0]])
                    tt('vector', t2, gi, tb(Tp_i), ALU.mult, ['G_i0', 'Tp'], [tk[1]])
                    tt('gpsimd', v4(h_r[0][:]), t1, t2, ALU.subtract, [tk[0], tk[1]], ['h_r0'])
                    tt('vector', t3, gr, tb(Tp_i), ALU.mult, ['G_r0', 'Tp'], [tk[2]])
                    tt('vector', t4, gi, tb(Tp_r), ALU.mult, ['G_i0', 'Tp'], [tk[3]])
                    tt('gpsimd', v4(h_i[0][:]), t3, t4, ALU.add, [tk[2], tk[3]], ['h_i0'])
                    cp('scalar', hb_r[0][:], h_r[0][:].rearrange('p (c t) -> p c t', c=4), ['h_r0'], ['hb_r0'])
                    cp('scalar', hb_i[0][:], h_i[0][:].rearrange('p (c t) -> p c t', c=4), ['h_i0'], ['hb_i0'])
                    cp('scalar', H_r[:, 4 * o:4 * o + 4, 0:ns], v4(h_r[0][:])[:, :, :, tlen - 1], ['h_r0', 'CA'], HK)
                    cp('scalar', H_i[:, 4 * o:4 * o + 4, 0:ns], v4(h_i[0][:])[:, :, :, tlen - 1], ['h_i0', 'CA'], HK)
                    py = 6 + o % 2
                    YR = ['CLr', 'CLn', 'CL3', 'hb_r0', 'hb_i0']
                    mm(PS[py][64:128, 0:128], CL3[:, o, 0, :], hb_r[0][:, 3, :], True, False, YR, [PSK[py]])
                    mm(PS[py][64:128, 0:128], CL3[:, o, 1, :], hb_i[0][:, 3, :], False, False, YR, [PSK[py]])
                    mm(PS[py][64:96, 0:128], CLr[:, 4 * o + 2, :], hb_r[0][:, 2, :], False, False, YR, [PSK[py]])
                    mm(PS[py][64:96, 0:128], CLn[:, 4 * o + 2, :], hb_i[0][:, 2, :], False, True, YR, [PSK[py]])
                    for cl in range(2):
                        c = 4 * o + cl
                        mm(PS[py][32 * cl:32 * cl + 32, 0:128], CLr[:, c, :], hb_r[0][:, cl, :], True, False, YR, [PSK[py]])
                        mm(PS[py][32 * cl:32 * cl + 32, 0:128], CLn[:, c, :], hb_i[0][:, cl, :], False, True, YR, [PSK[py]])
                    stt(yv[p][:], uT_f[:, o, :], Dt[:, o:o + 1], PS[py][:, 0:128], ALU.mult, ALU.add, ['uT_f0', 'uT_f1', 'Dt', PSK[py]], ['yv%d' % p])
                    act(ge[p][0][:], yv[p][:], AF.Square, ['yv%d' % p], ['ge%d_0' % p])
                    ts('vector', ge[p][0][:], ge[p][0][:], 0.044715, 1.0, ALU.mult, ALU.add, ['ge%d_0' % p], ['ge%d_0' % p])
                    tt('vector', ge[p][1][:], ge[p][0][:], yv[p][:], ALU.mult, ['ge%d_0' % p, 'yv%d' % p], ['ge%d_1' % p])
                    act(ge[p][0][:], ge[p][1][:], AF.Sigmoid, ['ge%d_1' % p], ['ge%d_0' % p], scale=1.5957691216057308)
                    tt('vector', gT[:, o, :], yv[p][:], ge[p][0][:], ALU.mult, ['yv%d' % p, 'ge%d_0' % p], ['gT%d' % o])
                if own:
                    GTK = ['gT%d' % o for o in range(8)]
                    for m in range(8):
                        p = 0
                        pg = 6 + m % 2
                        for k in range(8):
                            mm(PS[pg][:, 128:256], wgl[:, k, m * 128:(m + 1) * 128], gT[:, k, :], k == 0, k == 7, WGK + GTK, [PSK[pg]])
                        act(sg[p][:], PS[pg][:, 128:256], AF.Sigmoid, [PSK[pg], 'bg'], ['sg%d' % p], bias=bg[:, m:m + 1])
                        tt('vector', ccs[:, m, :], gT[:, m, :], sg[p][:], ALU.mult, GTK + ['sg%d' % p], ['ccs%d' % m])
                    dma('sync', 'ccs', cc_d[ob].rearrange('p (k t) -> p k t', t=128)[:, 0:8, :], ccs[:], ['ccs%d' % m for m in range(8)], ['cc_d'])
                    if blk == 31:
                        tr(PS[0][0:32, 0:128], H_r[:, :, 0], identf[:], HK + ['identf'], [PSK[0]])
                        tr(PS[0][0:32, 128:256], H_i[:, :, 0], identf[:], HK + ['identf'], [PSK[0]])
                        stp_t = tmp[1][1][0:32, 0:256].rearrange('p (a b) -> p a b', b=128)
                        cp('vector', stp_t, PS[0][0:32, 0:256].rearrange('p (a b) -> p a b', b=128), [PSK[0]], ['t1_1'])
                        dma('sync', 'stp_o', stp_o.rearrange('a c f -> c a f'), stp_t, ['t1_1'], ['stp_o'])
                    if smp:
                        for ri, Hx in enumerate((H_r, H_i)):
                            for c4 in range(8):
                                for i in range(4):
                                    c = 4 * c4 + i
                                    tr(PS[c4 % 2][0:16, i * 128:(i + 1) * 128], Hx[:, c, :], identf[:], HK + ['identf'], [PSK[c4 % 2]])
                                sto_ = tmp[c4 % 2][0][0:16, :]
                                cp('vector' if c4 % 2 == 0 else 'scalar', sto_, PS[c4 % 2][0:16, :], [PSK[c4 % 2]], ['t%d_0' % (c4 % 2)])
                                dma('sync', 'sts_o%d' % (c4 % 2), sts_o[ri, :, c4 * 512:(c4 + 1) * 512], sto_, ['t%d_0' % (c4 % 2)], ['sts_o'])
                elif SUB2 >= 5:
                    Hr0, Hi0 = H_r[:, :, 0], H_i[:, :, 0]
                    cmul_b(st(T5), st(T6), Hr0, Hi0, st(A128R), st(A128I), st(TMP), HK + SMK, SMK)
                    cmul_b(st(NUMR), st(DEN), Ss_r[:], Ss_i[:], st(A127R), st(A127I), st(TMP), ['Ss'] + SMK, SMK)
                    tt('vector', Hr0, st(T5), st(NUMR), ALU.add, SMK, HK)
                    tt('vector', Hi0, st(T6), st(DEN), ALU.add, SMK, HK)
            S.barrier()

        def phase_attn():
          with ExitStack() as ph:
            CKV = sbuf(ph, 'CKV', [128, NBLK, 512], BF16)
            KT = sbuf(ph, 'KT', [128, 4, NBLK * 128], BF16)
            KRT = sbuf(ph, 'KRT', [64, NBLK * 128], BF16)
            ssv = sbuf(ph, 'ssv', [128, 4], F32)
            w_in_v = w_in.rearrange('(k p) n -> p k n', p=128)
            with ExitStack() as pa:
                wA = sbuf(pa, 'wA', [128, 16, 640], BF16)
                gkv_b = sbuf(pa, 'gkv_b', [128, 512], F32)
                sq = sbuf(pa, 'sq', [128, 512], F32)
                xT = [sbuf(pa, 'xTa%d' % i, [128, 16, 128], BF16) for i in range(2)]
                dma('sync', 'gkv_b', gkv_b[:], g_kv.partition_broadcast(128), [], ['gkv_b'])
                ckv_f = [sbuf(pa, 'ckv_f%d' % i, [128, 512], F32) for i in range(2)]
                rk = [sbuf(pa, 'rk%d' % i, [128, 128], F32) for i in range(2)]
                krt = [sbuf(pa, 'krt%d' % i, [128, 128], F32) for i in range(2)]
                kr_f = [sbuf(pa, 'kr_f%d' % i, [128, 64], F32) for i in range(2)]
                krb = [sbuf(pa, 'krb%d' % i, [128, 64], BF16) for i in range(2)]
                for k4 in range(4):
                    dma('gpsimd', 'wA', wA[:, 4 * k4:4 * k4 + 4, 0:576], w_in_v[:, 4 * k4:4 * k4 + 4, 1536:2112], [], ['wA%d' % k4])
                WAK = ['wA%d' % k4 for k4 in range(4)]
                ts('vector', wA[:, :, 576:608], wA[:, :, 544:576], -1.0, None, ALU.mult, None, WAK, ['wArot0'])
                cp('vector', wA[:, :, 608:640], wA[:, :, 512:544], WAK, ['wArot1'])
                WAK = WAK + ['wArot0', 'wArot1']
                for blk in range(NBLK):
                    b = blk % 2
                    p0, p1, p2, p3 = (0, 1, 2, 3) if b == 0 else (4, 5, 6, 7)
                    dma('sync', 'xTa%d' % b, xT[b][:], xT_d[blk].rearrange('p (k t) -> p k t', t=128), [], ['xTa%d' % b])
                    dma('sync', 'rk%d' % b, rk[b][:], ropek[blk], [], ['rk%d' % b])
                    for k in range(16):
                        mm(PS[p0][:, 0:512], xT[b][:, k, :], wA[:, k, 0:512], k == 0, k == 15, WAK + ['xTa%d' % b], [PSK[p0]])
                    for k in range(16):
                        mm(PS[p1][:, 0:128], xT[b][:, k, :], wA[:, k, 512:640], k == 0, k == 15, WAK + ['xTa%d' % b], [PSK[p1]])
                    rmsnorm_rows(PS[p0][:, 0:512], PSK[p0], gkv_b[:], 'gkv_b', ckv_f[b][:], 'ckv_f%d' % b, sq, ssv, None)
                    cp('gpsimd', CKV[:, blk, :], ckv_f[b][:], ['ckv_f%d' % b], ['CKV%d' % blk])
                    dma('sync', 'o_ckv%d' % b, ckv_o[blk], ckv_f[b][:], ['ckv_f%d' % b], ['ckv_o'])
                    tt('vector', krt[b][:], PS[p1][:, 0:128], rk[b][:], ALU.mult, [PSK[p1], 'rk%d' % b], ['krt%d' % b])
                    tt('gpsimd', kr_f[b][:], krt[b][:, 0:64], krt[b][:, 64:128], ALU.add, ['krt%d' % b], ['kr_f%d' % b])
                    dma('sync', 'o_kr%d' % b, kr_o[blk], kr_f[b][:], ['kr_f%d' % b], ['kr_o'])
                    cp('gpsimd', krb[b][:], kr_f[b][:], ['kr_f%d' % b], ['krb%d' % b])
                    for c in range(4):
                        tr(psb(p2)[:, c * 128:(c + 1) * 128], CKV[:, blk, c * 128:(c + 1) * 128], identb[:], ['CKV%d' % blk, 'identb'], [PSK[p2]])
                    cp('scalar', KT[:, :, blk * 128:(blk + 1) * 128], psb(p2)[:, 0:512].rearrange('p (c t) -> p c t', t=128), [PSK[p2]], ['KT%d' % blk])
                    tr(psb(p3)[0:64, 0:128], krb[b][:], identb[:], ['krb%d' % b, 'identb'], [PSK[p3]])
                    cp('vector', KRT[:, blk * 128:(blk + 1) * 128], psb(p3)[0:64, 0:128], [PSK[p3]], ['KRT%d' % blk])
                S.barrier()
            if STAGE < 3:
                return
            with ExitStack() as pc:
                wq = sbuf(pc, 'wq', [128, 16, 512], BF16)
                gq_b = sbuf(pc, 'gq_b', [128, 512], F32)
                xT = [sbuf(pc, 'xTc', [128, 16, 128], BF16)]
                dma('sync', 'gq_b', gq_b[:], g_q.partition_broadcast(128), [], ['gq_b'])
                wuq = sbuf(pc, 'wuq', [128, 4, 1536], BF16)
                wuqr = sbuf(pc, 'wuqr', [128, 4, 8, 64], BF16)
                wukT = sbuf(pc, 'wukT', [128, 8, 512], BF16)
                wuv = sbuf(pc, 'wuv', [128, 4, 1024], BF16)
                cqn_b = sbuf(pc, 'cqn_b', [128, 512], BF16)
                cqnT = sbuf(pc, 'cqnT', [128, 4, 128], BF16)
                qnT = sbuf(pc, 'qnT', [128, 8, 128], BF16)
                QRT = sbuf(pc, 'QRT', [64, 8, 128], BF16)
                QLT = sbuf(pc, 'QLT', [128, 4, 8, 128], BF16)
                OLT = sbuf(pc, 'OLT', [128, 4, 8, 128], BF16)
                cca = sbuf(pc, 'cca', [128, 8, 128], BF16)
                rq = sbuf(pc, 'rq', [64, 256], F32)
                tq = sbuf(pc, 'tq', [64, 256], F32)
                Ssb = [sbuf(pc, 'Ssb%d' % i, [128, 512], F32) for i in range(2)]
                Pb = [sbuf(pc, 'Pb%d' % i, [128, 512], BF16) for i in range(2)]
                PTs = [sbuf(pc, 'PTs%d' % i, [128, 4, 128], BF16) for i in range(2)]
                acc2 = [sbuf(pc, 'acc%d' % i, [128, 512], F32) for i in range(2)]
                ob_ = sbuf(pc, 'ob_', [128, 512], BF16)
                stt2 = [sbuf(pc, 'stat%d' % i, [128, 16], F32) for i in range(2)]

                for k4 in range(4):
                    dma('gpsimd', 'wq', wq[:, 4 * k4:4 * k4 + 4, :], w_in_v[:, 4 * k4:4 * k4 + 4, 1024:1536], [], ['wq%d' % k4])
                WQK = ['wq%d' % k4 for k4 in range(4)]
                dma('gpsimd', 'wuq', wuq[:], w_uq.rearrange('(c p) n -> p c n', p=128), [], ['wuq'])
                dma('gpsimd', 'wuv', wuv[:], w_uv.rearrange('(c p) n -> p c n', p=128), [], ['wuv'])
                wv = wuq[:].rearrange('p c (h d) -> p c h d', d=192)
                ts('vector', wuqr[:, :, :, 0:32], wv[:, :, :, 160:192], -1.0, None, ALU.mult, None, ['wuq'], ['wuqr0'])
                cp('vector', wuqr[:, :, :, 32:64], wv[:, :, :, 128:160], ['wuq'], ['wuqr1'])
                with ExitStack() as pw:
                    wukn = sbuf(pw, 'wukn', [128, 4, 1024], BF16)
                    dma('gpsimd', 'wukn', wukn[:], w_uk.rearrange('(c p) n -> p c n', p=128), [], ['wukn'])
                    for h in range(8):
                        pb = 2 + h % 2
                        for c in range(4):
                            tr(psb(pb)[:, c * 128:(c + 1) * 128], wukn[:, c, h * 128:(h + 1) * 128], identb[:], ['wukn', 'identb'], [PSK[pb]])
                        cp('vector' if h % 2 == 0 else 'scalar', wukT[:, h, :], psb(pb)[:, 0:512], [PSK[pb]], ['wukT'])
                    S.barrier()

                MX, MNEW, NEGM, RSUM, CORR, LRUN, MRUN, LINV = range(8)

                def attn_tile(qr, lhs_list, rhs_list, nk, mask_ap, mask_key, first, pv_list, par, RK, PVK):
                    bs, bpt, bo = par, 2 + par, 4 + par
                    stt_ = stt2[par]
                    acc = acc2[par]
                    STK = ['stat%d' % par]
                    ACK = 'acc%d' % par
                    n = len(lhs_list)
                    for i in range(n):
                        mm(PS[bs][0:qr, 0:nk], lhs_list[i], rhs_list[i], i == 0, i == n - 1, RK, [PSK[bs]])
                    if mask_ap is not None:
                        tt('vector', Ssb[par][0:qr, 0:nk], PS[bs][0:qr, 0:nk], mask_ap, ALU.add, [PSK[bs], mask_key], ['Ssb%d' % par])
                        src, srck = Ssb[par][0:qr, 0:nk], 'Ssb%d' % par
                    else:
                        src, srck = PS[bs][0:qr, 0:nk], PSK[bs]
                    red(stt_[0:qr, MX:MX + 1], src, ALU.max, [srck], STK)
                    if first:
                        cp('vector', stt_[0:qr, MNEW:MNEW + 1], stt_[0:qr, MX:MX + 1], STK, STK)
                    else:
                        tt('vector', stt_[0:qr, MNEW:MNEW + 1], stt_[0:qr, MRUN:MRUN + 1], stt_[0:qr, MX:MX + 1], ALU.max, STK, STK)
                    ts('vector', stt_[0:qr, NEGM:NEGM + 1], stt_[0:qr, MNEW:MNEW + 1], -SCALE, None, ALU.mult, None, STK, STK)
                    act(Pb[par][0:qr, 0:nk], src, AF.Exp, [srck] + STK, ['Pb%d' % par] + STK, bias=stt_[0:qr, NEGM:NEGM + 1], scale=SCALE, accum=stt_[0:qr, RSUM:RSUM + 1])
                    if first:
                        cp('vector', stt_[0:qr, LRUN:LRUN + 1], stt_[0:qr, RSUM:RSUM + 1], STK, STK)
                    else:
                        act(stt_[0:qr, CORR:CORR + 1], stt_[0:qr, MRUN:MRUN + 1], AF.Exp, STK, STK, bias=stt_[0:qr, NEGM:NEGM + 1], scale=SCALE)
                        stt(stt_[0:qr, LRUN:LRUN + 1], stt_[0:qr, LRUN:LRUN + 1], stt_[0:qr, CORR:CORR + 1], stt_[0:qr, RSUM:RSUM + 1], ALU.mult, ALU.add, STK, STK)
                    cp('vector', stt_[0:qr, MRUN:MRUN + 1], stt_[0:qr, MNEW:MNEW + 1], STK, STK)
                    nsub = nk // 128
                    for i in range(nsub):
                        tr(psb(bpt)[:, i * 128:i * 128 + qr], Pb[par][0:qr, i * 128:(i + 1) * 128], identb[0:qr, 0:qr], ['Pb%d' % par, 'identb'], [PSK[bpt]])
                    cp('scalar', PTs[par][:, 0:nsub, 0:qr], psb(bpt)[:, 0:nsub * 128].rearrange('p (a b) -> p a b', b=128)[:, :, 0:qr], [PSK[bpt]], ['PTs%d' % par])
                    for i in range(nsub):
                        mm(PS[bo][0:qr, 0:512], PTs[par][:, i, 0:qr], pv_list[i], i == 0, i == nsub - 1, ['PTs%d' % par] + PVK, [PSK[bo]])
                    if first:
                        cp('vector', acc[0:qr, :], PS[bo][0:qr, 0:512], [PSK[bo]], [ACK])
                    else:
                        stt(acc[0:qr, :], acc[0:qr, :], stt_[0:qr, CORR:CORR + 1], PS[bo][0:qr, 0:512], ALU.mult, ALU.add, [ACK, PSK[bo]] + STK, [ACK])

                def attn_finish(qr, par):
                    stt_ = stt2[par]
                    acc = acc2[par]
                    STK = ['stat%d' % par]
                    recip(stt_[0:qr, LINV:LINV + 1], stt_[0:qr, LRUN:LRUN + 1], STK, STK)
                    ts('vector', ob_[0:qr, :], acc[0:qr, :], stt_[0:qr, LINV:LINV + 1], None, ALU.mult, None, ['acc%d' % par] + STK, ['ob_'])

                KALL = ['KT%d' % b for b in range(NBLK)] + ['KRT%d' % b for b in range(NBLK)]
                CALL = ['CKV%d' % b for b in range(NBLK)]
                tcount = 0
                NB = 2

                def do_block(ob):
                    nonlocal tcount
                    blk = OWN[ob]
                    smp = blk == 32
                    xb = 0
                    dma('sync', 'xTa%d' % xb, xT[xb][:], xT_d[blk].rearrange('p (k t) -> p k t', t=128), [], ['xTa%d' % xb])
                    dma('sync', 'rq', rq[:], ropeq[ob], [], ['rq'])
                    for k in range(16):
                        mm(PS[6][:, 0:512], xT[xb][:, k, :], wq[:, k, :], k == 0, k == 15, WQK + ['xTa%d' % xb], [PSK[6]])
                    rmsnorm_rows(PS[6][:, 0:512], PSK[6], gq_b[:], 'gq_b', cqn_b[:], 'cqn_b', Ssb[0], ssv, 'Ssb0')
                    for c in range(4):
                        tr(psb(7)[:, c * 128:(c + 1) * 128], cqn_b[:, c * 128:(c + 1) * 128], identb[:], ['cqn_b', 'identb'], [PSK[7]])
                    cp('vector', cqnT[:], psb(7)[:, 0:512].rearrange('p (c t) -> p c t', t=128), [PSK[7]], ['cqnT'])
                    for hh in range(2):
                        pb = 6 + hh
                        for i in range(4):
                            h = 4 * hh + i
                            for c in range(4):
                                mm(PS[pb][:, i * 128:(i + 1) * 128], wuq[:, c, h * 192:h * 192 + 128], cqnT[:, c, :], c == 0, c == 3, ['wuq', 'cqnT'], [PSK[pb]])
                        cp('scalar' if hh == 0 else 'vector', qnT[:, 4 * hh:4 * hh + 4, :], PS[pb][:].rearrange('p (a t) -> p a t', t=128), [PSK[pb]], ['qnT%d' % hh])
                    for h in range(8):
                        pb = 6 + h % 2
                        for c in range(4):
                            mm(PS[pb][0:64, 0:128], wuq[:, c, h * 192 + 128:h * 192 + 192], cqnT[:, c, :], c == 0, c == 3, ['wuq', 'cqnT'], [PSK[pb]])
                        for c in range(4):
                            mm(PS[pb][0:64, 128:256], wuqr[:, c, h, :], cqnT[:, c, :], c == 0, c == 3, ['wuqr0', 'wuqr1', 'cqnT'], [PSK[pb]])
                        tt('vector', tq[:], PS[pb][0:64, 0:256], rq[:], ALU.mult, [PSK[pb], 'rq'], ['tq'])
                        if smp:
                            qrs_ = QRT[:].rearrange('p h t -> p (h t)').rearrange('p (s x) -> p s x', x=64)
                            tt('gpsimd', qrs_[:, :, h * 8:(h + 1) * 8], tq[:, 0:128].rearrange('p (s t) -> p s t', t=8), tq[:, 128:256].rearrange('p (s t) -> p s t', t=8), ALU.add, ['tq'], ['QRT'])
                        else:
                            tt('gpsimd', QRT[:, h, :], tq[:, 0:128], tq[:, 128:256], ALU.add, ['tq'], ['QRT'])
                    for h in range(8):
                        pb = 6 + h % 2
                        for c in range(4):
                            mm(PS[pb][:, c * 128:(c + 1) * 128], wukT[:, h, c * 128:(c + 1) * 128], qnT[:, h, :], True, True, ['wukT', 'qnT0', 'qnT1'], [PSK[pb]])
                        if smp:
                            qls_ = QLT[:].rearrange('p c h t -> p c (h t)').rearrange('p c (s x) -> p c s x', x=64)
                            cp('scalar' if h % 2 == 0 else 'vector', qls_[:, :, :, h * 8:(h + 1) * 8], PS[pb][:].rearrange('p (c s t) -> p c s t', c=4, t=8), [PSK[pb]], ['QLT'])
                        else:
                            cp('scalar' if h % 2 == 0 else 'vector', QLT[:, :, h, :], PS[pb][:].rearrange('p (c t) -> p c t', t=128), [PSK[pb]], ['QLT'])
                    QK = ['QLT', 'QRT']
                    QLs = QLT[:].rearrange('p c h t -> p c (h t)').rearrange('p c (s x) -> p c s x', x=64)
                    QRs = QRT[:].rearrange('p h t -> p (h t)').rearrange('p (s x) -> p s x', x=64)
                    if not smp:
                        j = ob
                        for hp in range(4):
                          for kt in range(j + 1):
                            for par in range(2):
                                h = 2 * hp + par
                                lhs = [QLT[:, c, h, :] for c in range(4)] + [QRT[:, h, :]]
                                rhs = [KT[:, c, kt * 512:(kt + 1) * 512] for c in range(4)] + [KRT[:, kt * 512:(kt + 1) * 512]]
                                if kt == 0 and kt == j:
                                    mk, mkk = mboth[:], 'mboth'
                                elif kt == 0:
                                    mk, mkk = mpad[:], 'mpad'
                                elif kt == j:
                                    mk, mkk = mdiag[:], 'mdiag'
                                else:
                                    mk, mkk = None, None
                                pv = [CKV[:, 4 * kt + i, :] for i in range(4)]
                                attn_tile(128, lhs, rhs, 512, mk, mkk, kt == 0, pv, par, QK + KALL, CALL)
                                tcount += 1
                          for par in range(2):
                            h = 2 * hp + par
                            attn_finish(128, par)
                            for c in range(4):
                                tr(psb(6 + par)[:, c * 128:(c + 1) * 128], ob_[:, c * 128:(c + 1) * 128], identb[:], ['ob_', 'identb'], [PSK[6 + par]])
                            cp('scalar', OLT[:, :, h, :], psb(6 + par)[:, 0:512].rearrange('p (c t) -> p c t', t=128), [PSK[6 + par]], ['OLT'])
                    else:
                        for sp in range(8):
                          for kt in range(17):
                            for par in range(2):
                                s = 2 * sp + par
                                lhs = [QLs[:, c, s, :] for c in range(4)] + [QRs[:, s, :]]
                                if kt < 16:
                                    t = s * 16 + kt
                                    sl = kt % NB
                                    Kx, KRx = Kt[par][sl], KRt[par][sl]
                                    kk, krk = 'Kt%d_%d' % (par, sl), 'KRt%d_%d' % (par, sl)
                                    gather(kk, Kx[:].rearrange('p j f -> p (j f)'), cache_ckv, idx[:, t:t + 1], ['idx'], [kk])
                                    gather(krk, KRx[:].rearrange('p j f -> p (j f)'), cache_kr, idx[:, t:t + 1], ['idx'], [krk])
                                    for jj in range(4):
                                        for c in range(4):
                                            tr(psb(6 + c // 2)[:, (c % 2) * 512 + jj * 128:(c % 2) * 512 + (jj + 1) * 128], Kx[:, jj, c * 128:(c + 1) * 128], identb[:],
                                               [kk, 'identb'], [PSK[6 + c // 2]])
                                    cp('scalar', KTt[par][:, 0:2, :], psb(6)[:, :].rearrange('p (c k) -> p c k', k=512), [PSK[6]], ['KTt%d_0' % par])
                                    cp('vector', KTt[par][:, 2:4, :], psb(7)[:, :].rearrange('p (c k) -> p c k', k=512), [PSK[7]], ['KTt%d_1' % par])
                                    for jj in range(4):
                                        tr(psb(2 + par)[0:64, 512 + jj * 128:512 + (jj + 1) * 128], KRx[:, jj, :], identb[:], [krk, 'identb'], [PSK[2 + par]])
                                    cp('vector', KRTt[par][:, :], psb(2 + par)[0:64, 512:1024], [PSK[2 + par]], ['KRTt%d' % par])
                                    rhs = [KTt[par][:, c, :] for c in range(4)] + [KRTt[par][:, :]]
                                    pv = [Kx[:, jj, :] for jj in range(4)]
                                    attn_tile(64, lhs, rhs, 512, None, None, kt == 0, pv, par, QK + ['KTt%d_0' % par, 'KTt%d_1' % par, 'KRTt%d' % par], [kk])
                                else:
                                    rhs = [KT[:, c, 32 * 128:33 * 128] for c in range(4)] + [KRT[:, 32 * 128:33 * 128]]
                                    pv = [CKV[:, 32, :]]
                                    attn_tile(64, lhs, rhs, 128, msmp[:, s * 128:(s + 1) * 128], 'msmp', False, pv, par, QK + KALL, CALL)
                                tcount += 1
                          for par in range(2):
                            s = 2 * sp + par
                            attn_finish(64, par)
                            for c in range(4):
                                tr(psb(6 + par)[:, c * 64:(c + 1) * 64], ob_[0:64, c * 128:(c + 1) * 128], identb[0:64, 0:64], ['ob_', 'identb'], [PSK[6 + par]])
                            cp('scalar', OLT[:, :, :, s * 8:(s + 1) * 8], psb(6 + par)[:, 0:256].rearrange('p (c h t) -> p c h t', c=4, t=8), [PSK[6 + par]], ['OLT'])
                    for hh in range(2):
                        pb = 6 + hh
                        for i in range(4):
                            h = 4 * hh + i
                            for c in range(4):
                                mm(PS[pb][:, i * 128:(i + 1) * 128], wuv[:, c, h * 128:(h + 1) * 128], OLT[:, c, h, :], c == 0, c == 3, ['wuv', 'OLT'], [PSK[pb]])
                        cp('scalar' if hh == 0 else 'vector', cca[:, 4 * hh:4 * hh + 4, :], PS[pb][:].rearrange('p (a t) -> p a t', t=128), [PSK[pb]], ['cca%d' % hh])
                    dma('sync', 'cca', cc_d[ob].rearrange('p (k t) -> p k t', t=128)[:, 8:16, :], cca[:], ['cca0', 'cca1'], ['cc_d'])

                with ExitStack() as pp:
                    mpad = sbuf(pp, 'mpad', [128, 512], F32)
                    mdiag = sbuf(pp, 'mdiag', [128, 512], F32)
                    mboth = sbuf(pp, 'mboth', [128, 512], F32)
                    dma('sync', 'mpad', mpad[:], mask_pad, [], ['mpad'])
                    dma('sync', 'mdiag', mdiag[:], mask_diag, [], ['mdiag'])
                    tt('vector', mboth[:], mpad[:], mdiag[:], ALU.add, ['mpad', 'mdiag'], ['mboth'])
                    for ob in range(NOWN_RUN if NOWN_RUN < 8 else 8):
                        do_block(ob)
                    S.barrier()
                with ExitStack() as psm:
                    msmp = sbuf(psm, 'msmp', [64, 2048], F32)
                    Kt = [[sbuf(psm, 'Kt%d_%d' % (q_, i), [128, 4, 512], BF16) for i in range(NB)] for q_ in range(2)]
                    KRt = [[sbuf(psm, 'KRt%d_%d' % (q_, i), [128, 4, 64], BF16) for i in range(NB)] for q_ in range(2)]
                    KTt = [sbuf(psm, 'KTt%d' % i, [128, 4, 512], BF16) for i in range(2)]
                    KRTt = [sbuf(psm, 'KRTt%d' % i, [64, 512], BF16) for i in range(2)]
                    pti = sbuf(psm, 'pti', [128, 256], I32)
                    ptf = sbuf(psm, 'ptf', [128, 256], F32)
                    qof = sbuf(psm, 'qof', [128, 1], F32)
                    idx = sbuf(psm, 'idx', [128, 256], I32)
                    dma('sync', 'msmp', msmp[:], mask_smp, [], ['msmp'])
                    dma('sync', 'pti', pti[:], pt_exp, [], ['pti'])
                    dma('sync', 'qof', qof[:], qoff, [], ['qof'])
                    cp('vector', ptf[:], pti[:], ['pti'], ['ptf'])
                    ts('vector', ptf[:], ptf[:], 32.0, qof[:, 0:1], ALU.mult, ALU.add, ['ptf', 'qof'], ['ptf'])
                    cp('vector', idx[:], ptf[:], ['ptf'], ['idx'])
                    if NOWN_RUN >= 9:
                        do_block(8)
                    S.barrier()

        def layer_norm_rows(src, src_key, gb, bb, out, out_key, stats, mv, scope_keys):
            for i in range(4):
                S.op('vector', lambda e, i=i: e.bn_stats(out=stats[:, i, :], in_=src[:, i * 512:(i + 1) * 512]), [src_key], ['lnst'])
            S.op('vector', lambda e: e.bn_aggr(out=mv[:, 0:2], in_=stats[:].rearrange('p a b -> p (a b)')), ['lnst'], ['lnmv'])
            act(mv[:, 2:3], mv[:, 1:2], AF.Sqrt, ['lnmv'], ['lnmv2'], bias=epsq[:, 1:2], scale=1.0)
            recip(mv[:, 3:4], mv[:, 2:3], ['lnmv2'], ['lnmv3'])
            ts('vector', out, src[:], mv[:, 0:1], mv[:, 3:4], ALU.subtract, ALU.mult, [src_key, 'lnmv', 'lnmv3'], [out_key])
            tt('gpsimd', out, out, gb, ALU.mult, [out_key, 'lng'], [out_key])
            tt('gpsimd', out, out, bb, ALU.add, [out_key, 'lnb'], [out_key])

        def phase_out():
          with ExitStack() as ph:
            x1T = sbuf(ph, 'x1T', [128, 16, TOK], BF16)
            stats = sbuf(ph, 'lnstats', [128, 4, 6], F32)
            mv = sbuf(ph, 'lnmv', [128, 4], F32)
            with ExitStack() as p1:
                wo = sbuf(p1, 'wo', [128, 16, 2048], BF16)
                g1 = sbuf(p1, 'g1', [128, 2048], F32)
                b1 = sbuf(p1, 'b1', [128, 2048], F32)
                xa = sbuf(p1, 'xa', [128, 2048], F32)
                pre = sbuf(p1, 'pre', [128, 2048], F32)
                x1f = sbuf(p1, 'x1f', [128, 2048], F32)
                x1b = sbuf(p1, 'x1b', [128, 2048], BF16)
                w_out_v = w_out.rearrange('(k p) n -> p k n', p=128)
                for k in range(16):
                    dma('gpsimd', 'wo', wo[:, k, :], w_out_v[:, k, :], [], ['wo%d' % k])
                WOK = ['wo%d' % k for k in range(16)]
                dma('sync', 'g1', g1[:], ln1_g.partition_broadcast(128), [], ['lng'])
                dma('sync', 'b1', b1[:], ln1_b.partition_broadcast(128), [], ['lnb'])
                concatT = sbuf(p1, 'concatT', [128, 16, TOK], BF16)
                for ob in range(NOWN):
                    dma('sync', 'cc', concatT[:, :, ob * 128:(ob + 1) * 128], cc_d[ob].rearrange('p (k t) -> p k t', t=128), [], ['cc%d' % ob])
                CCK = ['cc%d' % ob for ob in range(NOWN)]
                for ob in range(NOWN):
                    blk = OWN[ob]
                    dma('sync', 'xa', xa[:], xs[blk], [], ['xa'])
                    for n in range(4):
                        for k in range(16):
                            mm(PS[n][:, 0:512], concatT[:, k, ob * 128:(ob + 1) * 128], wo[:, k, n * 512:(n + 1) * 512], k == 0, k == 15, WOK + CCK, [PSK[n]])
                        stt(pre[:, n * 512:(n + 1) * 512], xa[:, n * 512:(n + 1) * 512], ALPHA, PS[n][:, 0:512], ALU.mult, ALU.add, ['xa', PSK[n]], ['pre'])
                    layer_norm_rows(pre, 'pre', g1[:], b1[:], x1f[:], 'x1f', stats, mv, None)
                    dma('sync', 'x1_d', x1_d[ob], x1f[:], ['x1f'], ['x1_d%d' % ob])
                    cp('scalar', x1b[:], x1f[:], ['x1f'], ['x1b'])
                    for g in range(2):
                        pb = 4 + g
                        for i in range(8):
                            k = 8 * g + i
                            tr(psb(pb)[:, i * 128:(i + 1) * 128], x1b[:, k * 128:(k + 1) * 128], identb[:], ['x1b', 'identb'], [PSK[pb]])
                        cp('vector', x1T[:, 8 * g:8 * g + 8, ob * 128:(ob + 1) * 128], psb(pb)[:, :].rearrange('p (a t) -> p a t', t=128), [PSK[pb]], ['x1T'])
                S.barrier()
            HT = sbuf(ph, 'HT', [128, NFF, TOK], BF16)
            NT = [(0, 512), (512, 512), (1024, 128)]
            with ExitStack() as p2:
                wg = [sbuf(p2, 'wg%d' % i, [128, 16, 256], BF16) for i in range(2)]
                wu = [sbuf(p2, 'wu%d' % i, [128, 16, 256], BF16) for i in range(2)]
                sgl = [sbuf(p2, 'sgl%d' % i, [128, 512], F32) for i in range(2)]
                w_gate_v = w_gate.rearrange('(k p) n -> p k n', p=128)
                w_up_v = w_up.rearrange('(k p) n -> p k n', p=128)
                cnt = 0
                for fb in range(NFF // 2):
                    b = fb % 2
                    for k2 in range(2):
                        dma('gpsimd', 'wg%d' % b, wg[b][:, 8 * k2:8 * k2 + 8, :], w_gate_v[:, 8 * k2:8 * k2 + 8, fb * 256:(fb + 1) * 256], [], ['wg%d_%d' % (b, k2)])
                        dma('gpsimd', 'wu%d' % b, wu[b][:, 8 * k2:8 * k2 + 8, :], w_up_v[:, 8 * k2:8 * k2 + 8, fb * 256:(fb + 1) * 256], [], ['wu%d_%d' % (b, k2)])
                    WGK = ['wg%d_0' % b, 'wg%d_1' % b]
                    WUK = ['wu%d_0' % b, 'wu%d_1' % b]
                    for half in range(2):
                        f = 2 * fb + half
                        for (t0, tn) in NT:
                            p = cnt % 2
                            cnt += 1
                            pg, pu = p, 2 + p
                            for k in range(16):
                                mm(PS[pg][:, 0:tn], wg[b][:, k, half * 128:(half + 1) * 128], x1T[:, k, t0:t0 + tn], k == 0, k == 15, WGK + ['x1T'], [PSK[pg]])
                            for k in range(16):
                                mm(PS[pu][:, 0:tn], wu[b][:, k, half * 128:(half + 1) * 128], x1T[:, k, t0:t0 + tn], k == 0, k == 15, WUK + ['x1T'], [PSK[pu]])
                            act(sgl[p][:, 0:tn], PS[pg][:, 0:tn], AF.Silu, [PSK[pg]], ['sgl%d' % p])
                            tt('vector', HT[:, f, t0:t0 + tn], sgl[p][:, 0:tn], PS[pu][:, 0:tn], ALU.mult, ['sgl%d' % p, PSK[pu]], ['HT'])
                S.barrier()
            with ExitStack() as p3:
                wd = [sbuf(p3, 'wd%d' % i, [128, NFF, 256], BF16) for i in range(2)]
                yTf = [sbuf(p3, 'yTf%d' % i, [128, TOK], F32) for i in range(2)]
                ystrip = [sbuf(p3, 'ystrip%d' % i, [128, NOWN, 128], F32) for i in range(2)]
                w_down_v = w_down.rearrange('(f p) n -> p f n', p=128)
                cnt = 0
                for ocb in range(8):
                    b = ocb % 2
                    for f4 in range(4):
                        dma('gpsimd', 'wd%d' % b, wd[b][:, 11 * f4:11 * f4 + 11, :], w_down_v[:, 11 * f4:11 * f4 + 11, ocb * 256:(ocb + 1) * 256], [], ['wd%d_%d' % (b, f4)])
                    WDK = ['wd%d_%d' % (b, f4) for f4 in range(4)]
                    for half in range(2):
                        oc = 2 * ocb + half
                        yb = oc % 2
                        for ti, (t0, tn) in enumerate(NT):
                            pd = (cnt % 2) * 3 + ti
                            for f in range(NFF):
                                mm(PS[pd][:, 0:tn], wd[b][:, f, half * 128:(half + 1) * 128], HT[:, f, t0:t0 + tn], f == 0, f == NFF - 1, WDK + ['HT'], [PSK[pd]])
                            cp('scalar' if ti % 2 == 0 else 'vector', yTf[yb][:, t0:t0 + tn], PS[pd][:, 0:tn], [PSK[pd]], ['yTf%d_%d' % (yb, ti)])
                        cnt += 1
                        YK = ['yTf%d_%d' % (yb, ti) for ti in range(3)]
                        for g in range(3):
                            pb = 6 + g % 2
                            nb_ = 4 if g < 2 else 1
                            for i in range(nb_):
                                ob = 4 * g + i
                                tr(PS[pb][:, i * 128:(i + 1) * 128], yTf[yb][:, ob * 128:(ob + 1) * 128], identf[:], YK + ['identf'], [PSK[pb]])
                            cp('vector' if g % 2 == 0 else 'scalar', ystrip[yb][:, 4 * g:4 * g + nb_, :], PS[pb][:, 0:nb_ * 128].rearrange('p (a t) -> p a t', t=128), [PSK[pb]], ['ystrip%d_%d' % (yb, g)])
                        dma('sync', 'y2s%d' % yb, y2_d[:, :, oc * 128:(oc + 1) * 128].rearrange('b p f -> p b f'), ystrip[yb][:],
                            ['ystrip%d_%d' % (yb, g) for g in range(3)], ['y2_d'])
                S.barrier()
            with ExitStack() as p4:
                g2 = sbuf(p4, 'g2', [128, 2048], F32)
                b2 = sbuf(p4, 'b2', [128, 2048], F32)
                x1r = [sbuf(p4, 'x1r%d' % i, [128, 2048], F32) for i in range(2)]
                y2r = [sbuf(p4, 'y2r%d' % i, [128, 2048], F32) for i in range(2)]
                outt = [sbuf(p4, 'outt%d' % i, [128, 2048], F32) for i in range(2)]
                dma('sync', 'g2', g2[:], ln2_g.partition_broadcast(128), [], ['lng'])
                dma('sync', 'b2', b2[:], ln2_b.partition_broadcast(128), [], ['lnb'])
                for ob in range(NOWN):
                    b = ob % 2
                    dma('sync', 'x1r%d' % b, x1r[b][:], x1_d[ob], [], ['x1r%d' % b])
                    dma('sync', 'y2r%d' % b, y2r[b][:], y2_d[ob], [], ['y2r%d' % b])
                    stt(y2r[b][:], x1r[b][:], ALPHA, y2r[b][:], ALU.mult, ALU.add, ['x1r%d' % b, 'y2r%d' % b], ['y2r%d' % b])
                    layer_norm_rows(y2r[b], 'y2r%d' % b, g2[:], b2[:], outt[b][:], 'outt%d' % b, stats, mv, None)
                    dma('sync', 'yo%d' % b, y_o[ob], outt[b][:], ['outt%d' % b], ['y_o'])
                S.barrier()

        if STAGE >= 1:
            phase_s5()
        if STAGE >= 2:
            phase_attn()
        if STAGE >= 4:
            phase_out()
        S.barrier()
        S.emit()
    return nc


_NC = None


def _rope_tables():
    half = 32
    inv = (10000.0 ** (-2.0 * np.arange(half, dtype=np.float32) / 64.0)).astype(np.float32)
    return inv


def _host_inputs(inputs):
    f32 = np.float32
    x_prompt = np.asarray(inputs['x_prompt'], f32)
    x_sample = np.asarray(inputs['x_sample'], f32)
    page_table = np.asarray(inputs['page_table']).astype(np.int32)
    inv = _rope_tables()
    shared = {
        'w_in': np.asarray(inputs['w_in'], f32)[0],
        'g_q': np.asarray(inputs['g_q'], f32).reshape(1, 512),
        'w_uq': np.asarray(inputs['w_uq'], f32)[0],
        'w_uk': np.asarray(inputs['w_uk'], f32)[0].reshape(512, 1024),
        'g_kv': np.asarray(inputs['g_kv'], f32).reshape(1, 512),
        'w_uv': np.asarray(inputs['w_uv'], f32)[0].reshape(512, 1024),
        'ssm_a_re': np.asarray(inputs['ssm_a_re'], f32)[0],
        'ssm_a_im': np.asarray(inputs['ssm_a_im'], f32)[0],
        'ssm_log_step': np.asarray(inputs['ssm_log_step'], f32).reshape(1, 64),
        'ssm_b_re': np.asarray(inputs['ssm_b_re'], f32)[0],
        'ssm_b_im': np.asarray(inputs['ssm_b_im'], f32)[0],
        'ssm_c_re': np.asarray(inputs['ssm_c_re'], f32)[0],
        'ssm_c_im': np.asarray(inputs['ssm_c_im'], f32)[0],
        'ssm_d': np.asarray(inputs['ssm_d'], f32).reshape(1, 1024),
        'w_glu': np.asarray(inputs['w_glu'], f32)[0],
        'b_glu': np.asarray(inputs['b_glu'], f32).reshape(1, 1024),
        'w_out': np.asarray(inputs['w_out'], f32)[0],
        'ln1_g': np.asarray(inputs['ln1_g'], f32).reshape(1, 2048),
        'ln1_b': np.asarray(inputs['ln1_b'], f32).reshape(1, 2048),
        'w_gate': np.asarray(inputs['w_gate'], f32)[0],
        'w_up': np.asarray(inputs['w_up'], f32)[0],
        'w_down': np.asarray(inputs['w_down'], f32)[0],
        'ln2_g': np.asarray(inputs['ln2_g'], f32).reshape(1, 2048),
        'ln2_b': np.asarray(inputs['ln2_b'], f32).reshape(1, 2048),
        'cache_ckv': np.asarray(inputs['cache_ckv'], f32).reshape(-1, 2048),
        'cache_kr': np.asarray(inputs['cache_krope'], f32).reshape(-1, 256),
        'identf': np.eye(128, dtype=f32),
        'qoff': (np.arange(128) % 32).astype(f32).reshape(128, 1),
    }
    NEG = -30000.0
    md = np.zeros((128, 512), f32)
    md[:, 384:] = np.where(np.arange(128)[None, :] <= np.arange(128)[:, None], 0.0, NEG)
    shared['mask_diag'] = md
    ms = np.full((64, 16, 128), NEG, f32)
    tt_ = np.arange(64) % 8
    for s in range(16):
        for tp in range(8):
            ms[tt_ >= tp, s, s * 8 + tp] = 0.0
    shared['mask_smp'] = ms.reshape(64, 2048)
    st_re = np.asarray(inputs['state_ssm_re'], f32)[0].reshape(128, 4096)
    st_im = np.asarray(inputs['state_ssm_im'], f32)[0].reshape(128, 4096)
    in_maps = []
    for c in range(8):
        b, r = c // 4, c % 4
        xs = np.zeros((NBLK, 128, 2048), f32)
        pos = np.zeros((NBLK, 128), f32)
        mp = np.zeros((128, 512), f32)
        for i in range(32):
            a = i + r - 3
            if a >= 0:
                xs[i] = x_prompt[b, a * 128:(a + 1) * 128]
                pos[i] = a * 128 + np.arange(128)
            elif i < 4:
                mp[:, i * 128:(i + 1) * 128] = NEG
        xs[32] = x_sample[16 * c:16 * c + 16].reshape(128, 2048)
        pos[32] = 8192 + (np.arange(128) % 8)
        ang = pos[:, :, None] * inv[None, None, :]
        cs, sn = np.cos(ang).astype(f32), np.sin(ang).astype(f32)
        ropek = np.concatenate([cs, cs, sn, sn], axis=2).astype(f32)
        rq = np.zeros((NOWN, 64, 256), f32)
        for ob, blk in enumerate(OWN):
            rq[ob, :, 0:128] = np.concatenate([cs[blk], cs[blk]], axis=1).T
            rq[ob, :, 128:256] = np.concatenate([sn[blk], sn[blk]], axis=1).T
        pt = page_table[16 * c:16 * c + 16]
        ptx = pt.reshape(16, 16, 4)[:, :, np.arange(128) // 32]
        ptx = np.ascontiguousarray(ptx.transpose(2, 0, 1).reshape(128, 256)).astype(np.int32)
        m = dict(shared)
        m.update({'xs': xs, 'ropek': ropek, 'ropeq': rq, 'mask_pad': mp, 'pt_exp': ptx,
                  'st_re': np.ascontiguousarray(st_re[16 * c:16 * c + 16]),
                  'st_im': np.ascontiguousarray(st_im[16 * c:16 * c + 16])})
        in_maps.append(m)
    return in_maps


def kernel(**inputs):
    global _NC
    if _NC is None:
        _NC = build()
    in_maps = _host_inputs(inputs)
    if os.environ.get('MK_TRACE'):
        res = run_bass_kernel_spmd(_NC, in_maps[:NCORES], core_ids=list(range(NCORES)), trace=True)
        print('EXEC_TIME_NS', res.exec_time_ns)
    else:
        res = run_bass_kernel_spmd(_NC, in_maps[:NCORES], core_ids=list(range(NCORES)))
    R = list(res.results) + [res.results[0]] * (8 - NCORES)
    f32 = np.float32
    y_p = np.zeros((2, 4096, 2048), f32)
    y_s = np.zeros((128, 8, 2048), f32)
    ckv_p = np.zeros((1, 2, 4096, 512), f32)
    kr_p = np.zeros((1, 2, 4096, 64), f32)
    re_p = np.zeros((1, 2, 64, 64), f32)
    im_p = np.zeros((1, 2, 64, 64), f32)
    ckv_s = np.zeros((1, 128, 8, 512), f32)
    kr_s = np.zeros((1, 128, 8, 64), f32)
    re_s = np.zeros((1, 128, 64, 64), f32)
    im_s = np.zeros((1, 128, 64, 64), f32)
    for c in range(8):
        b, r = c // 4, c % 4
        o = R[c]
        for j in range(8):
            a = 4 * j + r
            y_p[b, a * 128:(a + 1) * 128] = o['y_o'][j]
        y_s[16 * c:16 * c + 16] = o['y_o'][8].reshape(16, 8, 2048)
        ckv_s[0, 16 * c:16 * c + 16] = o['ckv_o'][32].reshape(16, 8, 512)
        kr_s[0, 16 * c:16 * c + 16] = o['kr_o'][32].reshape(16, 8, 64)
        re_s[0, 16 * c:16 * c + 16] = o['sts_o'][0].reshape(16, 64, 64)
        im_s[0, 16 * c:16 * c + 16] = o['sts_o'][1].reshape(16, 64, 64)
        if r == 3:
            ckv_p[0, b] = o['ckv_o'][0:32].reshape(4096, 512)
            kr_p[0, b] = o['kr_o'][0:32].reshape(4096, 64)
            re_p[0, b] = o['stp_o'][0].reshape(64, 64)
            im_p[0, b] = o['stp_o'][1].reshape(64, 64)
    return (y_p, y_s, ckv_p, kr_p, re_p, im_p, ckv_s, kr_s, re_s, im_s)
```
